# Optimizing a Trainium2 kernel written in Bass

```python
import jax, jax.numpy as jnp
from jax import lax
import numpy as np

D_MODEL = 1024
BATCH = 8
SEQ = 4096
DEPTH = 4

CHUNK = 64
N_MIXERS = 2
N_A_LAYERS = (DEPTH + 1) // 2
N_B_LAYERS = DEPTH // 2

SGU_CHUNK = 2 * CHUNK
SGU_HALF = D_MODEL
SGU_GROUPS = 16
SGU_GROUP_DIM = SGU_HALF // SGU_GROUPS

RWKV_HEAD_DIM = 64
RWKV_HEADS = D_MODEL // RWKV_HEAD_DIM
DECAY_LORA = 64
AAA_LORA = 64
GATE_LORA = 160
GN_EPS = 64e-5
N_SHIFT_MIX = 6

FFN_DIM = 2816
CONV_WIDTH = 3
RMS_EPS = 1e-6

kernel_name = "hybrid_sgu_rwkv7_convffn_trunk"


def rms_norm(x, g):
    xf = x.astype(jnp.float32)
    y = xf * lax.rsqrt(jnp.mean(xf * xf, axis=-1, keepdims=True) + RMS_EPS)
    return (y * g.astype(jnp.float32)).astype(x.dtype)


def sgu_mixer(h, w_in, b_in, g_v, w_s, b_s, w_out):
    B, S, _ = h.shape
    z = jax.nn.gelu(h @ w_in + b_in, approximate=False)
    u, v = jnp.split(z, 2, axis=-1)
    v = rms_norm(v, g_v)
    n_blk = S // SGU_CHUNK
    v = v.reshape(B, n_blk, SGU_CHUNK, SGU_GROUPS, SGU_GROUP_DIM)
    causal = jnp.tril(jnp.ones((SGU_CHUNK, SGU_CHUNK), dtype=bool))
    w_causal = jnp.where(causal[None], w_s, 0)
    v = jnp.einsum('gts,bnsgc->bntgc', w_causal, v) + b_s.T[None, None, :, :, None]
    y = u * v.reshape(B, S, SGU_HALF)
    return y @ w_out


def token_shift(x):
    return jnp.pad(x, ((0, 0), (1, 0), (0, 0)))[:, :-1]


def wkv7_scan(r, w, k, v, a, b):
    B, S, H, N = r.shape

    def step(state, inp):
        r_t, w_t, k_t, v_t, a_t, b_t = inp
        sa = jnp.einsum('bhvk,bhk->bhv', state, a_t)
        state = (state * w_t[:, :, None, :]
                 + sa[..., None] * b_t[:, :, None, :]
                 + v_t[..., None] * k_t[:, :, None, :])
        y_t = jnp.einsum('bhvk,bhk->bhv', state, r_t)
        return state, y_t

    xs = tuple(jnp.swapaxes(t, 0, 1) for t in (r, w, k, v, a, b))
    s0 = jnp.zeros((B, H, N, N), jnp.float32)
    _, ys = lax.scan(step, s0, xs)
    return jnp.swapaxes(ys, 0, 1)


def rwkv7_mixer(h, mu, w_r, w_k, w_v, w_o, w0, w1, w2, a0, a1, a2,
                g1, g2, k_k, k_a, r_k, ln_w, ln_b):
    B, S, D = h.shape
    H, N = RWKV_HEADS, RWKV_HEAD_DIM
    f32 = jnp.float32
    xx = token_shift(h) - h
    xr = h + xx * mu[0]
    xw = h + xx * mu[1]
    xk = h + xx * mu[2]
    xv = h + xx * mu[3]
    xa = h + xx * mu[4]
    xg = h + xx * mu[5]
    r = xr @ w_r
    k = xk @ w_k
    v = xv @ w_v
    w = -jax.nn.softplus(-(w0 + jnp.tanh(xw @ w1) @ w2)) - 0.5
    a = jax.nn.sigmoid(a0 + (xa @ a1) @ a2)
    g = jax.nn.sigmoid(xg @ g1) @ g2

    def heads(t):
        return t.reshape(B, S, H, N).astype(f32)

    kk = heads(k * k_k)
    kk = kk * lax.rsqrt(jnp.maximum(jnp.sum(kk * kk, -1, keepdims=True), 1e-24))
    k = k * (1 + (a - 1) * k_a)
    r_h, k_h, v_h, a_h = heads(r), heads(k), heads(v), heads(a)
    decay = jnp.exp(-jnp.exp(heads(w)))
    y = wkv7_scan(r_h, decay, k_h, v_h, -kk, kk * a_h)
    mean = jnp.mean(y, -1, keepdims=True)
    var = jnp.mean(jnp.square(y - mean), -1, keepdims=True)
    y = ((y - mean) * lax.rsqrt(var + GN_EPS)).reshape(B, S, D)
    y = y * ln_w.astype(f32) + ln_b.astype(f32)
    bonus = jnp.sum(r_h * k_h * r_k.astype(f32), -1, keepdims=True) * v_h
    y = (y + bonus.reshape(B, S, D)).astype(h.dtype)
    return (y * g) @ w_o


def conv_ffn(h, w_up, conv_w, conv_b, w_down):
    S = h.shape[1]
    z = h @ w_up
    zp = jnp.pad(z, ((0, 0), (CONV_WIDTH - 1, 0), (0, 0)))
    z = sum(zp[:, j:j + S] * conv_w[j] for j in range(CONV_WIDTH)) + conv_b
    gate, val = jnp.split(z, 2, axis=-1)
    return (jax.nn.silu(gate) * val) @ w_down


def setup_inputs(seed: int = 0) -> dict:
    key = jax.random.key(seed)
    ks = iter(jax.random.split(key, 40))
    f32 = jnp.float32

    def nrm(shape, scale):
        return jax.random.normal(next(ks), shape, f32) * scale

    def gain(shape):
        return 1.0 + 0.05 * jax.random.normal(next(ks), shape, f32)

    D, NA, NB = D_MODEL, N_A_LAYERS, N_B_LAYERS
    H2, F2 = 2 * SGU_HALF, 2 * FFN_DIM
    return {
        "x": nrm((BATCH, SEQ, D), 1.0),
        "norm_mix_g": gain((DEPTH, D)),
        "norm_ffn_g": gain((DEPTH, D)),
        "norm_final_g": gain((D,)),
        "sgu_w_in": nrm((NA, D, H2), D ** -0.5),
        "sgu_b_in": nrm((NA, H2), 0.02),
        "sgu_g_v": gain((NA, SGU_HALF)),
        "sgu_w_s": nrm((NA, SGU_GROUPS, SGU_CHUNK, SGU_CHUNK), 0.5 * SGU_CHUNK ** -0.5),
        "sgu_b_s": gain((NA, SGU_GROUPS, SGU_CHUNK)),
        "sgu_w_out": nrm((NA, SGU_HALF, D), SGU_HALF ** -0.5),
        "rwkv_mu": jax.random.uniform(next(ks), (NB, N_SHIFT_MIX, D), f32),
        "rwkv_w_r": nrm((NB, D, D), D ** -0.5),
        "rwkv_w_k": nrm((NB, D, D), D ** -0.5),
        "rwkv_w_v": nrm((NB, D, D), D ** -0.5),
        "rwkv_w_o": nrm((NB, D, D), D ** -0.5),
        "rwkv_w0": jax.random.uniform(next(ks), (NB, D), f32, minval=-6.0, maxval=-1.0),
        "rwkv_w1": nrm((NB, D, DECAY_LORA), 0.1 * D ** -0.5),
        "rwkv_w2": nrm((NB, DECAY_LORA, D), 0.1 * DECAY_LORA ** -0.5),
        "rwkv_a0": nrm((NB, D), 0.1),
        "rwkv_a1": nrm((NB, D, AAA_LORA), 0.1 * D ** -0.5),
        "rwkv_a2": nrm((NB, AAA_LORA, D), 0.1 * AAA_LORA ** -0.5),
        "rwkv_g1": nrm((NB, D, GATE_LORA), D ** -0.5),
        "rwkv_g2": nrm((NB, GATE_LORA, D), GATE_LORA ** -0.5),
        "rwkv_k_k": 0.85 + 0.05 * jax.random.normal(next(ks), (NB, D), f32),
        "rwkv_k_a": gain((NB, D)),
        "rwkv_r_k": nrm((NB, RWKV_HEADS, RWKV_HEAD_DIM), 0.1),
        "rwkv_ln_w": gain((NB, D)),
        "rwkv_ln_b": nrm((NB, D), 0.02),
        "ffn_w_up": nrm((DEPTH, D, F2), D ** -0.5),
        "ffn_conv_w": nrm((DEPTH, CONV_WIDTH, F2), CONV_WIDTH ** -0.5),
        "ffn_conv_b": nrm((DEPTH, F2), 0.02),
        "ffn_w_down": nrm((DEPTH, FFN_DIM, D), FFN_DIM ** -0.5),
    }


def reference(x, norm_mix_g, norm_ffn_g, norm_final_g,
              sgu_w_in, sgu_b_in, sgu_g_v, sgu_w_s, sgu_b_s, sgu_w_out,
              rwkv_mu, rwkv_w_r, rwkv_w_k, rwkv_w_v, rwkv_w_o,
              rwkv_w0, rwkv_w1, rwkv_w2, rwkv_a0, rwkv_a1, rwkv_a2,
              rwkv_g1, rwkv_g2, rwkv_k_k, rwkv_k_a, rwkv_r_k, rwkv_ln_w, rwkv_ln_b,
              ffn_w_up, ffn_conv_w, ffn_conv_b, ffn_w_down):
    h = x
    for i in range(DEPTH):
        hn = rms_norm(h, norm_mix_g[i])
        j = i // N_MIXERS
        if i % N_MIXERS == 0:
            h = h + sgu_mixer(hn, sgu_w_in[j], sgu_b_in[j], sgu_g_v[j],
                              sgu_w_s[j], sgu_b_s[j], sgu_w_out[j])
        else:
            h = h + rwkv7_mixer(hn, rwkv_mu[j], rwkv_w_r[j], rwkv_w_k[j], rwkv_w_v[j], rwkv_w_o[j],
                                rwkv_w0[j], rwkv_w1[j], rwkv_w2[j],
                                rwkv_a0[j], rwkv_a1[j], rwkv_a2[j],
                                rwkv_g1[j], rwkv_g2[j], rwkv_k_k[j], rwkv_k_a[j], rwkv_r_k[j],
                                rwkv_ln_w[j], rwkv_ln_b[j])
        h = h + conv_ffn(rms_norm(h, norm_ffn_g[i]), ffn_w_up[i], ffn_conv_w[i],
                         ffn_conv_b[i], ffn_w_down[i])
    return rms_norm(h, norm_final_g)
```

```python
import numpy as np
from contextlib import ExitStack
import concourse.bass as bass
import concourse.mybir as mybir
from concourse.bass_utils import run_bass_kernel_spmd

F32 = mybir.dt.float32
BF16 = mybir.dt.bfloat16
AF = mybir.ActivationFunctionType
ALU = mybir.AluOpType
AX = mybir.AxisListType

D = 1024
S = 4096
DEPTH = 4
FF = 2816
FC = 22
G = 512
GT = G // 128
NG = S // G
NH = 16
RMS_EPS = 1e-6
GN_EPS = 64e-5


class Res:
    __slots__ = ("name", "w", "r", "dsem")

    def __init__(self, name):
        self.name = name
        self.w = None
        self.r = []
        self.dsem = None


class T:
    def __init__(self, h, res):
        self.h = h
        self.res = res

    def __getitem__(self, k):
        return self.h[k]


def _res(t):
    return t.res if isinstance(t, T) else t


class Prog:
    ENG = ("pe", "dve", "act", "pool", "sp")

    def __init__(self, nc):
        self.nc = nc
        self.es = ExitStack()
        self.streams = {e: [] for e in self.ENG}
        self.count = {e: 0 for e in self.ENG}
        self.semvals = {}
        self.seen = {e: {} for e in self.ENG}
        self.ndsem = 0
        self.banks = []
        self.bank_i = 0

    def sb(self, name, shape, dtype):
        t = self.es.enter_context(self.nc.sbuf_tensor(name, list(shape), dtype))
        return T(t, Res(name))

    def ps(self, name, shape, dtype=F32):
        t = self.es.enter_context(self.nc.psum_tensor(name, list(shape), dtype))
        return T(t, Res(name))

    def bank(self):
        b = self.banks[self.bank_i % len(self.banks)]
        self.bank_i += 1
        return b

    def _need(self, eng, ev, waits):
        if ev is None:
            return
        key, val = ev
        if key == "pe" and eng == "pe":
            return
        if self.seen[eng].get(key, 0) >= val:
            return
        self.seen[eng][key] = val
        for i, (k, v) in enumerate(waits):
            if k == key:
                waits[i] = (k, max(v, val))
                return
        waits.append((key, val))

    def _deps(self, eng, reads, writes):
        waits = []
        for t in reads:
            self._need(eng, _res(t).w, waits)
        for t in writes:
            r = _res(t)
            self._need(eng, r.w, waits)
            for ev in r.r:
                self._need(eng, ev, waits)
        return waits

    def _commit(self, ev, reads, writes):
        for t in reads:
            r = _res(t)
            r.r.append(ev)
            if len(r.r) > 48:
                best = {}
                for k, v in r.r:
                    best[k] = max(best.get(k, 0), v)
                r.r = list(best.items())
        for t in writes:
            r = _res(t)
            r.w = ev
            r.r = []

    def op(self, eng, fn, reads=(), writes=()):
        waits = self._deps(eng, reads, writes)
        self.count[eng] += 1
        ev = (eng, self.count[eng])
        self.streams[eng].append((waits, fn, (eng, 1)))
        self._commit(ev, reads, writes)
        return ev

    def dma(self, q, out_ap, in_ap, reads=(), writes=(), sem_res=None, **kw):
        sr = sem_res if sem_res is not None else (writes[0] if writes else reads[0])
        sr = _res(sr)
        if sr.dsem is None:
            sr.dsem = "d%d" % self.ndsem
            self.ndsem += 1
            self.semvals[sr.dsem] = 0
        key = sr.dsem
        waits = self._deps(q, reads, writes)
        self.semvals[key] += 16
        ev = (key, self.semvals[key])

        def fn(e, out_ap=out_ap, in_ap=in_ap, kw=kw):
            return e.dma_start(out=out_ap, in_=in_ap, **kw)
        self.streams[q].append((waits, fn, (key, 16)))
        self._commit(ev, reads, writes)
        return ev

    def wait_all(self, eng, events):
        waits = []
        for ev in events:
            self._need(eng, ev, waits)
        self.streams[eng].append((waits, None, None))

    def mm(self, out, lhsT, rhs, start, stop, R, W):
        return self.op("pe", lambda e: e.matmul(out, lhsT=lhsT, rhs=rhs, start=start, stop=stop), R, W)

    def tr(self, out, in_, ident, R, W):
        return self.op("pe", lambda e: e.transpose(out, in_, ident[:]), list(R) + [ident], W)

    def act(self, out, in_, func, R, W, **kw):
        return self.op("act", lambda e: e.activation(out=out, in_=in_, func=func, **kw), R, W)

    def tt(self, eng, out, in0, in1, op, R, W):
        return self.op(eng, lambda e: e.tensor_tensor(out=out, in0=in0, in1=in1, op=op), R, W)

    def stt(self, eng, out, in0, scalar, in1, op0, op1, R, W, **kw):
        return self.op(eng, lambda e: e.scalar_tensor_tensor(out=out, in0=in0, scalar=scalar, in1=in1,
                                                             op0=op0, op1=op1, **kw), R, W)

    def ts(self, eng, out, in0, s1, s2, op0, op1, R, W):
        if s2 is None:
            return self.op(eng, lambda e: e.tensor_scalar(out=out, in0=in0, scalar1=s1, scalar2=None, op0=op0), R, W)
        return self.op(eng, lambda e: e.tensor_scalar(out=out, in0=in0, scalar1=s1, scalar2=s2, op0=op0, op1=op1), R, W)

    def cp(self, eng, out, in_, R, W):
        if eng == "act":
            return self.op("act", lambda e: e.activation(out=out, in_=in_, func=AF.Copy), R, W)
        return self.op(eng, lambda e: e.tensor_copy(out=out, in_=in_), R, W)

    def memset(self, eng, out, val, W):
        return self.op(eng, lambda e: e.memset(out, val), (), W)

    def emit(self):
        nc = self.nc
        keys = list(self.ENG) + list(self.semvals.keys())
        sems = {}
        for k in keys:
            sems[k] = self.es.enter_context(nc.semaphore("s_" + k))
        streams = self.streams
        with nc.Block() as block:
            def mk(ename):
                def body(e):
                    for waits, fn, inc in streams[ename]:
                        for (k, v) in waits:
                            e.wait_ge(sems[k], v)
                        if fn is not None:
                            ins = fn(e)
                            ins.then_inc(sems[inc[0]], inc[1])
                return body
            block.tensor(mk("pe"))
            block.vector(mk("dve"))
            block.scalar(mk("act"))
            block.gpsimd(mk("pool"))
            block.sync(mk("sp"))
        self.es.close()


NEGC = -float(np.exp(-0.5))


class Ctx:
    pass


def split_res(parent, children):
    for c in children:
        c.w = parent.w
        c.r = list(parent.r)


def merge_res(parent, children):
    evs = list(parent.r)
    for c in children:
        evs.extend(c.r)
        if c.w is not None:
            evs.append(c.w)
    best = {}
    for k, v in evs:
        best[k] = max(best.get(k, 0), v)
    parent.r = list(best.items())


def declare_inputs(nc, layers, last):
    I = {}

    def din(name, shape):
        I[name] = nc.dram_tensor(name, list(shape), F32, kind="ExternalInput").ap()

    din("hin", [S, D])
    din("cmask", [128, 3, 128])
    for l in layers:
        din("gmix%d" % l, [1, D])
        din("gffn%d" % l, [1, D])
        if l % 2 == 0:
            din("sgu_w_in%d" % l, [D, 2 * D])
            din("sgu_b_in_v%d" % l, [1, D])
            din("sgu_b_in_u%d" % l, [128, 8])
            din("sgu_g_v%d" % l, [1, D])
            din("sgu_w_sT%d" % l, [128, 16, 128])
            din("sgu_b_s%d" % l, [1, 2048])
            din("sgu_w_out%d" % l, [D, D])
        else:
            din("rwkv_mu%d" % l, [128, 6, 8])
            for nm in ("w_r", "w_k", "w_v", "w_o"):
                din("rwkv_%s%d" % (nm, l), [D, D])
            din("rwkv_w0a0%d" % l, [1, 2048])
            din("rwkv_w1%d" % l, [D, 64])
            din("rwkv_a1%d" % l, [D, 64])
            din("rwkv_g1%d" % l, [D, 160])
            din("rwkv_w2%d" % l, [64, D])
            din("rwkv_a2%d" % l, [64, D])
            din("rwkv_g2%d" % l, [160, D])
            for nm in ("k_k", "k_a", "r_k", "ln_w", "ln_b"):
                din("rwkv_%s%d" % (nm, l), [1, D])
        din("ffn_w_up%d" % l, [D, 2 * FF])
        din("ffn_cw%d" % l, [128, 3, 44])
        din("ffn_cb%d" % l, [128, 44])
        din("ffn_w_down%d" % l, [FF, D])
    if last:
        din("gfinal", [1, D])
    O = nc.dram_tensor("hout", [S, D], F32, kind="ExternalOutput").ap()
    return I, O


def build_program(layers, last=True, stop_after=None, ngroups=NG):
    nc = bass.Bass("TRN2", target_bir_lowering=False)
    I, O = declare_inputs(nc, layers, last)
    P = Prog(nc)
    C = Ctx()
    C.I = I
    P.banks = [P.ps("bk%d" % i, [128, 512], F32) for i in range(8)]
    C.ident = P.sb("ident", [128, 128], BF16)
    C.ones = P.sb("ones", [128, 128], BF16)
    C.msk = P.sb("msk", [128, 3, 128], F32)
    C.trin = P.sb("trin", [128, 3, 128], F32)
    C.onesn = P.sb("onesn", [128, 2], F32)
    P.memset("pool", C.ident[:], 1.0, [C.ident])
    P.op("pool", lambda e: e.affine_select(out=C.ident[:], in_=C.ident[:], pattern=[[-1, 128]],
                                           compare_op=ALU.is_equal, fill=0.0, base=0, channel_multiplier=1),
         [C.ident], [C.ident])
    P.memset("pool", C.ones[:], 1.0, [C.ones])
    P.memset("pool", C.onesn[:], NEGC, [C.onesn])
    P.dma("sp", C.msk[:], I["cmask"], writes=[C.msk])
    P.ts("dve", C.trin[:], C.msk[:], NEGC, None, ALU.mult, None, [C.msk], [C.trin])
    C.arena = P.sb("arena", [128, 32768], BF16)
    C.A = [Res("A0"), Res("A1"), Res("A2"), Res("A3a"), Res("A3b")]
    C.h = [P.sb("h%d" % j, [128, D], F32) for j in range(GT)]
    C.bc = [P.sb("bc%d" % i, [128, D], F32) for i in range(6)]
    C.rows = P.sb("rows", [128, 2048], BF16)
    C.hn = [P.sb("hn%d" % i, [128, D], BF16) for i in range(2)]
    C.stat = [P.sb("stat%d" % i, [128, 8], F32) for i in range(2)]
    C.hnT = P.sb("hnT", [128, 8, G], BF16)
    C.fw = [P.sb("fw%d" % i, [128, 520], F32) for i in range(8)]
    C.bw = [P.sb("bw%d" % i, [128, 512], BF16) for i in range(8)]
    C.big = P.sb("big", [128, FC * G], BF16)
    C.nstat = 0
    C.halo, C.cw, C.cb = {}, {}, {}
    for l in layers:
        C.halo[l] = P.sb("halo%d" % l, [128, 44, 2], F32)
        P.memset("pool", C.halo[l][:], 0.0, [C.halo[l]])
        C.cw[l] = P.sb("cw%d" % l, [128, 3, 44], F32)
        C.cb[l] = P.sb("cb%d" % l, [128, 44], F32)
        P.dma("sp", C.cw[l][:], I["ffn_cw%d" % l], writes=[C.cw[l]])
        P.dma("sp", C.cb[l][:], I["ffn_cb%d" % l], writes=[C.cb[l]])
    sgu_setup(P, C, layers)
    rwkv_setup(P, C, layers)

    out_events = []
    for g in range(ngroups):
        for j in range(GT):
            r0 = (g * GT + j) * 128
            P.dma("sp", C.h[j][:], I["hin"][r0:r0 + 128, :], writes=[C.h[j]])
        done = False
        for l in layers:
            if l % 2 == 0:
                sgu_mixer(P, C, l, g)
            else:
                rwkv_mixer(P, C, l, g)
            if stop_after == ("mix", l):
                done = True
                break
            ffn(P, C, l, g)
            if stop_after == ("ffn", l):
                done = True
                break
        if last and not done:
            final_norm(P, C, g)
        for j in range(GT):
            r0 = (g * GT + j) * 128
            out_events.append(P.dma("sp", O[r0:r0 + 128, :], C.h[j][:], reads=[C.h[j]]))
    P.wait_all("sp", out_events)
    P.emit()
    return nc


def bcast_load(P, tile, dram_row):
    P.dma("sp", tile[:], dram_row.partition_broadcast(128), writes=[tile])


def rstd_from_ss(P, st, n_inv, eps):
    P.act(st[:, 1:2], st[:, 0:1], AF.Sqrt, [st], [st], scale=n_inv, bias=eps)
    P.op("dve", lambda e: e.reciprocal(out=st[:, 2:3], in_=st[:, 1:2]), [st], [st])


def rmsnorm_to_bf(P, C, h, gB, out_bf, junk):
    st = C.stat[C.nstat % 2]
    C.nstat += 1
    P.act(junk[:], h[:], AF.Square, [h], [junk, st], accum_out=st[:, 0:1])
    rstd_from_ss(P, st, 1.0 / D, RMS_EPS)
    P.stt("dve", out_bf[:], h[:], st[:, 2:3], gB[:], ALU.mult, ALU.mult, [h, st, gB], [out_bf])


def transpose8(P, C, srcs, dst_ap, R, W, evac_eng="act"):
    bk = P.bank()
    bkb = bk[:].bitcast(BF16)
    n = len(srcs)
    for k, s_ap in enumerate(srcs):
        P.tr(bkb[:, k * 128:(k + 1) * 128], s_ap, C.ident, list(R), [bk])
    P.cp(evac_eng, dst_ap, bkb[:, 0:n * 128].rearrange("p (k t) -> p k t", k=n), [bk], W)


def norm_and_transpose_group(P, C, gB):
    for j in range(GT):
        hn = C.hn[j % 2]
        rmsnorm_to_bf(P, C, C.h[j], gB, hn, C.hn[1 - j % 2])
        transpose8(P, C, [hn[:, k * 128:(k + 1) * 128] for k in range(8)],
                   C.hnT[:, :, j * 128:(j + 1) * 128], [hn], [C.hnT],
                   evac_eng="act" if j % 2 == 0 else "dve")


def ffn(P, C, l, g):
    I = C.I
    Wup = I["ffn_w_up%d" % l]
    Wdn_d = I["ffn_w_down%d" % l]
    gB = C.bc[0]
    bcast_load(P, gB, I["gffn%d" % l])
    ring = [C.arena[:, 24576 + s * 4096: 24576 + (s + 1) * 4096].rearrange("p (k h f) -> p k h f", k=8, h=2)
            for s in range(2)]
    ringR = [C.A[3], C.A[4]]
    Wdn = C.arena[:, 0:FC * 1024].rearrange("p (c f) -> p c f", c=FC)
    WdnR = [C.A[0], C.A[1], C.A[2]]
    actT = C.big[:].rearrange("p (c t) -> p c t", c=FC)

    def load_piece(i):
        s = i % 2
        for hh in range(2):
            c0 = hh * FF + i * 256
            P.dma("pool", ring[s][:, :, hh, :], Wup[:, c0:c0 + 256].rearrange("(k p) f -> p k f", p=128),
                  writes=[ringR[s]])

    load_piece(0)
    load_piece(1)
    for part in range(2):
        P.dma("pool", Wdn[:, part * 11:(part + 1) * 11, :],
              Wdn_d[part * 11 * 128:(part + 1) * 11 * 128, :].rearrange("(c p) f -> p c f", p=128),
              writes=WdnR)
    norm_and_transpose_group(P, C, gB)
    cw, cb, halo = C.cw[l], C.cb[l], C.halo[l]
    for i in range(11):
        s = i % 2
        for cc in range(2):
            ch = 2 * i + cc
            cres = []
            for hh in range(2):
                chh = hh * FC + ch
                bk = P.bank()
                for k in range(8):
                    P.mm(bk[:, 0:G], ring[s][:, k, hh, cc * 128:(cc + 1) * 128], C.hnT[:, k, :],
                         k == 0, k == 7, [ringR[s], C.hnT], [bk])
                zs = C.fw[hh * 2]
                cc_ = C.fw[hh * 2 + 1]
                P.cp("act", zs[:, 2:2 + G], bk[:, 0:G], [bk], [zs])
                P.cp("pool", zs[:, 0:2], halo[:, chh, :], [halo], [zs])
                P.act(cc_[:, 0:G], bk[:, 0:G], AF.Identity, [bk, cw, cb], [cc_],
                      scale=cw[:, 2, chh:chh + 1], bias=cb[:, chh:chh + 1])
                P.stt("dve", cc_[:, 0:G], zs[:, 1:1 + G], cw[:, 1, chh:chh + 1], cc_[:, 0:G], ALU.mult, ALU.add,
                      [zs, cw, cc_], [cc_])
                P.stt("dve", cc_[:, 0:G], zs[:, 0:G], cw[:, 0, chh:chh + 1], cc_[:, 0:G], ALU.mult, ALU.add,
                      [zs, cw, cc_], [cc_])
                P.cp("pool", halo[:, chh, :], zs[:, G:G + 2], [zs], [halo])
                cres.append(cc_)
            sg = C.fw[4]
            P.act(sg[:, 0:G], cres[0][:, 0:G], AF.Silu, [cres[0]], [sg])
            P.tt("dve", actT[:, ch, :], sg[:, 0:G], cres[1][:, 0:G], ALU.mult, [sg, cres[1]], [C.big])
        if i + 2 < 11:
            load_piece(i + 2)
    for j in range(GT):
        for hf in range(2):
            bk = P.bank()
            for c in range(FC):
                P.mm(bk[:, 0:512], actT[:, c, j * 128:(j + 1) * 128], Wdn[:, c, hf * 512:(hf + 1) * 512],
                     c == 0, c == FC - 1, [C.big] + WdnR, [bk])
            P.tt("dve", C.h[j][:, hf * 512:(hf + 1) * 512], bk[:, 0:512], C.h[j][:, hf * 512:(hf + 1) * 512], ALU.add,
                 [bk, C.h[j]], [C.h[j]])


def final_norm(P, C, g):
    gB = C.bc[0]
    bcast_load(P, gB, C.I["gfinal"])
    for j in range(GT):
        h = C.h[j]
        st = C.stat[C.nstat % 2]
        C.nstat += 1
        P.act(C.hn[j % 2][:], h[:], AF.Square, [h], [C.hn[j % 2], st], accum_out=st[:, 0:1])
        rstd_from_ss(P, st, 1.0 / D, RMS_EPS)
        P.stt("dve", h[:], h[:], st[:, 2:3], gB[:], ALU.mult, ALU.mult, [h, st, gB], [h])


def sgu_setup(P, C, layers):
    C.sgu = {}
    if any(l % 2 == 0 for l in layers):
        C.wsT = P.sb("wsT", [128, 16, 128], BF16)
    for l in layers:
        if l % 2 != 0:
            continue
        X = Ctx()
        C.sgu[l] = X
        X.biu = P.sb("biu%d" % l, [128, 8], F32)
        P.dma("sp", X.biu[:], C.I["sgu_b_in_u%d" % l], writes=[X.biu])


def sgu_mixer(P, C, l, g):
    I = C.I
    X = C.sgu[l]
    Win = C.arena[:, 0:16384].rearrange("p (k f) -> p k f", k=8)
    WinR = [C.A[0], C.A[1]]
    Wout = C.arena[:, 16384:24576].rearrange("p (k f) -> p k f", k=8)
    WoutR = [C.A[2]]
    gB, gv = C.bc[0], C.bc[1]
    bcast_load(P, gB, I["gmix%d" % l])
    bcast_load(P, gv, I["sgu_g_v%d" % l])
    P.dma("pool", C.rows[0:1, 0:2048], I["sgu_b_s%d" % l], writes=[C.rows])
    P.dma("pool", C.rows[32:33, 0:1024], I["sgu_b_in_v%d" % l], writes=[C.rows])
    for q in range(2):
        P.dma("pool", Win[:, :, q * 1024:(q + 1) * 1024],
              I["sgu_w_in%d" % l][:, q * 1024:(q + 1) * 1024].rearrange("(k p) f -> p k f", p=128), writes=WinR)
    P.dma("pool", Wout, I["sgu_w_out%d" % l].rearrange("(k p) f -> p k f", p=128), writes=WoutR)
    stg = C.big[:, 8192:8192 + 2048].bitcast(F32).rearrange("p (g t) -> p g t", g=8)
    for q in range(2):
        P.dma("sp", stg, I["sgu_w_sT%d" % l][:, q * 8:(q + 1) * 8, :], writes=[C.big])
        P.tt("dve", C.wsT[:, q * 8:(q + 1) * 8, :], stg,
             C.msk[:, 0:1, :].to_broadcast([128, 8, 128]), ALU.mult, [C.big, C.msk], [C.wsT])
    import os
    STG = int(os.environ.get("SGU_STAGE", "9"))
    if STG < 1:
        return
    norm_and_transpose_group(P, C, gB)
    if STG < 2:
        return
    uT = C.big[:, 0:8192].bitcast(F32).rearrange("p (c t) -> p c t", c=8)
    for c in range(8):
        bk = P.bank()
        for k in range(8):
            P.mm(bk[:, 0:G], Win[:, k, c * 128:(c + 1) * 128], C.hnT[:, k, :], k == 0, k == 7,
                 WinR + [C.hnT], [bk])
        P.act(uT[:, c, :], bk[:, 0:G], AF.Gelu, [bk, X.biu], [C.big], bias=X.biu[:, c:c + 1])
    if STG < 3:
        return
    for j in range(GT):
        st = C.stat[C.nstat % 2]
        C.nstat += 1
        vh = [C.fw[4 + hf] for hf in range(2)]
        for hf in range(2):
            bk = P.bank()
            for k in range(8):
                P.mm(bk[:, 0:512], C.hnT[:, k, j * 128:(j + 1) * 128],
                     Win[:, k, 1024 + hf * 512: 1024 + (hf + 1) * 512],
                     k == 0, False, WinR + [C.hnT], [bk])
            P.mm(bk[:, 0:512], C.ones[32:33, 0:128], C.rows[32:33, hf * 512:(hf + 1) * 512], False, True,
                 [C.ones, C.rows], [bk])
            P.act(vh[hf][:, 0:512], bk[:, 0:512], AF.Gelu, [bk], [vh[hf]])
            P.act(C.hn[0][:, 0:512], vh[hf][:, 0:512], AF.Square, [vh[hf]], [C.hn[0], st],
                  accum_out=st[:, 3 + hf:4 + hf])
        P.tt("dve", st[:, 0:1], st[:, 3:4], st[:, 4:5], ALU.add, [st], [st])
        rstd_from_ss(P, st, 1.0 / D, RMS_EPS)
        if STG < 4:
            continue
        vn = [C.bw[(j % 2) * 4 + q] for q in range(2)]
        yT = [C.bw[(j % 2) * 4 + 2 + q] for q in range(2)]
        for q in range(2):
            P.stt("dve", vn[q][:], vh[q][:, 0:512], st[:, 2:3], gv[:, q * 512:(q + 1) * 512], ALU.mult, ALU.mult,
                  [vh[q], st, gv], [vn[q]])
        for q in range(2):
            bk = P.bank()
            for c4 in range(4):
                for gg in range(2):
                    gi = 8 * q + 2 * c4 + gg
                    gl = 2 * c4 + gg
                    o = bk[gg * 64:(gg + 1) * 64, c4 * 128:(c4 + 1) * 128]
                    P.mm(o, vn[q][:, gl * 64:(gl + 1) * 64], C.wsT[:, gi, :], True, False, [vn[q], C.wsT], [bk])
                    P.mm(o, C.ones[0:1, 0:64], C.rows[0:1, gi * 128:(gi + 1) * 128], False, True,
                         [C.ones, C.rows], [bk])
            P.tt("dve", yT[q][:].rearrange("p (c t) -> p c t", c=4),
                 bk[:, 0:512].rearrange("p (c t) -> p c t", c=4),
                 uT[:, q * 4:(q + 1) * 4, j * 128:(j + 1) * 128], ALU.mult, [bk, C.big], [yT[q]])
        if STG < 5:
            continue
        for hf in range(2):
            bk = P.bank()
            for c in range(8):
                P.mm(bk[:, 0:512], yT[c // 4][:, (c % 4) * 128:(c % 4 + 1) * 128], Wout[:, c, hf * 512:(hf + 1) * 512],
                     c == 0, c == 7, [yT[c // 4]] + WoutR, [bk])
            P.tt("dve", C.h[j][:, hf * 512:(hf + 1) * 512], bk[:, 0:512], C.h[j][:, hf * 512:(hf + 1) * 512], ALU.add,
                 [bk, C.h[j]], [C.h[j]])


def rwkv_setup(P, C, layers):
    C.rw = {}
    if not any(l % 2 == 1 for l in layers):
        return
    C.lw1 = P.sb("lw1", [128, 8, 288], BF16)
    C.w2 = P.sb("w2", [64, D], BF16)
    C.a2 = P.sb("a2", [64, D], BF16)
    C.g2a = P.sb("g2a", [128, D], BF16)
    C.g2b = P.sb("g2b", [32, D], BF16)
    C.lo = P.sb("lo", [128, 512], BF16)
    C.zT = P.sb("zT", [128, 8, 128], BF16)
    C.sst = P.sb("sst", [128, 64], F32)
    for l in layers:
        if l % 2 != 1:
            continue
        X = Ctx()
        C.rw[l] = X
        X.STf = P.sb("STf%d" % l, [128, 8, 64], F32)
        X.STz = P.sb("STz%d" % l, [128, 16, 64], BF16)
        X.carry = P.sb("carry%d" % l, [128, 8, 1], BF16)
        X.mu = P.sb("mu%d" % l, [128, 6, 8], F32)
        P.memset("pool", X.STf[:], 0.0, [X.STf])
        P.memset("pool", X.STz[:], 0.0, [X.STz])
        P.memset("pool", X.carry[:], 0.0, [X.carry])
        P.dma("sp", X.mu[:], C.I["rwkv_mu%d" % l], writes=[X.mu])


def rwkv_mixer(P, C, l, g):
    I = C.I
    X = C.rw[l]
    ar = C.arena
    Wr = ar[:, 0:8192].rearrange("p (k f) -> p k f", k=8)
    Wk = ar[:, 8192:16384].rearrange("p (k f) -> p k f", k=8)
    Wv = ar[:, 16384:24576].rearrange("p (k f) -> p k f", k=8)
    Wo = ar[:, 24576:32768].rearrange("p (k f) -> p k f", k=8)
    WrR, WkR, WvR, WoR = [C.A[0]], [C.A[1]], [C.A[2]], [C.A[3], C.A[4]]
    gB, bkk, bka, brk, blw, blb = C.bc
    bcast_load(P, gB, I["gmix%d" % l])
    bcast_load(P, bkk, I["rwkv_k_k%d" % l])
    bcast_load(P, bka, I["rwkv_k_a%d" % l])
    bcast_load(P, brk, I["rwkv_r_k%d" % l])
    bcast_load(P, blw, I["rwkv_ln_w%d" % l])
    bcast_load(P, blb, I["rwkv_ln_b%d" % l])
    P.dma("pool", C.rows[0:1, 0:2048], I["rwkv_w0a0%d" % l], writes=[C.rows])
    rk = lambda a: a.rearrange("(k p) f -> p k f", p=128)
    P.dma("pool", C.lw1[:, :, 0:64], rk(I["rwkv_w1%d" % l]), writes=[C.lw1])
    P.dma("pool", C.lw1[:, :, 64:128], rk(I["rwkv_a1%d" % l]), writes=[C.lw1])
    P.dma("pool", C.lw1[:, :, 128:288], rk(I["rwkv_g1%d" % l]), writes=[C.lw1])
    P.dma("pool", C.w2[:], I["rwkv_w2%d" % l], writes=[C.w2])
    P.dma("pool", C.a2[:], I["rwkv_a2%d" % l], writes=[C.a2])
    P.dma("pool", C.g2a[:], I["rwkv_g2%d" % l][0:128, :], writes=[C.g2a])
    P.dma("pool", C.g2b[:], I["rwkv_g2%d" % l][128:160, :], writes=[C.g2b])
    P.dma("pool", Wr, rk(I["rwkv_w_r%d" % l]), writes=WrR)
    P.dma("pool", Wk, rk(I["rwkv_w_k%d" % l]), writes=WkR)
    P.dma("pool", Wv, rk(I["rwkv_w_v%d" % l]), writes=WvR)
    P.dma("pool", Wo, rk(I["rwkv_w_o%d" % l]), writes=WoR)

    hx, xsR = Res("hx"), [Res("xs0"), Res("xs1")]
    split_res(C.hnT.res, [hx] + xsR)
    RA, KB, AKBr, XsR = Res("RA"), Res("KB"), Res("AKB"), Res("Xs")
    LsR, WcR = [Res("Ls0"), Res("Ls1")], [Res("Wc0"), Res("Wc1")]
    bigsub = [RA, KB, AKBr, XsR] + LsR + WcR
    split_res(C.big.res, bigsub)
    big = C.big
    RAt4 = big[:, 0:1024].rearrange("p (a q t) -> p a q t", a=4, q=2)
    KBt4 = big[:, 1024:2048].rearrange("p (a q t) -> p a q t", a=4, q=2)
    AKB4 = big[:, 2048:4096].rearrange("p (h q t) -> p h q t", h=4, q=4)
    Xs4 = big[:, 4096:7168].rearrange("p (k h t) -> p k h t", k=6, h=4)
    Ls4 = big[:, 7168:8192].rearrange("p (s h t) -> p s h t", s=2, h=4)
    Wc4 = big[:, 8192:8704].rearrange("p (s h m) -> p s h m", s=2, h=4)
    xs = [C.hnT[:, :, 130 + s * 128: 258 + s * 128] for s in range(2)]
    sst = C.sst
    lo = C.lo
    v3 = lambda t: t[:, 0:512].rearrange("p (h n) -> p h n", h=8)
    bc3 = lambda ap: ap.unsqueeze(2).to_broadcast([128, 8, 64])

    for j in range(GT):
        hn = C.hn[j % 2]
        jk = C.hn[1 - j % 2]
        rmsnorm_to_bf(P, C, C.h[j], gB, hn, jk)
        P.cp("pool", C.hnT[:, :, 1:2], X.carry[:], [X.carry], [hx])
        transpose8(P, C, [hn[:, k * 128:(k + 1) * 128] for k in range(8)], C.hnT[:, :, 2:130], [hn], [hx])
        P.cp("pool", X.carry[:], C.hnT[:, :, 129:130], [hx], [X.carry])
        xx = jk[:].rearrange("p (k t) -> p k t", k=8)
        P.tt("dve", xx, C.hnT[:, :, 1:129], C.hnT[:, :, 2:130], ALU.subtract, [hx], [jk])

        def gen_x(i, slot):
            P.tt("pool", xs[slot], xx, X.mu[:, i, :].unsqueeze(2).to_broadcast([128, 8, 128]), ALU.mult,
                 [jk, X.mu], [xsR[slot]])
            P.tt("pool", xs[slot], xs[slot], C.hnT[:, :, 2:130], ALU.add, [hx, xsR[slot]], [xsR[slot]])

        bk = P.bank()
        gen_x(1, 0)
        for k in range(8):
            P.mm(bk[0:64, 0:128], C.lw1[:, k, 0:64], xs[0][:, k, :], k == 0, k == 7, [C.lw1, xsR[0]], [bk])
        gen_x(4, 1)
        for k in range(8):
            P.mm(bk[0:64, 128:256], C.lw1[:, k, 64:128], xs[1][:, k, :], k == 0, k == 7, [C.lw1, xsR[1]], [bk])
        gen_x(5, 0)
        for k in range(8):
            P.mm(bk[:, 256:384], C.lw1[:, k, 128:256], xs[0][:, k, :], k == 0, k == 7, [C.lw1, xsR[0]], [bk])
        for k in range(8):
            P.mm(bk[0:32, 384:512], C.lw1[:, k, 256:288], xs[0][:, k, :], k == 0, k == 7, [C.lw1, xsR[0]], [bk])
        P.act(lo[0:64, 0:128], bk[0:64, 0:128], AF.Tanh, [bk], [lo])
        P.cp("act", lo[0:64, 128:256], bk[0:64, 128:256], [bk], [lo])
        P.act(lo[:, 256:384], bk[:, 256:384], AF.Sigmoid, [bk], [lo])
        P.act(lo[0:32, 384:512], bk[0:32, 384:512], AF.Sigmoid, [bk], [lo])

        for hf in range(2):
            f0 = hf * 512
            r_t, k_t, v_t, a_t, sg_t, tA, tB, tE = C.fw
            rb, ab, kb, bb, khb, bhb, vb, zb = C.bw

            def proj(i, slot, W, WR, dst):
                gen_x(i, slot)
                bk = P.bank()
                for k in range(8):
                    P.mm(bk[:, 0:512], xs[slot][:, k, :], W[:, k, f0:f0 + 512], k == 0, k == 7,
                         [xsR[slot]] + WR, [bk])
                P.cp("act", dst[:, 0:512], bk[:, 0:512], [bk], [dst])

            proj(0, 1, Wr, WrR, r_t)
            proj(2, 0, Wk, WkR, k_t)
            proj(3, 1, Wv, WvR, v_t)
            bk = P.bank()
            P.mm(bk[:, 0:512], lo[0:64, 0:128], C.w2[0:64, f0:f0 + 512], True, False, [lo, C.w2], [bk])
            P.mm(bk[:, 0:512], C.ones[0:1, 0:128], C.rows[0:1, f0:f0 + 512], False, True, [C.ones, C.rows], [bk])
            P.act(sg_t[:, 0:512], bk[:, 0:512], AF.Sigmoid, [bk], [sg_t])
            bk = P.bank()
            P.mm(bk[:, 0:512], lo[0:64, 128:256], C.a2[0:64, f0:f0 + 512], True, False, [lo, C.a2], [bk])
            P.mm(bk[:, 0:512], C.ones[0:1, 0:128], C.rows[0:1, 1024 + f0:1024 + f0 + 512], False, True,
                 [C.ones, C.rows], [bk])
            P.act(a_t[:, 0:512], bk[:, 0:512], AF.Sigmoid, [bk], [a_t])
            P.tt("pool", tA[:, 0:512], k_t[:, 0:512], bkk[:, f0:f0 + 512], ALU.mult, [k_t, bkk], [tA])
            P.tt("pool", tB[:, 0:512], tA[:, 0:512], tA[:, 0:512], ALU.mult, [tA], [tB])
            P.op("dve", lambda e: e.tensor_reduce(out=sst[:, 0:8], in_=v3(tB), axis=AX.X, op=ALU.add), [tB], [sst])
            P.ts("dve", sst[:, 0:8], sst[:, 0:8], 1e-24, None, ALU.max, None, [sst], [sst])
            P.act(sst[:, 8:16], sst[:, 0:8], AF.Sqrt, [sst], [sst])
            P.op("dve", lambda e: e.reciprocal(out=sst[:, 16:24], in_=sst[:, 8:16]), [sst], [sst])
            P.tt("dve", v3(tA), v3(tA), bc3(sst[:, 16:24]), ALU.mult, [tA, sst], [tA])
            P.stt("dve", tB[:, 0:512], a_t[:, 0:512], -1.0, bka[:, f0:f0 + 512], ALU.add, ALU.mult,
                  [a_t, bka], [tB])
            P.stt("dve", k_t[:, 0:512], tB[:, 0:512], 1.0, k_t[:, 0:512], ALU.add, ALU.mult, [tB, k_t], [k_t])
            P.tt("pool", a_t[:, 0:512], tA[:, 0:512], a_t[:, 0:512], ALU.mult, [tA, a_t], [a_t])
            P.tt("pool", tB[:, 0:512], r_t[:, 0:512], k_t[:, 0:512], ALU.mult, [r_t, k_t], [tB])
            P.tt("pool", tB[:, 0:512], tB[:, 0:512], brk[:, f0:f0 + 512], ALU.mult, [tB, brk], [tB])
            P.op("dve", lambda e: e.tensor_reduce(out=sst[:, 24:32], in_=v3(tB), axis=AX.X, op=ALU.add), [tB], [sst])
            P.cp("pool", vb[:], v_t[:, 0:512], [v_t], [vb])
            bI, bE, bR = P.bank(), P.bank(), P.bank()
            for bq, ti in ((bI, 0), (bE, 1), (bR, 2)):
                P.mm(bq[:, 0:512], C.trin[:, ti, :], sg_t[:, 0:512], True, True, [C.trin, sg_t], [bq])
            bG = P.bank()
            for pp in range(4):
                P.mm(bG[:, 2 * pp:2 * pp + 2], sg_t[:, pp * 128:(pp + 1) * 128], C.onesn[:, 0:2], True, True,
                     [sg_t, C.onesn], [bG])
            P.act(tB[:, 0:512], bI[:, 0:512], AF.Exp, [bI], [tB])
            P.tt("dve", rb[:], r_t[:, 0:512], tB[:, 0:512], ALU.mult, [r_t, tB], [rb])
            P.act(tE[:, 0:512], bE[:, 0:512], AF.Exp, [bE], [tE])
            P.stt("dve", ab[:], tA[:, 0:512], -1.0, tE[:, 0:512], ALU.mult, ALU.mult, [tA, tE], [ab])
            P.act(tB[:, 0:512], bI[:, 0:512], AF.Exp, [bI], [tB], scale=-1.0)
            P.tt("dve", kb[:], k_t[:, 0:512], tB[:, 0:512], ALU.mult, [k_t, tB], [kb])
            P.tt("pool", bb[:], a_t[:, 0:512], tB[:, 0:512], ALU.mult, [a_t, tB], [bb])
            P.act(tE[:, 0:512], bR[:, 0:512], AF.Exp, [bR], [tE])
            P.tt("dve", khb[:], k_t[:, 0:512], tE[:, 0:512], ALU.mult, [k_t, tE], [khb])
            P.tt("pool", bhb[:], a_t[:, 0:512], tE[:, 0:512], ALU.mult, [a_t, tE], [bhb])
            P.act(sst[:, 32:40], bG[:, 0:8], AF.Exp, [bG], [sst])
            srcs = []
            for pp in range(4):
                srcs += [rb[:, pp * 128:(pp + 1) * 128], ab[:, pp * 128:(pp + 1) * 128]]
            transpose8(P, C, srcs, big[:, 0:1024].rearrange("p (k t) -> p k t", k=8), [rb, ab], [RA], "act")
            srcs = []
            for pp in range(4):
                srcs += [kb[:, pp * 128:(pp + 1) * 128], bb[:, pp * 128:(pp + 1) * 128]]
            transpose8(P, C, srcs, big[:, 1024:2048].rearrange("p (k t) -> p k t", k=8), [kb, bb], [KB], "dve")
            y_t = r_t
            for qd in range(2):
                heads = [(hd, qd * 2 + hd // 2, hd % 2, qd * 4 + hd) for hd in range(4)]
                for (qsel, dst_q, mslc) in ((0, 0, None), (1, 2, None)):
                    bnk = [P.bank(), P.bank()]
                    for (hd, pp, hh, hl) in heads:
                        col = (hd // 2) * 256
                        P.mm(bnk[hh][:, col:col + 256], KBt4[hh * 64:(hh + 1) * 64, pp, qsel, :],
                             big[hh * 64:(hh + 1) * 64, pp * 256:(pp + 1) * 256], True, True, [KB, RA], [bnk[hh]])
                    for hh in range(2):
                        P.tt("dve", AKB4[:, hh::2, dst_q:dst_q + 2, :],
                             bnk[hh][:, 0:512].rearrange("p (h q t) -> p h q t", h=2, q=2),
                             C.msk[:, 0:2, :].unsqueeze(1).to_broadcast([128, 2, 2, 128]), ALU.mult,
                             [bnk[hh], C.msk], [AKBr])
                bnk = [P.bank(), P.bank()]
                for (hd, pp, hh, hl) in heads:
                    col = (hd // 2) * 128
                    P.mm(bnk[hh][:, col:col + 128], RAt4[hh * 64:(hh + 1) * 64, pp, 1, :],
                         KBt4[hh * 64:(hh + 1) * 64, pp, 1, :], True, True, [KB, RA], [bnk[hh]])
                for hh in range(2):
                    P.tt("dve", Ls4[:, 0, hh::2, :], bnk[hh][:, 0:256].rearrange("p (h t) -> p h t", h=2),
                         C.msk[:, 2:3, :].to_broadcast([128, 2, 128]), ALU.mult, [bnk[hh], C.msk], [LsR[0]])

                def Xk(kl, hd):
                    return AKB4[:, hd, 3, :] if kl == 0 else Xs4[:, kl - 1, hd, :]
                for kl in range(6):
                    bX = P.bank()
                    for (hd, pp, hh, hl) in heads:
                        P.mm(bX[:, hd * 128:(hd + 1) * 128], Ls4[:, kl % 2, hd, :], Xk(kl, hd), True, True,
                             [LsR[kl % 2], AKBr, XsR], [bX])
                    P.cp("act", Xs4[:, kl, :, :], bX[:, 0:512].rearrange("p (h t) -> p h t", h=4), [bX], [XsR])
                    if kl < 5:
                        bL = P.bank()
                        for (hd, pp, hh, hl) in heads:
                            P.mm(bL[:, hd * 128:(hd + 1) * 128], Xk(kl, hd), Ls4[:, kl % 2, hd, :], True, True,
                                 [LsR[kl % 2], AKBr, XsR], [bL])
                        P.cp("dve", Ls4[:, (kl + 1) % 2, :, :], bL[:, 0:512].rearrange("p (h t) -> p h t", h=4),
                             [bL], [LsR[(kl + 1) % 2]])
                bW = P.bank()
                for (hd, pp, hh, hl) in heads:
                    gh = hf * 8 + hl
                    o = bW[:, hd * 64:(hd + 1) * 64]
                    P.mm(o, RAt4[:, pp, 1, :], X.STz[:, gh, :], True, False, [RA, X.STz], [bW])
                    P.mm(o, AKB4[:, hd, 1, :], vb[:, hl * 64:(hl + 1) * 64], False, True, [AKBr, vb], [bW])
                P.cp("act", Wc4[:, 0, :, :], bW[:, 0:256].rearrange("p (h m) -> p h m", h=4), [bW], [WcR[0]])
                for kl in range(7):
                    bU = P.bank()
                    for (hd, pp, hh, hl) in heads:
                        P.mm(bU[:, hd * 64:(hd + 1) * 64], Xk(kl, hd), Wc4[:, kl % 2, hd, :], True, True,
                             [AKBr, XsR, WcR[kl % 2]], [bU])
                    P.tt("dve", Wc4[:, (kl + 1) % 2, :, :], bU[:, 0:256].rearrange("p (h m) -> p h m", h=4),
                         Wc4[:, kl % 2, :, :], ALU.add, [bU, WcR[kl % 2]], [WcR[(kl + 1) % 2]])
                U = lambda hd: Wc4[:, 1, hd, :]
                bY = P.bank()
                for (hd, pp, hh, hl) in heads:
                    gh = hf * 8 + hl
                    o = bY[:, hd * 64:(hd + 1) * 64]
                    P.mm(o, RAt4[:, pp, 0, :], X.STz[:, gh, :], True, False, [RA, X.STz], [bY])
                    P.mm(o, AKB4[:, hd, 0, :], vb[:, hl * 64:(hl + 1) * 64], False, False, [AKBr, vb], [bY])
                    P.mm(o, AKB4[:, hd, 2, :], U(hd), False, True, [AKBr, WcR[1]], [bY])
                P.cp("act", y_t[:, qd * 256:(qd + 1) * 256], bY[:, 0:256], [bY], [y_t])
                bS = P.bank()
                for (hd, pp, hh, hl) in heads:
                    o = bS[hh * 64:(hh + 1) * 64, (hd // 2) * 64:(hd // 2 + 1) * 64]
                    P.mm(o, khb[:, hl * 64:(hl + 1) * 64], vb[:, hl * 64:(hl + 1) * 64], True, False, [khb, vb], [bS])
                    P.mm(o, bhb[:, hl * 64:(hl + 1) * 64], U(hd), False, True, [bhb, WcR[1]], [bS])
                for pr in range(2):
                    ppl = qd * 2 + pr
                    gp = hf * 4 + ppl
                    P.stt("dve", X.STf[:, gp, :], X.STf[:, gp, :], sst[:, 32 + 2 * ppl:33 + 2 * ppl],
                          bS[:, pr * 64:(pr + 1) * 64], ALU.mult, ALU.add, [X.STf, sst, bS], [X.STf])
                    P.cp("pool", X.STz[0:64, 2 * gp, :], X.STf[0:64, gp, :], [X.STf], [X.STz])
                    P.cp("pool", X.STz[64:128, 2 * gp + 1, :], X.STf[64:128, gp, :], [X.STf], [X.STz])
            P.op("dve", lambda e: e.tensor_reduce(out=sst[:, 40:48], in_=v3(y_t), axis=AX.X, op=ALU.add), [y_t], [sst])
            P.tt("pool", tB[:, 0:512], y_t[:, 0:512], y_t[:, 0:512], ALU.mult, [y_t], [tB])
            P.op("dve", lambda e: e.tensor_reduce(out=sst[:, 48:56], in_=v3(tB), axis=AX.X, op=ALU.add), [tB], [sst])
            P.ts("dve", sst[:, 40:48], sst[:, 40:48], 1.0 / 64, None, ALU.mult, None, [sst], [sst])
            P.tt("dve", sst[:, 56:64], sst[:, 40:48], sst[:, 40:48], ALU.mult, [sst], [sst])
            P.stt("dve", sst[:, 48:56], sst[:, 48:56], 1.0 / 64, sst[:, 56:64], ALU.mult, ALU.subtract, [sst], [sst])
            P.act(sst[:, 56:64], sst[:, 48:56], AF.Sqrt, [sst], [sst], bias=GN_EPS)
            P.op("dve", lambda e: e.reciprocal(out=sst[:, 48:56], in_=sst[:, 56:64]), [sst], [sst])
            P.tt("dve", v3(y_t), v3(y_t), bc3(sst[:, 40:48]), ALU.subtract, [y_t, sst], [y_t])
            P.tt("dve", v3(y_t), v3(y_t), bc3(sst[:, 48:56]), ALU.mult, [y_t, sst], [y_t])
            P.tt("pool", y_t[:, 0:512], y_t[:, 0:512], blw[:, f0:f0 + 512], ALU.mult, [y_t, blw], [y_t])
            P.tt("pool", y_t[:, 0:512], y_t[:, 0:512], blb[:, f0:f0 + 512], ALU.add, [y_t, blb], [y_t])
            P.tt("dve", v3(tB), v3(v_t), bc3(sst[:, 24:32]), ALU.mult, [v_t, sst], [tB])
            P.tt("pool", y_t[:, 0:512], y_t[:, 0:512], tB[:, 0:512], ALU.add, [y_t, tB], [y_t])
            bk = P.bank()
            P.mm(bk[:, 0:512], lo[:, 256:384], C.g2a[:, f0:f0 + 512], True, False, [lo, C.g2a], [bk])
            P.mm(bk[:, 0:512], lo[0:32, 384:512], C.g2b[0:32, f0:f0 + 512], False, True, [lo, C.g2b], [bk])
            P.tt("dve", zb[:], y_t[:, 0:512], bk[:, 0:512], ALU.mult, [y_t, bk], [zb])
            transpose8(P, C, [zb[:, pp * 128:(pp + 1) * 128] for pp in range(4)],
                       C.zT[:, hf * 4:(hf + 1) * 4, :], [zb], [C.zT], "act")
        for hfo in range(2):
            bk = P.bank()
            for c in range(8):
                P.mm(bk[:, 0:512], C.zT[:, c, :], Wo[:, c, hfo * 512:(hfo + 1) * 512], c == 0, c == 7,
                     [C.zT] + WoR, [bk])
            P.tt("dve", C.h[j][:, hfo * 512:(hfo + 1) * 512], bk[:, 0:512], C.h[j][:, hfo * 512:(hfo + 1) * 512],
                 ALU.add, [bk, C.h[j]], [C.h[j]])
    merge_res(C.hnT.res, [hx] + xsR)
    merge_res(C.big.res, bigsub)


def prep_layer_inputs(inp, layers, last):
    c = np.ascontiguousarray
    W = {}
    pp_, ff_ = np.arange(128)[:, None], np.arange(128)[None, :]
    W["cmask"] = c(np.stack([(pp_ <= ff_), (pp_ < ff_), (pp_ > ff_)], axis=1).astype(np.float32))
    for l in layers:
        j = l // 2
        W["gmix%d" % l] = c(inp["norm_mix_g"][l].reshape(1, D))
        W["gffn%d" % l] = c(inp["norm_ffn_g"][l].reshape(1, D))
        if l % 2 == 0:
            W["sgu_w_in%d" % l] = c(inp["sgu_w_in"][j])
            b = inp["sgu_b_in"][j]
            W["sgu_b_in_u%d" % l] = c(b[:D].reshape(8, 128).T)
            W["sgu_b_in_v%d" % l] = c(b[D:].reshape(1, D))
            W["sgu_g_v%d" % l] = c(inp["sgu_g_v"][j].reshape(1, D))
            W["sgu_w_sT%d" % l] = c(np.transpose(inp["sgu_w_s"][j], (2, 0, 1)))
            W["sgu_b_s%d" % l] = c(inp["sgu_b_s"][j].reshape(1, 2048))
            W["sgu_w_out%d" % l] = c(inp["sgu_w_out"][j])
        else:
            W["rwkv_mu%d" % l] = c(np.transpose(inp["rwkv_mu"][j].reshape(6, 8, 128), (2, 0, 1)))
            for nm in ("w_r", "w_k", "w_v", "w_o", "w1", "a1", "g1", "w2", "a2", "g2"):
                W["rwkv_%s%d" % (nm, l)] = c(inp["rwkv_" + nm][j])
            W["rwkv_w0a0%d" % l] = c(np.concatenate([inp["rwkv_w0"][j], inp["rwkv_a0"][j]]).reshape(1, 2048))
            for nm in ("k_k", "k_a", "r_k", "ln_w", "ln_b"):
                W["rwkv_%s%d" % (nm, l)] = c(inp["rwkv_" + nm][j].reshape(1, D))
        W["ffn_w_up%d" % l] = c(inp["ffn_w_up"][l])
        W["ffn_cw%d" % l] = c(np.transpose(inp["ffn_conv_w"][l].reshape(3, 44, 128), (2, 0, 1)))
        W["ffn_cb%d" % l] = c(inp["ffn_conv_b"][l].reshape(44, 128).T)
        W["ffn_w_down%d" % l] = c(inp["ffn_w_down"][l])
    if last:
        W["gfinal"] = c(inp["norm_final_g"].reshape(1, D))
    return W


_NC_CACHE = {}


def run_launch(h, inp, layers, last, ncores=8, **bk):
    key = (tuple(layers), last, tuple(sorted(bk.items())))
    if key not in _NC_CACHE:
        _NC_CACHE[key] = build_program(layers, last=last, **bk)
    nc = _NC_CACHE[key]
    W = prep_layer_inputs(inp, layers, last)
    in_maps = []
    for b in range(ncores):
        m = dict(W)
        m["hin"] = np.ascontiguousarray(h[b])
        in_maps.append(m)
    res = run_bass_kernel_spmd(nc, in_maps, core_ids=list(range(ncores)))
    return np.stack([np.asarray(r["hout"]) for r in res.results], axis=0)


FUSED = False


def kernel(**inputs):
    inp = {k: np.asarray(v) for k, v in inputs.items()}
    h = np.ascontiguousarray(inp["x"], dtype=np.float32)
    if FUSED:
        return run_launch(h, inp, [0, 1, 2, 3], True).astype(np.float32)
    for l in range(DEPTH):
        h = run_launch(h, inp, [l], l == DEPTH - 1)
    return h.astype(np.float32)
```

```python
import numpy as np
from contextlib import ExitStack
import concourse.bass as bass
import concourse.mybir as mybir
from concourse.bass_utils import run_bass_kernel_spmd

F32 = mybir.dt.float32
BF16 = mybir.dt.bfloat16
AF = mybir.ActivationFunctionType
ALU = mybir.AluOpType
AX = mybir.AxisListType

D = 1024
S = 4096
DEPTH = 4
FF = 2816
FC = 22
G = 512
GT = G // 128
NG = S // G
NH = 16
RMS_EPS = 1e-6
GN_EPS = 64e-5


class Res:
    __slots__ = ("name", "w", "r", "dsem")

    def __init__(self, name):
        self.name = name
        self.w = None
        self.r = []
        self.dsem = None


class T:
    def __init__(self, h, res):
        self.h = h
        self.res = res

    def __getitem__(self, k):
        return self.h[k]


def _res(t):
    return t.res if isinstance(t, T) else t


class Prog:
    ENG = ("pe", "dve", "act", "pool", "sp")

    def __init__(self, nc):
        self.nc = nc
        self.es = ExitStack()
        self.streams = {e: [] for e in self.ENG}
        self.count = {e: 0 for e in self.ENG}
        self.semvals = {}
        self.seen = {e: {} for e in self.ENG}
        self.ndsem = 0
        self.banks = []
        self.bank_i = 0

    def sb(self, name, shape, dtype):
        t = self.es.enter_context(self.nc.sbuf_tensor(name, list(shape), dtype))
        return T(t, Res(name))

    def ps(self, name, shape, dtype=F32):
        t = self.es.enter_context(self.nc.psum_tensor(name, list(shape), dtype))
        return T(t, Res(name))

    def bank(self):
        b = self.banks[self.bank_i % len(self.banks)]
        self.bank_i += 1
        return b

    def _need(self, eng, ev, waits):
        if ev is None:
            return
        key, val = ev
        if key == "pe" and eng == "pe":
            return
        if self.seen[eng].get(key, 0) >= val:
            return
        self.seen[eng][key] = val
        for i, (k, v) in enumerate(waits):
            if k == key:
                waits[i] = (k, max(v, val))
                return
        waits.append((key, val))

    def _deps(self, eng, reads, writes):
        waits = []
        for t in reads:
            self._need(eng, _res(t).w, waits)
        for t in writes:
            r = _res(t)
            self._need(eng, r.w, waits)
            for ev in r.r:
                self._need(eng, ev, waits)
        return waits

    def _commit(self, ev, reads, writes):
        for t in reads:
            r = _res(t)
            r.r.append(ev)
            if len(r.r) > 48:
                best = {}
                for k, v in r.r:
                    best[k] = max(best.get(k, 0), v)
                r.r = list(best.items())
        for t in writes:
            r = _res(t)
            r.w = ev
            r.r = []

    def op(self, eng, fn, reads=(), writes=()):
        waits = self._deps(eng, reads, writes)
        self.count[eng] += 1
        ev = (eng, self.count[eng])
        self.streams[eng].append((waits, fn, (eng, 1)))
        self._commit(ev, reads, writes)
        return ev

    def dma(self, q, out_ap, in_ap, reads=(), writes=(), sem_res=None, **kw):
        sr = sem_res if sem_res is not None else (writes[0] if writes else reads[0])
        sr = _res(sr)
        if sr.dsem is None:
            sr.dsem = "d%d" % self.ndsem
            self.ndsem += 1
            self.semvals[sr.dsem] = 0
        key = sr.dsem
        waits = self._deps(q, reads, writes)
        self.semvals[key] += 16
        ev = (key, self.semvals[key])

        def fn(e, out_ap=out_ap, in_ap=in_ap, kw=kw):
            return e.dma_start(out=out_ap, in_=in_ap, **kw)
        self.streams[q].append((waits, fn, (key, 16)))
        self._commit(ev, reads, writes)
        return ev

    def wait_all(self, eng, events):
        waits = []
        for ev in events:
            self._need(eng, ev, waits)
        self.streams[eng].append((waits, None, None))

    def mm(self, out, lhsT, rhs, start, stop, R, W):
        return self.op("pe", lambda e: e.matmul(out, lhsT=lhsT, rhs=rhs, start=start, stop=stop), R, W)

    def tr(self, out, in_, ident, R, W):
        return self.op("pe", lambda e: e.transpose(out, in_, ident[:]), list(R) + [ident], W)

    def act(self, out, in_, func, R, W, **kw):
        return self.op("act", lambda e: e.activation(out=out, in_=in_, func=func, **kw), R, W)

    def tt(self, eng, out, in0, in1, op, R, W):
        return self.op(eng, lambda e: e.tensor_tensor(out=out, in0=in0, in1=in1, op=op), R, W)

    def stt(self, eng, out, in0, scalar, in1, op0, op1, R, W, **kw):
        return self.op(eng, lambda e: e.scalar_tensor_tensor(out=out, in0=in0, scalar=scalar, in1=in1,
                                                             op0=op0, op1=op1, **kw), R, W)

    def ts(self, eng, out, in0, s1, s2, op0, op1, R, W):
        if s2 is None:
            return self.op(eng, lambda e: e.tensor_scalar(out=out, in0=in0, scalar1=s1, scalar2=None, op0=op0), R, W)
        return self.op(eng, lambda e: e.tensor_scalar(out=out, in0=in0, scalar1=s1, scalar2=s2, op0=op0, op1=op1), R, W)

    def cp(self, eng, out, in_, R, W):
        if eng == "act":
            return self.op("act", lambda e: e.activation(out=out, in_=in_, func=AF.Copy), R, W)
        return self.op(eng, lambda e: e.tensor_copy(out=out, in_=in_), R, W)

    def memset(self, eng, out, val, W):
        return self.op(eng, lambda e: e.memset(out, val), (), W)

    def emit(self):
        nc = self.nc
        keys = list(self.ENG) + list(self.semvals.keys())
        sems = {}
        for k in keys:
            sems[k] = self.es.enter_context(nc.semaphore("s_" + k))
        streams = self.streams
        with nc.Block() as block:
            def mk(ename):
                def body(e):
                    for waits, fn, inc in streams[ename]:
                        for (k, v) in waits:
                            e.wait_ge(sems[k], v)
                        if fn is not None:
                            ins = fn(e)
                            ins.then_inc(sems[inc[0]], inc[1])
                return body
            block.tensor(mk("pe"))
            block.vector(mk("dve"))
            block.scalar(mk("act"))
            block.gpsimd(mk("pool"))
            block.sync(mk("sp"))
        self.es.close()


NEGC = -float(np.exp(-0.5))


class Ctx:
    pass


def split_res(parent, children):
    for c in children:
        c.w = parent.w
        c.r = list(parent.r)


def merge_res(parent, children):
    evs = list(parent.r)
    for c in children:
        evs.extend(c.r)
        if c.w is not None:
            evs.append(c.w)
    best = {}
    for k, v in evs:
        best[k] = max(best.get(k, 0), v)
    parent.r = list(best.items())


def declare_inputs(nc, layers, last):
    I = {}

    def din(name, shape):
        I[name] = nc.dram_tensor(name, list(shape), F32, kind="ExternalInput").ap()

    din("hin", [S, D])
    din("cmask", [128, 3, 128])
    for l in layers:
        din("gmix%d" % l, [1, D])
        din("gffn%d" % l, [1, D])
        if l % 2 == 0:
            din("sgu_w_in%d" % l, [D, 2 * D])
            din("sgu_b_in_v%d" % l, [1, D])
            din("sgu_b_in_u%d" % l, [128, 8])
            din("sgu_g_v%d" % l, [1, D])
            din("sgu_w_sT%d" % l, [128, 16, 128])
            din("sgu_b_s%d" % l, [1, 2048])
            din("sgu_w_out%d" % l, [D, D])
        else:
            din("rwkv_mu%d" % l, [128, 6, 8])
            for nm in ("w_r", "w_k", "w_v", "w_o"):
                din("rwkv_%s%d" % (nm, l), [D, D])
            din("rwkv_w0a0%d" % l, [1, 2048])
            din("rwkv_w1%d" % l, [D, 64])
            din("rwkv_a1%d" % l, [D, 64])
            din("rwkv_g1%d" % l, [D, 160])
            din("rwkv_w2%d" % l, [64, D])
            din("rwkv_a2%d" % l, [64, D])
            din("rwkv_g2%d" % l, [160, D])
            for nm in ("k_k", "k_a", "r_k", "ln_w", "ln_b"):
                din("rwkv_%s%d" % (nm, l), [1, D])
        din("ffn_w_up%d" % l, [D, 2 * FF])
        din("ffn_cw%d" % l, [128, 3, 44])
        din("ffn_cb%d" % l, [128, 44])
        din("ffn_w_down%d" % l, [FF, D])
    if last:
        din("gfinal", [1, D])
    O = nc.dram_tensor("hout", [S, D], F32, kind="ExternalOutput").ap()
    return I, O


def build_program(layers, last=True, stop_after=None, ngroups=NG):
    nc = bass.Bass("TRN2", target_bir_lowering=False)
    I, O = declare_inputs(nc, layers, last)
    P = Prog(nc)
    C = Ctx()
    C.I = I
    P.banks = [P.ps("bk%d" % i, [128, 512], F32) for i in range(8)]
    C.ident = P.sb("ident", [128, 128], BF16)
    C.ones = P.sb("ones", [128, 128], BF16)
    C.msk = P.sb("msk", [128, 3, 128], F32)
    C.trin = P.sb("trin", [128, 3, 128], F32)
    C.onesn = P.sb("onesn", [128, 2], F32)
    P.memset("pool", C.ident[:], 1.0, [C.ident])
    P.op("pool", lambda e: e.affine_select(out=C.ident[:], in_=C.ident[:], pattern=[[-1, 128]],
                                           compare_op=ALU.is_equal, fill=0.0, base=0, channel_multiplier=1),
         [C.ident], [C.ident])
    P.memset("pool", C.ones[:], 1.0, [C.ones])
    P.memset("pool", C.onesn[:], NEGC, [C.onesn])
    P.dma("sp", C.msk[:], I["cmask"], writes=[C.msk])
    P.ts("dve", C.trin[:], C.msk[:], NEGC, None, ALU.mult, None, [C.msk], [C.trin])
    C.arena = P.sb("arena", [128, 32768], BF16)
    C.A = [Res("A0"), Res("A1"), Res("A2"), Res("A3a"), Res("A3b")]
    C.h = [P.sb("h%d" % j, [128, D], F32) for j in range(GT)]
    C.bc = [P.sb("bc%d" % i, [128, D], F32) for i in range(6)]
    C.rows = P.sb("rows", [128, 2048], BF16)
    C.hn = [P.sb("hn%d" % i, [128, D], BF16) for i in range(2)]
    C.stat = [P.sb("stat%d" % i, [128, 8], F32) for i in range(2)]
    C.hnT = P.sb("hnT", [128, 8, G], BF16)
    C.fw = [P.sb("fw%d" % i, [128, 520], F32) for i in range(8)]
    C.bw = [P.sb("bw%d" % i, [128, 512], BF16) for i in range(8)]
    C.big = P.sb("big", [128, FC * G], BF16)
    C.nstat = 0
    C.halo, C.cw, C.cb = {}, {}, {}
    for l in layers:
        C.halo[l] = P.sb("halo%d" % l, [128, 44, 2], F32)
        P.memset("pool", C.halo[l][:], 0.0, [C.halo[l]])
        C.cw[l] = P.sb("cw%d" % l, [128, 3, 44], F32)
        C.cb[l] = P.sb("cb%d" % l, [128, 44], F32)
        P.dma("sp", C.cw[l][:], I["ffn_cw%d" % l], writes=[C.cw[l]])
        P.dma("sp", C.cb[l][:], I["ffn_cb%d" % l], writes=[C.cb[l]])
    sgu_setup(P, C, layers)
    rwkv_setup(P, C, layers)

    out_events = []
    for g in range(ngroups):
        for j in range(GT):
            r0 = (g * GT + j) * 128
            P.dma("sp", C.h[j][:], I["hin"][r0:r0 + 128, :], writes=[C.h[j]])
        done = False
        for l in layers:
            if l % 2 == 0:
                sgu_mixer(P, C, l, g)
            else:
                rwkv_mixer(P, C, l, g)
            if stop_after == ("mix", l):
                done = True
                break
            ffn(P, C, l, g)
            if stop_after == ("ffn", l):
                done = True
                break
        if last and not done:
            final_norm(P, C, g)
        for j in range(GT):
            r0 = (g * GT + j) * 128
            out_events.append(P.dma("sp", O[r0:r0 + 128, :], C.h[j][:], reads=[C.h[j]]))
    P.wait_all("sp", out_events)
    P.emit()
    return nc


def bcast_load(P, tile, dram_row):
    P.dma("sp", tile[:], dram_row.partition_broadcast(128), writes=[tile])


def rstd_from_ss(P, st, n_inv, eps):
    P.act(st[:, 1:2], st[:, 0:1], AF.Sqrt, [st], [st], scale=n_inv, bias=eps)
    P.op("dve", lambda e: e.reciprocal(out=st[:, 2:3], in_=st[:, 1:2]), [st], [st])


def rmsnorm_to_bf(P, C, h, gB, out_bf, junk):
    st = C.stat[C.nstat % 2]
    C.nstat += 1
    P.act(junk[:], h[:], AF.Square, [h], [junk, st], accum_out=st[:, 0:1])
    rstd_from_ss(P, st, 1.0 / D, RMS_EPS)
    P.stt("dve", out_bf[:], h[:], st[:, 2:3], gB[:], ALU.mult, ALU.mult, [h, st, gB], [out_bf])


def transpose8(P, C, srcs, dst_ap, R, W, evac_eng="act"):
    bk = P.bank()
    bkb = bk[:].bitcast(BF16)
    n = len(srcs)
    for k, s_ap in enumerate(srcs):
        P.tr(bkb[:, k * 128:(k + 1) * 128], s_ap, C.ident, list(R), [bk])
    P.cp(evac_eng, dst_ap, bkb[:, 0:n * 128].rearrange("p (k t) -> p k t", k=n), [bk], W)


def norm_and_transpose_group(P, C, gB):
    for j in range(GT):
        hn = C.hn[j % 2]
        rmsnorm_to_bf(P, C, C.h[j], gB, hn, C.hn[1 - j % 2])
        transpose8(P, C, [hn[:, k * 128:(k + 1) * 128] for k in range(8)],
                   C.hnT[:, :, j * 128:(j + 1) * 128], [hn], [C.hnT],
                   evac_eng="act" if j % 2 == 0 else "dve")


def ffn(P, C, l, g):
    I = C.I
    Wup = I["ffn_w_up%d" % l]
    Wdn_d = I["ffn_w_down%d" % l]
    gB = C.bc[0]
    bcast_load(P, gB, I["gffn%d" % l])
    ring = [C.arena[:, 24576 + s * 4096: 24576 + (s + 1) * 4096].rearrange("p (k h f) -> p k h f", k=8, h=2)
            for s in range(2)]
    ringR = [C.A[3], C.A[4]]
    Wdn = C.arena[:, 0:FC * 1024].rearrange("p (c f) -> p c f", c=FC)
    WdnR = [C.A[0], C.A[1], C.A[2]]
    actT = C.big[:].rearrange("p (c t) -> p c t", c=FC)

    def load_piece(i):
        s = i % 2
        for hh in range(2):
            c0 = hh * FF + i * 256
            P.dma("pool", ring[s][:, :, hh, :], Wup[:, c0:c0 + 256].rearrange("(k p) f -> p k f", p=128),
                  writes=[ringR[s]])

    load_piece(0)
    load_piece(1)
    for part in range(2):
        P.dma("pool", Wdn[:, part * 11:(part + 1) * 11, :],
              Wdn_d[part * 11 * 128:(part + 1) * 11 * 128, :].rearrange("(c p) f -> p c f", p=128),
              writes=WdnR)
    norm_and_transpose_group(P, C, gB)
    cw, cb, halo = C.cw[l], C.cb[l], C.halo[l]
    for i in range(11):
        s = i % 2
        for cc in range(2):
            ch = 2 * i + cc
            cres = []
            for hh in range(2):
                chh = hh * FC + ch
                bk = P.bank()
                for k in range(8):
                    P.mm(bk[:, 0:G], ring[s][:, k, hh, cc * 128:(cc + 1) * 128], C.hnT[:, k, :],
                         k == 0, k == 7, [ringR[s], C.hnT], [bk])
                zs = C.fw[hh * 2]
                cc_ = C.fw[hh * 2 + 1]
                P.cp("act", zs[:, 2:2 + G], bk[:, 0:G], [bk], [zs])
                P.cp("pool", zs[:, 0:2], halo[:, chh, :], [halo], [zs])
                P.act(cc_[:, 0:G], bk[:, 0:G], AF.Identity, [bk, cw, cb], [cc_],
                      scale=cw[:, 2, chh:chh + 1], bias=cb[:, chh:chh + 1])
                P.stt("dve", cc_[:, 0:G], zs[:, 1:1 + G], cw[:, 1, chh:chh + 1], cc_[:, 0:G], ALU.mult, ALU.add,
                      [zs, cw, cc_], [cc_])
                P.stt("dve", cc_[:, 0:G], zs[:, 0:G], cw[:, 0, chh:chh + 1], cc_[:, 0:G], ALU.mult, ALU.add,
                      [zs, cw, cc_], [cc_])
                P.cp("pool", halo[:, chh, :], zs[:, G:G + 2], [zs], [halo])
                cres.append(cc_)
            sg = C.fw[4]
            P.act(sg[:, 0:G], cres[0][:, 0:G], AF.Silu, [cres[0]], [sg])
            P.tt("dve", actT[:, ch, :], sg[:, 0:G], cres[1][:, 0:G], ALU.mult, [sg, cres[1]], [C.big])
        if i + 2 < 11:
            load_piece(i + 2)
    for j in range(GT):
        for hf in range(2):
            bk = P.bank()
            for c in range(FC):
                P.mm(bk[:, 0:512], actT[:, c, j * 128:(j + 1) * 128], Wdn[:, c, hf * 512:(hf + 1) * 512],
                     c == 0, c == FC - 1, [C.big] + WdnR, [bk])
            P.tt("dve", C.h[j][:, hf * 512:(hf + 1) * 512], bk[:, 0:512], C.h[j][:, hf * 512:(hf + 1) * 512], ALU.add,
                 [bk, C.h[j]], [C.h[j]])


def final_norm(P, C, g):
    gB = C.bc[0]
    bcast_load(P, gB, C.I["gfinal"])
    for j in range(GT):
        h = C.h[j]
        st = C.stat[C.nstat % 2]
        C.nstat += 1
        P.act(C.hn[j % 2][:], h[:], AF.Square, [h], [C.hn[j % 2], st], accum_out=st[:, 0:1])
        rstd_from_ss(P, st, 1.0 / D, RMS_EPS)
        P.stt("dve", h[:], h[:], st[:, 2:3], gB[:], ALU.mult, ALU.mult, [h, st, gB], [h])


def sgu_setup(P, C, layers):
    C.sgu = {}
    if any(l % 2 == 0 for l in layers):
        C.wsT = P.sb("wsT", [128, 16, 128], BF16)
    for l in layers:
        if l % 2 != 0:
            continue
        X = Ctx()
        C.sgu[l] = X
        X.biu = P.sb("biu%d" % l, [128, 8], F32)
        P.dma("sp", X.biu[:], C.I["sgu_b_in_u%d" % l], writes=[X.biu])


def sgu_mixer(P, C, l, g):
    I = C.I
    X = C.sgu[l]
    Win = C.arena[:, 0:16384].rearrange("p (k f) -> p k f", k=8)
    WinR = [C.A[0], C.A[1]]
    Wout = C.arena[:, 16384:24576].rearrange("p (k f) -> p k f", k=8)
    WoutR = [C.A[2]]
    gB, gv = C.bc[0], C.bc[1]
    bcast_load(P, gB, I["gmix%d" % l])
    bcast_load(P, gv, I["sgu_g_v%d" % l])
    P.dma("pool", C.rows[0:1, 0:2048], I["sgu_b_s%d" % l], writes=[C.rows])
    P.dma("pool", C.rows[32:33, 0:1024], I["sgu_b_in_v%d" % l], writes=[C.rows])
    for q in range(2):
        P.dma("pool", Win[:, :, q * 1024:(q + 1) * 1024],
              I["sgu_w_in%d" % l][:, q * 1024:(q + 1) * 1024].rearrange("(k p) f -> p k f", p=128), writes=WinR)
    P.dma("pool", Wout, I["sgu_w_out%d" % l].rearrange("(k p) f -> p k f", p=128), writes=WoutR)
    stg = C.big[:, 8192:8192 + 2048].bitcast(F32).rearrange("p (g t) -> p g t", g=8)
    for q in range(2):
        P.dma("sp", stg, I["sgu_w_sT%d" % l][:, q * 8:(q + 1) * 8, :], writes=[C.big])
        P.tt("dve", C.wsT[:, q * 8:(q + 1) * 8, :], stg,
             C.msk[:, 0:1, :].to_broadcast([128, 8, 128]), ALU.mult, [C.big, C.msk], [C.wsT])
    import os
    STG = int(os.environ.get("SGU_STAGE", "9"))
    if STG < 1:
        return
    norm_and_transpose_group(P, C, gB)
    if STG < 2:
        return
    uT = C.big[:, 0:8192].bitcast(F32).rearrange("p (c t) -> p c t", c=8)
    for c in range(8):
        bk = P.bank()
        for k in range(8):
            P.mm(bk[:, 0:G], Win[:, k, c * 128:(c + 1) * 128], C.hnT[:, k, :], k == 0, k == 7,
                 WinR + [C.hnT], [bk])
        P.act(uT[:, c, :], bk[:, 0:G], AF.Gelu, [bk, X.biu], [C.big], bias=X.biu[:, c:c + 1])
    if STG < 3:
        return
    for j in range(GT):
        st = C.stat[C.nstat % 2]
        C.nstat += 1
        vh = [C.fw[4 + hf] for hf in range(2)]
        for hf in range(2):
            bk = P.bank()
            for k in range(8):
                P.mm(bk[:, 0:512], C.hnT[:, k, j * 128:(j + 1) * 128],
                     Win[:, k, 1024 + hf * 512: 1024 + (hf + 1) * 512],
                     k == 0, False, WinR + [C.hnT], [bk])
            P.mm(bk[:, 0:512], C.ones[32:33, 0:128], C.rows[32:33, hf * 512:(hf + 1) * 512], False, True,
                 [C.ones, C.rows], [bk])
            P.act(vh[hf][:, 0:512], bk[:, 0:512], AF.Gelu, [bk], [vh[hf]])
            P.act(C.hn[0][:, 0:512], vh[hf][:, 0:512], AF.Square, [vh[hf]], [C.hn[0], st],
                  accum_out=st[:, 3 + hf:4 + hf])
        P.tt("dve", st[:, 0:1], st[:, 3:4], st[:, 4:5], ALU.add, [st], [st])
        rstd_from_ss(P, st, 1.0 / D, RMS_EPS)
        if STG < 4:
            continue
        vn = [C.bw[(j % 2) * 4 + q] for q in range(2)]
        yT = [C.bw[(j % 2) * 4 + 2 + q] for q in range(2)]
        for q in range(2):
            P.stt("dve", vn[q][:], vh[q][:, 0:512], st[:, 2:3], gv[:, q * 512:(q + 1) * 512], ALU.mult, ALU.mult,
                  [vh[q], st, gv], [vn[q]])
        for q in range(2):
            bk = P.bank()
            for c4 in range(4):
                for gg in range(2):
                    gi = 8 * q + 2 * c4 + gg
                    gl = 2 * c4 + gg
                    o = bk[gg * 64:(gg + 1) * 64, c4 * 128:(c4 + 1) * 128]
                    P.mm(o, vn[q][:, gl * 64:(gl + 1) * 64], C.wsT[:, gi, :], True, False, [vn[q], C.wsT], [bk])
                    P.mm(o, C.ones[0:1, 0:64], C.rows[0:1, gi * 128:(gi + 1) * 128], False, True,
                         [C.ones, C.rows], [bk])
            P.tt("dve", yT[q][:].rearrange("p (c t) -> p c t", c=4),
                 bk[:, 0:512].rearrange("p (c t) -> p c t", c=4),
                 uT[:, q * 4:(q + 1) * 4, j * 128:(j + 1) * 128], ALU.mult, [bk, C.big], [yT[q]])
        if STG < 5:
            continue
        for hf in range(2):
            bk = P.bank()
            for c in range(8):
                P.mm(bk[:, 0:512], yT[c // 4][:, (c % 4) * 128:(c % 4 + 1) * 128], Wout[:, c, hf * 512:(hf + 1) * 512],
                     c == 0, c == 7, [yT[c // 4]] + WoutR, [bk])
            P.tt("dve", C.h[j][:, hf * 512:(hf + 1) * 512], bk[:, 0:512], C.h[j][:, hf * 512:(hf + 1) * 512], ALU.add,
                 [bk, C.h[j]], [C.h[j]])


def rwkv_setup(P, C, layers):
    C.rw = {}
    if not any(l % 2 == 1 for l in layers):
        return
    C.lw1 = P.sb("lw1", [128, 8, 288], BF16)
    C.w2 = P.sb("w2", [64, D], BF16)
    C.a2 = P.sb("a2", [64, D], BF16)
    C.g2a = P.sb("g2a", [128, D], BF16)
    C.g2b = P.sb("g2b", [32, D], BF16)
    C.lo = P.sb("lo", [128, 512], BF16)
    C.zT = P.sb("zT", [128, 8, 128], BF16)
    C.sst = P.sb("sst", [128, 64], F32)
    for l in layers:
        if l % 2 != 1:
            continue
        X = Ctx()
        C.rw[l] = X
        X.STf = P.sb("STf%d" % l, [128, 8, 64], F32)
        X.STz = P.sb("STz%d" % l, [128, 16, 64], BF16)
        X.carry = P.sb("carry%d" % l, [128, 8, 1], BF16)
        X.mu = P.sb("mu%d" % l, [128, 6, 8], F32)
        P.memset("pool", X.STf[:], 0.0, [X.STf])
        P.memset("pool", X.STz[:], 0.0, [X.STz])
        P.memset("pool", X.carry[:], 0.0, [X.carry])
        P.dma("sp", X.mu[:], C.I["rwkv_mu%d" % l], writes=[X.mu])


def rwkv_mixer(P, C, l, g):
    I = C.I
    X = C.rw[l]
    ar = C.arena
    Wr = ar[:, 0:8192].rearrange("p (k f) -> p k f", k=8)
    Wk = ar[:, 8192:16384].rearrange("p (k f) -> p k f", k=8)
    Wv = ar[:, 16384:24576].rearrange("p (k f) -> p k f", k=8)
    Wo = ar[:, 24576:32768].rearrange("p (k f) -> p k f", k=8)
    WrR, WkR, WvR, WoR = [C.A[0]], [C.A[1]], [C.A[2]], [C.A[3], C.A[4]]
    gB, bkk, bka, brk, blw, blb = C.bc
    bcast_load(P, gB, I["gmix%d" % l])
    bcast_load(P, bkk, I["rwkv_k_k%d" % l])
    bcast_load(P, bka, I["rwkv_k_a%d" % l])
    bcast_load(P, brk, I["rwkv_r_k%d" % l])
    bcast_load(P, blw, I["rwkv_ln_w%d" % l])
    bcast_load(P, blb, I["rwkv_ln_b%d" % l])
    P.dma("pool", C.rows[0:1, 0:2048], I["rwkv_w0a0%d" % l], writes=[C.rows])
    rk = lambda a: a.rearrange("(k p) f -> p k f", p=128)
    P.dma("pool", C.lw1[:, :, 0:64], rk(I["rwkv_w1%d" % l]), writes=[C.lw1])
    P.dma("pool", C.lw1[:, :, 64:128], rk(I["rwkv_a1%d" % l]), writes=[C.lw1])
    P.dma("pool", C.lw1[:, :, 128:288], rk(I["rwkv_g1%d" % l]), writes=[C.lw1])
    P.dma("pool", C.w2[:], I["rwkv_w2%d" % l], writes=[C.w2])
    P.dma("pool", C.a2[:], I["rwkv_a2%d" % l], writes=[C.a2])
    P.dma("pool", C.g2a[:], I["rwkv_g2%d" % l][0:128, :], writes=[C.g2a])
    P.dma("pool", C.g2b[:], I["rwkv_g2%d" % l][128:160, :], writes=[C.g2b])
    P.dma("pool", Wr, rk(I["rwkv_w_r%d" % l]), writes=WrR)
    P.dma("pool", Wk, rk(I["rwkv_w_k%d" % l]), writes=WkR)
    P.dma("pool", Wv, rk(I["rwkv_w_v%d" % l]), writes=WvR)
    P.dma("pool", Wo, rk(I["rwkv_w_o%d" % l]), writes=WoR)

    hx, xsR = Res("hx"), [Res("xs0"), Res("xs1")]
    split_res(C.hnT.res, [hx] + xsR)
    RA, KB, AKBr, XsR = Res("RA"), Res("KB"), Res("AKB"), Res("Xs")
    LsR, WcR = [Res("Ls0"), Res("Ls1")], [Res("Wc0"), Res("Wc1")]
    bigsub = [RA, KB, AKBr, XsR] + LsR + WcR
    split_res(C.big.res, bigsub)
    big = C.big
    RAt4 = big[:, 0:1024].rearrange("p (a q t) -> p a q t", a=4, q=2)
    KBt4 = big[:, 1024:2048].rearrange("p (a q t) -> p a q t", a=4, q=2)
    AKB4 = big[:, 2048:4096].rearrange("p (h q t) -> p h q t", h=4, q=4)
    Xs4 = big[:, 4096:7168].rearrange("p (k h t) -> p k h t", k=6, h=4)
    Ls4 = big[:, 7168:8192].rearrange("p (s h t) -> p s h t", s=2, h=4)
    Wc4 = big[:, 8192:8704].rearrange("p (s h m) -> p s h m", s=2, h=4)
    xs = [C.hnT[:, :, 130 + s * 128: 258 + s * 128] for s in range(2)]
    sst = C.sst
    lo = C.lo
    v3 = lambda t: t[:, 0:512].rearrange("p (h n) -> p h n", h=8)
    bc3 = lambda ap: ap.unsqueeze(2).to_broadcast([128, 8, 64])

    for j in range(GT):
        hn = C.hn[j % 2]
        jk = C.hn[1 - j % 2]
        rmsnorm_to_bf(P, C, C.h[j], gB, hn, jk)
        P.cp("pool", C.hnT[:, :, 1:2], X.carry[:], [X.carry], [hx])
        transpose8(P, C, [hn[:, k * 128:(k + 1) * 128] for k in range(8)], C.hnT[:, :, 2:130], [hn], [hx])
        P.cp("pool", X.carry[:], C.hnT[:, :, 129:130], [hx], [X.carry])
        xx = jk[:].rearrange("p (k t) -> p k t", k=8)
        P.tt("dve", xx, C.hnT[:, :, 1:129], C.hnT[:, :, 2:130], ALU.subtract, [hx], [jk])

        def gen_x(i, slot):
            P.tt("pool", xs[slot], xx, X.mu[:, i, :].unsqueeze(2).to_broadcast([128, 8, 128]), ALU.mult,
                 [jk, X.mu], [xsR[slot]])
            P.tt("pool", xs[slot], xs[slot], C.hnT[:, :, 2:130], ALU.add, [hx, xsR[slot]], [xsR[slot]])

        bk = P.bank()
        gen_x(1, 0)
        for k in range(8):
            P.mm(bk[0:64, 0:128], C.lw1[:, k, 0:64], xs[0][:, k, :], k == 0, k == 7, [C.lw1, xsR[0]], [bk])
        gen_x(4, 1)
        for k in range(8):
            P.mm(bk[0:64, 128:256], C.lw1[:, k, 64:128], xs[1][:, k, :], k == 0, k == 7, [C.lw1, xsR[1]], [bk])
        gen_x(5, 0)
        for k in range(8):
            P.mm(bk[:, 256:384], C.lw1[:, k, 128:256], xs[0][:, k, :], k == 0, k == 7, [C.lw1, xsR[0]], [bk])
        for k in range(8):
            P.mm(bk[0:32, 384:512], C.lw1[:, k, 256:288], xs[0][:, k, :], k == 0, k == 7, [C.lw1, xsR[0]], [bk])
        P.act(lo[0:64, 0:128], bk[0:64, 0:128], AF.Tanh, [bk], [lo])
        P.cp("act", lo[0:64, 128:256], bk[0:64, 128:256], [bk], [lo])
        P.act(lo[:, 256:384], bk[:, 256:384], AF.Sigmoid, [bk], [lo])
        P.act(lo[0:32, 384:512], bk[0:32, 384:512], AF.Sigmoid, [bk], [lo])

        for hf in range(2):
            f0 = hf * 512
            r_t, k_t, v_t, a_t, sg_t, tA, tB, tE = C.fw
            rb, ab, kb, bb, khb, bhb, vb, zb = C.bw

            def proj(i, slot, W, WR, dst):
                gen_x(i, slot)
                bk = P.bank()
                for k in range(8):
                    P.mm(bk[:, 0:512], xs[slot][:, k, :], W[:, k, f0:f0 + 512], k == 0, k == 7,
                         [xsR[slot]] + WR, [bk])
                P.cp("act", dst[:, 0:512], bk[:, 0:512], [bk], [dst])

            proj(0, 1, Wr, WrR, r_t)
            proj(2, 0, Wk, WkR, k_t)
            proj(3, 1, Wv, WvR, v_t)
            bk = P.bank()
            P.mm(bk[:, 0:512], lo[0:64, 0:128], C.w2[0:64, f0:f0 + 512], True, False, [lo, C.w2], [bk])
            P.mm(bk[:, 0:512], C.ones[0:1, 0:128], C.rows[0:1, f0:f0 + 512], False, True, [C.ones, C.rows], [bk])
            P.act(sg_t[:, 0:512], bk[:, 0:512], AF.Sigmoid, [bk], [sg_t])
            bk = P.bank()
            P.mm(bk[:, 0:512], lo[0:64, 128:256], C.a2[0:64, f0:f0 + 512], True, False, [lo, C.a2], [bk])
            P.mm(bk[:, 0:512], C.ones[0:1, 0:128], C.rows[0:1, 1024 + f0:1024 + f0 + 512], False, True,
                 [C.ones, C.rows], [bk])
            P.act(a_t[:, 0:512], bk[:, 0:512], AF.Sigmoid, [bk], [a_t])
            P.tt("pool", tA[:, 0:512], k_t[:, 0:512], bkk[:, f0:f0 + 512], ALU.mult, [k_t, bkk], [tA])
            P.tt("pool", tB[:, 0:512], tA[:, 0:512], tA[:, 0:512], ALU.mult, [tA], [tB])
            P.op("dve", lambda e: e.tensor_reduce(out=sst[:, 0:8], in_=v3(tB), axis=AX.X, op=ALU.add), [tB], [sst])
            P.ts("dve", sst[:, 0:8], sst[:, 0:8], 1e-24, None, ALU.max, None, [sst], [sst])
            P.act(sst[:, 8:16], sst[:, 0:8], AF.Sqrt, [sst], [sst])
            P.op("dve", lambda e: e.reciprocal(out=sst[:, 16:24], in_=sst[:, 8:16]), [sst], [sst])
            P.tt("dve", v3(tA), v3(tA), bc3(sst[:, 16:24]), ALU.mult, [tA, sst], [tA])
            P.stt("dve", tB[:, 0:512], a_t[:, 0:512], -1.0, bka[:, f0:f0 + 512], ALU.add, ALU.mult,
                  [a_t, bka], [tB])
            P.stt("dve", k_t[:, 0:512], tB[:, 0:512], 1.0, k_t[:, 0:512], ALU.add, ALU.mult, [tB, k_t], [k_t])
            P.tt("pool", a_t[:, 0:512], tA[:, 0:512], a_t[:, 0:512], ALU.mult, [tA, a_t], [a_t])
            P.tt("pool", tB[:, 0:512], r_t[:, 0:512], k_t[:, 0:512], ALU.mult, [r_t, k_t], [tB])
            P.tt("pool", tB[:, 0:512], tB[:, 0:512], brk[:, f0:f0 + 512], ALU.mult, [tB, brk], [tB])
            P.op("dve", lambda e: e.tensor_reduce(out=sst[:, 24:32], in_=v3(tB), axis=AX.X, op=ALU.add), [tB], [sst])
            P.cp("pool", vb[:], v_t[:, 0:512], [v_t], [vb])
            bI, bE, bR = P.bank(), P.bank(), P.bank()
            for bq, ti in ((bI, 0), (bE, 1), (bR, 2)):
                P.mm(bq[:, 0:512], C.trin[:, ti, :], sg_t[:, 0:512], True, True, [C.trin, sg_t], [bq])
            bG = P.bank()
            for pp in range(4):
                P.mm(bG[:, 2 * pp:2 * pp + 2], sg_t[:, pp * 128:(pp + 1) * 128], C.onesn[:, 0:2], True, True,
                     [sg_t, C.onesn], [bG])
            P.act(tB[:, 0:512], bI[:, 0:512], AF.Exp, [bI], [tB])
            P.tt("dve", rb[:], r_t[:, 0:512], tB[:, 0:512], ALU.mult, [r_t, tB], [rb])
            P.act(tE[:, 0:512], bE[:, 0:512], AF.Exp, [bE], [tE])
            P.stt("dve", ab[:], tA[:, 0:512], -1.0, tE[:, 0:512], ALU.mult, ALU.mult, [tA, tE], [ab])
            P.act(tB[:, 0:512], bI[:, 0:512], AF.Exp, [bI], [tB], scale=-1.0)
            P.tt("dve", kb[:], k_t[:, 0:512], tB[:, 0:512], ALU.mult, [k_t, tB], [kb])
            P.tt("pool", bb[:], a_t[:, 0:512], tB[:, 0:512], ALU.mult, [a_t, tB], [bb])
            P.act(tE[:, 0:512], bR[:, 0:512], AF.Exp, [bR], [tE])
            P.tt("dve", khb[:], k_t[:, 0:512], tE[:, 0:512], ALU.mult, [k_t, tE], [khb])
            P.tt("pool", bhb[:], a_t[:, 0:512], tE[:, 0:512], ALU.mult, [a_t, tE], [bhb])
            P.act(sst[:, 32:40], bG[:, 0:8], AF.Exp, [bG], [sst])
            srcs = []
            for pp in range(4):
                srcs += [rb[:, pp * 128:(pp + 1) * 128], ab[:, pp * 128:(pp + 1) * 128]]
            transpose8(P, C, srcs, big[:, 0:1024].rearrange("p (k t) -> p k t", k=8), [rb, ab], [RA], "act")
            srcs = []
            for pp in range(4):
                srcs += [kb[:, pp * 128:(pp + 1) * 128], bb[:, pp * 128:(pp + 1) * 128]]
            transpose8(P, C, srcs, big[:, 1024:2048].rearrange("p (k t) -> p k t", k=8), [kb, bb], [KB], "dve")
            y_t = r_t
            for qd in range(2):
                heads = [(hd, qd * 2 + hd // 2, hd % 2, qd * 4 + hd) for hd in range(4)]
                for (qsel, dst_q, mslc) in ((0, 0, None), (1, 2, None)):
                    bnk = [P.bank(), P.bank()]
                    for (hd, pp, hh, hl) in heads:
                        col = (hd // 2) * 256
                        P.mm(bnk[hh][:, col:col + 256], KBt4[hh * 64:(hh + 1) * 64, pp, qsel, :],
                             big[hh * 64:(hh + 1) * 64, pp * 256:(pp + 1) * 256], True, True, [KB, RA], [bnk[hh]])
                    for hh in range(2):
                        P.tt("dve", AKB4[:, hh::2, dst_q:dst_q + 2, :],
                             bnk[hh][:, 0:512].rearrange("p (h q t) -> p h q t", h=2, q=2),
                             C.msk[:, 0:2, :].unsqueeze(1).to_broadcast([128, 2, 2, 128]), ALU.mult,
                             [bnk[hh], C.msk], [AKBr])
                bnk = [P.bank(), P.bank()]
                for (hd, pp, hh, hl) in heads:
                    col = (hd // 2) * 128
                    P.mm(bnk[hh][:, col:col + 128], RAt4[hh * 64:(hh + 1) * 64, pp, 1, :],
                         KBt4[hh * 64:(hh + 1) * 64, pp, 1, :], True, True, [KB, RA], [bnk[hh]])
                for hh in range(2):
                    P.tt("dve", Ls4[:, 0, hh::2, :], bnk[hh][:, 0:256].rearrange("p (h t) -> p h t", h=2),
                         C.msk[:, 2:3, :].to_broadcast([128, 2, 128]), ALU.mult, [bnk[hh], C.msk], [LsR[0]])

                def Xk(kl, hd):
                    return AKB4[:, hd, 3, :] if kl == 0 else Xs4[:, kl - 1, hd, :]
                for kl in range(6):
                    bX = P.bank()
                    for (hd, pp, hh, hl) in heads:
                        P.mm(bX[:, hd * 128:(hd + 1) * 128], Ls4[:, kl % 2, hd, :], Xk(kl, hd), True, True,
                             [LsR[kl % 2], AKBr, XsR], [bX])
                    P.cp("act", Xs4[:, kl, :, :], bX[:, 0:512].rearrange("p (h t) -> p h t", h=4), [bX], [XsR])
                    if kl < 5:
                        bL = P.bank()
                        for (hd, pp, hh, hl) in heads:
                            P.mm(bL[:, hd * 128:(hd + 1) * 128], Xk(kl, hd), Ls4[:, kl % 2, hd, :], True, True,
                                 [LsR[kl % 2], AKBr, XsR], [bL])
                        P.cp("dve", Ls4[:, (kl + 1) % 2, :, :], bL[:, 0:512].rearrange("p (h t) -> p h t", h=4),
                             [bL], [LsR[(kl + 1) % 2]])
                bW = P.bank()
                for (hd, pp, hh, hl) in heads:
                    gh = hf * 8 + hl
                    o = bW[:, hd * 64:(hd + 1) * 64]
                    P.mm(o, RAt4[:, pp, 1, :], X.STz[:, gh, :], True, False, [RA, X.STz], [bW])
                    P.mm(o, AKB4[:, hd, 1, :], vb[:, hl * 64:(hl + 1) * 64], False, True, [AKBr, vb], [bW])
                P.cp("act", Wc4[:, 0, :, :], bW[:, 0:256].rearrange("p (h m) -> p h m", h=4), [bW], [WcR[0]])
                for kl in range(7):
                    bU = P.bank()
                    for (hd, pp, hh, hl) in heads:
                        P.mm(bU[:, hd * 64:(hd + 1) * 64], Xk(kl, hd), Wc4[:, kl % 2, hd, :], True, True,
                             [AKBr, XsR, WcR[kl % 2]], [bU])
                    P.tt("dve", Wc4[:, (kl + 1) % 2, :, :], bU[:, 0:256].rearrange("p (h m) -> p h m", h=4),
                         Wc4[:, kl % 2, :, :], ALU.add, [bU, WcR[kl % 2]], [WcR[(kl + 1) % 2]])
                U = lambda hd: Wc4[:, 1, hd, :]
                bY = P.bank()
                for (hd, pp, hh, hl) in heads:
                    gh = hf * 8 + hl
                    o = bY[:, hd * 64:(hd + 1) * 64]
                    P.mm(o, RAt4[:, pp, 0, :], X.STz[:, gh, :], True, False, [RA, X.STz], [bY])
                    P.mm(o, AKB4[:, hd, 0, :], vb[:, hl * 64:(hl + 1) * 64], False, False, [AKBr, vb], [bY])
                    P.mm(o, AKB4[:, hd, 2, :], U(hd), False, True, [AKBr, WcR[1]], [bY])
                P.cp("act", y_t[:, qd * 256:(qd + 1) * 256], bY[:, 0:256], [bY], [y_t])
                bS = P.bank()
                for (hd, pp, hh, hl) in heads:
                    o = bS[hh * 64:(hh + 1) * 64, (hd // 2) * 64:(hd // 2 + 1) * 64]
                    P.mm(o, khb[:, hl * 64:(hl + 1) * 64], vb[:, hl * 64:(hl + 1) * 64], True, False, [khb, vb], [bS])
                    P.mm(o, bhb[:, hl * 64:(hl + 1) * 64], U(hd), False, True, [bhb, WcR[1]], [bS])
                for pr in range(2):
                    ppl = qd * 2 + pr
                    gp = hf * 4 + ppl
                    P.stt("dve", X.STf[:, gp, :], X.STf[:, gp, :], sst[:, 32 + 2 * ppl:33 + 2 * ppl],
                          bS[:, pr * 64:(pr + 1) * 64], ALU.mult, ALU.add, [X.STf, sst, bS], [X.STf])
                    P.cp("pool", X.STz[0:64, 2 * gp, :], X.STf[0:64, gp, :], [X.STf], [X.STz])
                    P.cp("pool", X.STz[64:128, 2 * gp + 1, :], X.STf[64:128, gp, :], [X.STf], [X.STz])
            P.op("dve", lambda e: e.tensor_reduce(out=sst[:, 40:48], in_=v3(y_t), axis=AX.X, op=ALU.add), [y_t], [sst])
            P.tt("pool", tB[:, 0:512], y_t[:, 0:512], y_t[:, 0:512], ALU.mult, [y_t], [tB])
            P.op("dve", lambda e: e.tensor_reduce(out=sst[:, 48:56], in_=v3(tB), axis=AX.X, op=ALU.add), [tB], [sst])
            P.ts("dve", sst[:, 40:48], sst[:, 40:48], 1.0 / 64, None, ALU.mult, None, [sst], [sst])
            P.tt("dve", sst[:, 56:64], sst[:, 40:48], sst[:, 40:48], ALU.mult, [sst], [sst])
            P.stt("dve", sst[:, 48:56], sst[:, 48:56], 1.0 / 64, sst[:, 56:64], ALU.mult, ALU.subtract, [sst], [sst])
            P.act(sst[:, 56:64], sst[:, 48:56], AF.Sqrt, [sst], [sst], bias=GN_EPS)
            P.op("dve", lambda e: e.reciprocal(out=sst[:, 48:56], in_=sst[:, 56:64]), [sst], [sst])
            P.tt("dve", v3(y_t), v3(y_t), bc3(sst[:, 40:48]), ALU.subtract, [y_t, sst], [y_t])
            P.tt("dve", v3(y_t), v3(y_t), bc3(sst[:, 48:56]), ALU.mult, [y_t, sst], [y_t])
            P.tt("pool", y_t[:, 0:512], y_t[:, 0:512], blw[:, f0:f0 + 512], ALU.mult, [y_t, blw], [y_t])
            P.tt("pool", y_t[:, 0:512], y_t[:, 0:512], blb[:, f0:f0 + 512], ALU.add, [y_t, blb], [y_t])
            P.tt("dve", v3(tB), v3(v_t), bc3(sst[:, 24:32]), ALU.mult, [v_t, sst], [tB])
            P.tt("pool", y_t[:, 0:512], y_t[:, 0:512], tB[:, 0:512], ALU.add, [y_t, tB], [y_t])
            bk = P.bank()
            P.mm(bk[:, 0:512], lo[:, 256:384], C.g2a[:, f0:f0 + 512], True, False, [lo, C.g2a], [bk])
            P.mm(bk[:, 0:512], lo[0:32, 384:512], C.g2b[0:32, f0:f0 + 512], False, True, [lo, C.g2b], [bk])
            P.tt("dve", zb[:], y_t[:, 0:512], bk[:, 0:512], ALU.mult, [y_t, bk], [zb])
            transpose8(P, C, [zb[:, pp * 128:(pp + 1) * 128] for pp in range(4)],
                       C.zT[:, hf * 4:(hf + 1) * 4, :], [zb], [C.zT], "act")
        for hfo in range(2):
            bk = P.bank()
            for c in range(8):
                P.mm(bk[:, 0:512], C.zT[:, c, :], Wo[:, c, hfo * 512:(hfo + 1) * 512], c == 0, c == 7,
                     [C.zT] + WoR, [bk])
            P.tt("dve", C.h[j][:, hfo * 512:(hfo + 1) * 512], bk[:, 0:512], C.h[j][:, hfo * 512:(hfo + 1) * 512],
                 ALU.add, [bk, C.h[j]], [C.h[j]])
    merge_res(C.hnT.res, [hx] + xsR)
    merge_res(C.big.res, bigsub)


def prep_layer_inputs(inp, layers, last):
    c = np.ascontiguousarray
    W = {}
    pp_, ff_ = np.arange(128)[:, None], np.arange(128)[None, :]
    W["cmask"] = c(np.stack([(pp_ <= ff_), (pp_ < ff_), (pp_ > ff_)], axis=1).astype(np.float32))
    for l in layers:
        j = l // 2
        W["gmix%d" % l] = c(inp["norm_mix_g"][l].reshape(1, D))
        W["gffn%d" % l] = c(inp["norm_ffn_g"][l].reshape(1, D))
        if l % 2 == 0:
            W["sgu_w_in%d" % l] = c(inp["sgu_w_in"][j])
            b = inp["sgu_b_in"][j]
            W["sgu_b_in_u%d" % l] = c(b[:D].reshape(8, 128).T)
            W["sgu_b_in_v%d" % l] = c(b[D:].reshape(1, D))
            W["sgu_g_v%d" % l] = c(inp["sgu_g_v"][j].reshape(1, D))
            W["sgu_w_sT%d" % l] = c(np.transpose(inp["sgu_w_s"][j], (2, 0, 1)))
            W["sgu_b_s%d" % l] = c(inp["sgu_b_s"][j].reshape(1, 2048))
            W["sgu_w_out%d" % l] = c(inp["sgu_w_out"][j])
        else:
            W["rwkv_mu%d" % l] = c(np.transpose(inp["rwkv_mu"][j].reshape(6, 8, 128), (2, 0, 1)))
            for nm in ("w_r", "w_k", "w_v", "w_o", "w1", "a1", "g1", "w2", "a2", "g2"):
                W["rwkv_%s%d" % (nm, l)] = c(inp["rwkv_" + nm][j])
            W["rwkv_w0a0%d" % l] = c(np.concatenate([inp["rwkv_w0"][j], inp["rwkv_a0"][j]]).reshape(1, 2048))
            for nm in ("k_k", "k_a", "r_k", "ln_w", "ln_b"):
                W["rwkv_%s%d" % (nm, l)] = c(inp["rwkv_" + nm][j].reshape(1, D))
        W["ffn_w_up%d" % l] = c(inp["ffn_w_up"][l])
        W["ffn_cw%d" % l] = c(np.transpose(inp["ffn_conv_w"][l].reshape(3, 44, 128), (2, 0, 1)))
        W["ffn_cb%d" % l] = c(inp["ffn_conv_b"][l].reshape(44, 128).T)
        W["ffn_w_down%d" % l] = c(inp["ffn_w_down"][l])
    if last:
        W["gfinal"] = c(inp["norm_final_g"].reshape(1, D))
    return W


_NC_CACHE = {}


def run_launch(h, inp, layers, last, ncores=8, **bk):
    key = (tuple(layers), last, tuple(sorted(bk.items())))
    if key not in _NC_CACHE:
        _NC_CACHE[key] = build_program(layers, last=last, **bk)
    nc = _NC_CACHE[key]
    W = prep_layer_inputs(inp, layers, last)
    in_maps = []
    for b in range(ncores):
        m = dict(W)
        m["hin"] = np.ascontiguousarray(h[b])
        in_maps.append(m)
    res = run_bass_kernel_spmd(nc, in_maps, core_ids=list(range(ncores)))
    return np.stack([np.asarray(r["hout"]) for r in res.results], axis=0)


FUSED = True


def kernel(**inputs):
    inp = {k: np.asarray(v) for k, v in inputs.items()}
    h = np.ascontiguousarray(inp["x"], dtype=np.float32)
    if FUSED:
        return run_launch(h, inp, [0, 1, 2, 3], True).astype(np.float32)
    for l in range(DEPTH):
        h = run_launch(h, inp, [l], l == DEPTH - 1)
    return h.astype(np.float32)
```

```python
import numpy as np
from contextlib import ExitStack
import concourse.bass as bass
import concourse.mybir as mybir
from concourse.bass_utils import run_bass_kernel_spmd

F32 = mybir.dt.float32
BF16 = mybir.dt.bfloat16
AF = mybir.ActivationFunctionType
ALU = mybir.AluOpType
AX = mybir.AxisListType

D = 1024
S = 4096
DEPTH = 4
FF = 2816
FC = 22
G = 512
GT = G // 128
NG = S // G
NH = 16
RMS_EPS = 1e-6
GN_EPS = 64e-5


class Res:
    __slots__ = ("name", "w", "r", "dsem")

    def __init__(self, name):
        self.name = name
        self.w = None
        self.r = []
        self.dsem = None


class T:
    def __init__(self, h, res):
        self.h = h
        self.res = res

    def __getitem__(self, k):
        return self.h[k]


def _res(t):
    return t.res if isinstance(t, T) else t


class Prog:
    ENG = ("pe", "dve", "act", "pool", "sp")

    def __init__(self, nc):
        self.nc = nc
        self.es = ExitStack()
        self.streams = {e: [] for e in self.ENG}
        self.count = {e: 0 for e in self.ENG}
        self.semvals = {}
        self.seen = {e: {} for e in self.ENG}
        self.ndsem = 0
        self.banks = []
        self.bank_i = 0

    def sb(self, name, shape, dtype):
        t = self.es.enter_context(self.nc.sbuf_tensor(name, list(shape), dtype))
        return T(t, Res(name))

    def ps(self, name, shape, dtype=F32):
        t = self.es.enter_context(self.nc.psum_tensor(name, list(shape), dtype))
        return T(t, Res(name))

    def bank(self):
        b = self.banks[self.bank_i % len(self.banks)]
        self.bank_i += 1
        return b

    def _need(self, eng, ev, waits):
        if ev is None:
            return
        key, val = ev
        if key == "pe" and eng == "pe":
            return
        if self.seen[eng].get(key, 0) >= val:
            return
        self.seen[eng][key] = val
        for i, (k, v) in enumerate(waits):
            if k == key:
                waits[i] = (k, max(v, val))
                return
        waits.append((key, val))

    def _deps(self, eng, reads, writes):
        waits = []
        for t in reads:
            self._need(eng, _res(t).w, waits)
        for t in writes:
            r = _res(t)
            self._need(eng, r.w, waits)
            for ev in r.r:
                self._need(eng, ev, waits)
        return waits

    def _commit(self, ev, reads, writes):
        for t in reads:
            r = _res(t)
            r.r.append(ev)
            if len(r.r) > 48:
                best = {}
                for k, v in r.r:
                    best[k] = max(best.get(k, 0), v)
                r.r = list(best.items())
        for t in writes:
            r = _res(t)
            r.w = ev
            r.r = []

    def op(self, eng, fn, reads=(), writes=()):
        waits = self._deps(eng, reads, writes)
        self.count[eng] += 1
        ev = (eng, self.count[eng])
        self.streams[eng].append((waits, fn, (eng, 1)))
        self._commit(ev, reads, writes)
        return ev

    def dma(self, q, out_ap, in_ap, reads=(), writes=(), sem_res=None, **kw):
        sr = sem_res if sem_res is not None else (writes[0] if writes else reads[0])
        sr = _res(sr)
        if sr.dsem is None:
            sr.dsem = "d%d" % self.ndsem
            self.ndsem += 1
            self.semvals[sr.dsem] = 0
        key = sr.dsem
        waits = self._deps(q, reads, writes)
        self.semvals[key] += 16
        ev = (key, self.semvals[key])

        def fn(e, out_ap=out_ap, in_ap=in_ap, kw=kw):
            return e.dma_start(out=out_ap, in_=in_ap, **kw)
        self.streams[q].append((waits, fn, (key, 16)))
        self._commit(ev, reads, writes)
        return ev

    def wait_all(self, eng, events):
        waits = []
        for ev in events:
            self._need(eng, ev, waits)
        self.streams[eng].append((waits, None, None))

    def mm(self, out, lhsT, rhs, start, stop, R, W):
        return self.op("pe", lambda e: e.matmul(out, lhsT=lhsT, rhs=rhs, start=start, stop=stop), R, W)

    def tr(self, out, in_, ident, R, W):
        return self.op("pe", lambda e: e.transpose(out, in_, ident[:]), list(R) + [ident], W)

    def act(self, out, in_, func, R, W, **kw):
        return self.op("act", lambda e: e.activation(out=out, in_=in_, func=func, **kw), R, W)

    def tt(self, eng, out, in0, in1, op, R, W):
        return self.op(eng, lambda e: e.tensor_tensor(out=out, in0=in0, in1=in1, op=op), R, W)

    def stt(self, eng, out, in0, scalar, in1, op0, op1, R, W, **kw):
        return self.op(eng, lambda e: e.scalar_tensor_tensor(out=out, in0=in0, scalar=scalar, in1=in1,
                                                             op0=op0, op1=op1, **kw), R, W)

    def ts(self, eng, out, in0, s1, s2, op0, op1, R, W):
        if s2 is None:
            return self.op(eng, lambda e: e.tensor_scalar(out=out, in0=in0, scalar1=s1, scalar2=None, op0=op0), R, W)
        return self.op(eng, lambda e: e.tensor_scalar(out=out, in0=in0, scalar1=s1, scalar2=s2, op0=op0, op1=op1), R, W)

    def cp(self, eng, out, in_, R, W):
        if eng == "act":
            return self.op("act", lambda e: e.activation(out=out, in_=in_, func=AF.Copy), R, W)
        return self.op(eng, lambda e: e.tensor_copy(out=out, in_=in_), R, W)

    def memset(self, eng, out, val, W):
        return self.op(eng, lambda e: e.memset(out, val), (), W)

    def emit(self):
        nc = self.nc
        keys = list(self.ENG) + list(self.semvals.keys())
        sems = {}
        for k in keys:
            sems[k] = self.es.enter_context(nc.semaphore("s_" + k))
        streams = self.streams
        with nc.Block() as block:
            def mk(ename):
                def body(e):
                    for waits, fn, inc in streams[ename]:
                        for (k, v) in waits:
                            e.wait_ge(sems[k], v)
                        if fn is not None:
                            ins = fn(e)
                            ins.then_inc(sems[inc[0]], inc[1])
                return body
            block.tensor(mk("pe"))
            block.vector(mk("dve"))
            block.scalar(mk("act"))
            block.gpsimd(mk("pool"))
            block.sync(mk("sp"))
        self.es.close()


NEGC = -float(np.exp(-0.5))


class Ctx:
    pass


def split_res(parent, children):
    for c in children:
        c.w = parent.w
        c.r = list(parent.r)


def merge_res(parent, children):
    evs = list(parent.r)
    for c in children:
        evs.extend(c.r)
        if c.w is not None:
            evs.append(c.w)
    best = {}
    for k, v in evs:
        best[k] = max(best.get(k, 0), v)
    parent.r = list(best.items())


def declare_inputs(nc, layers, last):
    I = {}

    def din(name, shape):
        I[name] = nc.dram_tensor(name, list(shape), F32, kind="ExternalInput").ap()

    din("hin", [S, D])
    din("cmask", [128, 3, 128])
    for l in layers:
        din("gmix%d" % l, [1, D])
        din("gffn%d" % l, [1, D])
        if l % 2 == 0:
            din("sgu_w_in%d" % l, [D, 2 * D])
            din("sgu_b_in_v%d" % l, [1, D])
            din("sgu_b_in_u%d" % l, [128, 8])
            din("sgu_g_v%d" % l, [1, D])
            din("sgu_w_sT%d" % l, [128, 16, 128])
            din("sgu_b_s%d" % l, [1, 2048])
            din("sgu_w_out%d" % l, [D, D])
        else:
            din("rwkv_mu%d" % l, [128, 6, 8])
            for nm in ("w_r", "w_k", "w_v", "w_o"):
                din("rwkv_%s%d" % (nm, l), [D, D])
            din("rwkv_w0a0%d" % l, [1, 2048])
            din("rwkv_w1%d" % l, [D, 64])
            din("rwkv_a1%d" % l, [D, 64])
            din("rwkv_g1%d" % l, [D, 160])
            din("rwkv_w2%d" % l, [64, D])
            din("rwkv_a2%d" % l, [64, D])
            din("rwkv_g2%d" % l, [160, D])
            for nm in ("k_k", "k_a", "r_k", "ln_w", "ln_b"):
                din("rwkv_%s%d" % (nm, l), [1, D])
        din("ffn_w_up%d" % l, [D, 2 * FF])
        din("ffn_cw%d" % l, [128, 3, 44])
        din("ffn_cb%d" % l, [128, 44])
        din("ffn_w_down%d" % l, [FF, D])
    if last:
        din("gfinal", [1, D])
    O = nc.dram_tensor("hout", [S, D], F32, kind="ExternalOutput").ap()
    return I, O


def build_program(layers, last=True, stop_after=None, ngroups=NG):
    nc = bass.Bass("TRN2", target_bir_lowering=False)
    I, O = declare_inputs(nc, layers, last)
    P = Prog(nc)
    C = Ctx()
    C.I = I
    P.banks = [P.ps("bk%d" % i, [128, 512], F32) for i in range(8)]
    C.ident = P.sb("ident", [128, 128], BF16)
    C.ones = P.sb("ones", [128, 128], BF16)
    C.msk = P.sb("msk", [128, 3, 128], F32)
    C.trin = P.sb("trin", [128, 3, 128], F32)
    C.onesn = P.sb("onesn", [128, 2], F32)
    P.memset("pool", C.ident[:], 1.0, [C.ident])
    P.op("pool", lambda e: e.affine_select(out=C.ident[:], in_=C.ident[:], pattern=[[-1, 128]],
                                           compare_op=ALU.is_equal, fill=0.0, base=0, channel_multiplier=1),
         [C.ident], [C.ident])
    P.memset("pool", C.ones[:], 1.0, [C.ones])
    P.memset("pool", C.onesn[:], NEGC, [C.onesn])
    P.dma("sp", C.msk[:], I["cmask"], writes=[C.msk])
    P.ts("dve", C.trin[:], C.msk[:], NEGC, None, ALU.mult, None, [C.msk], [C.trin])
    C.arena = P.sb("arena", [128, 32768], BF16)
    C.A = [Res("A0"), Res("A1"), Res("A2"), Res("A3a"), Res("A3b")]
    C.h = [P.sb("h%d" % j, [128, D], F32) for j in range(GT)]
    C.bc = [P.sb("bc%d" % i, [128, D], F32) for i in range(6)]
    C.rows = P.sb("rows", [128, 2048], BF16)
    C.hn = [P.sb("hn%d" % i, [128, D], BF16) for i in range(2)]
    C.stat = [P.sb("stat%d" % i, [128, 8], F32) for i in range(2)]
    C.hnTfull = P.sb("hnT", [128, 8, 642], BF16)
    C.hnT = T(C.hnTfull.h[:, :, 0:G], C.hnTfull.res)
    C.fw = [P.sb("fw%d" % i, [128, 520], F32) for i in range(8)]
    C.bw = [P.sb("bw%d" % i, [128, 512], BF16) for i in range(8)]
    C.big = P.sb("big", [128, FC * G], BF16)
    C.nstat = 0
    C.halo, C.cw, C.cb = {}, {}, {}
    for l in layers:
        C.halo[l] = P.sb("halo%d" % l, [128, 44, 2], F32)
        P.memset("pool", C.halo[l][:], 0.0, [C.halo[l]])
        C.cw[l] = P.sb("cw%d" % l, [128, 3, 44], F32)
        C.cb[l] = P.sb("cb%d" % l, [128, 44], F32)
        P.dma("sp", C.cw[l][:], I["ffn_cw%d" % l], writes=[C.cw[l]])
        P.dma("sp", C.cb[l][:], I["ffn_cb%d" % l], writes=[C.cb[l]])
    sgu_setup(P, C, layers)
    rwkv_setup(P, C, layers)

    out_events = []
    for g in range(ngroups):
        for j in range(GT):
            r0 = (g * GT + j) * 128
            P.dma("sp", C.h[j][:], I["hin"][r0:r0 + 128, :], writes=[C.h[j]])
        done = False
        for l in layers:
            if l % 2 == 0:
                sgu_mixer(P, C, l, g)
            else:
                rwkv_mixer(P, C, l, g)
            if stop_after == ("mix", l):
                done = True
                break
            ffn(P, C, l, g)
            if stop_after == ("ffn", l):
                done = True
                break
        if last and not done:
            final_norm(P, C, g)
        for j in range(GT):
            r0 = (g * GT + j) * 128
            out_events.append(P.dma("sp", O[r0:r0 + 128, :], C.h[j][:], reads=[C.h[j]]))
    P.wait_all("sp", out_events)
    P.emit()
    return nc


def bcast_load(P, tile, dram_row):
    P.dma("sp", tile[:], dram_row.partition_broadcast(128), writes=[tile])


def rstd_from_ss(P, st, n_inv, eps):
    P.act(st[:, 1:2], st[:, 0:1], AF.Sqrt, [st], [st], scale=n_inv, bias=eps)
    P.op("dve", lambda e: e.reciprocal(out=st[:, 2:3], in_=st[:, 1:2]), [st], [st])


def rmsnorm_to_bf(P, C, h, gB, out_bf, junk):
    st = C.stat[C.nstat % 2]
    C.nstat += 1
    P.act(junk[:], h[:], AF.Square, [h], [junk, st], accum_out=st[:, 0:1])
    rstd_from_ss(P, st, 1.0 / D, RMS_EPS)
    P.stt("dve", out_bf[:], h[:], st[:, 2:3], gB[:], ALU.mult, ALU.mult, [h, st, gB], [out_bf])


def transpose8(P, C, srcs, dst_ap, R, W, evac_eng="act"):
    bk = P.bank()
    bkb = bk[:].bitcast(BF16)
    n = len(srcs)
    for k, s_ap in enumerate(srcs):
        P.tr(bkb[:, k * 128:(k + 1) * 128], s_ap, C.ident, list(R), [bk])
    P.cp(evac_eng, dst_ap, bkb[:, 0:n * 128].rearrange("p (k t) -> p k t", k=n), [bk], W)


def norm_and_transpose_group(P, C, gB):
    for j in range(GT):
        hn = C.hn[j % 2]
        rmsnorm_to_bf(P, C, C.h[j], gB, hn, C.hn[1 - j % 2])
        transpose8(P, C, [hn[:, k * 128:(k + 1) * 128] for k in range(8)],
                   C.hnT[:, :, j * 128:(j + 1) * 128], [hn], [C.hnT],
                   evac_eng="act" if j % 2 == 0 else "dve")


def ffn(P, C, l, g):
    I = C.I
    Wup = I["ffn_w_up%d" % l]
    Wdn_d = I["ffn_w_down%d" % l]
    gB = C.bc[0]
    bcast_load(P, gB, I["gffn%d" % l])
    ring = [C.arena[:, 24576 + s * 4096: 24576 + (s + 1) * 4096].rearrange("p (k h f) -> p k h f", k=8, h=2)
            for s in range(2)]
    ringR = [C.A[3], C.A[4]]
    Wdn = C.arena[:, 0:FC * 1024].rearrange("p (c f) -> p c f", c=FC)
    WdnR = [C.A[0], C.A[1], C.A[2]]
    actT = C.big[:].rearrange("p (c t) -> p c t", c=FC)

    def load_piece(i):
        s = i % 2
        for hh in range(2):
            c0 = hh * FF + i * 256
            P.dma("pool", ring[s][:, :, hh, :], Wup[:, c0:c0 + 256].rearrange("(k p) f -> p k f", p=128),
                  writes=[ringR[s]])

    load_piece(0)
    load_piece(1)
    for part in range(2):
        P.dma("pool", Wdn[:, part * 11:(part + 1) * 11, :],
              Wdn_d[part * 11 * 128:(part + 1) * 11 * 128, :].rearrange("(c p) f -> p c f", p=128),
              writes=WdnR)
    norm_and_transpose_group(P, C, gB)
    cw, cb, halo = C.cw[l], C.cb[l], C.halo[l]
    for i in range(11):
        s = i % 2
        for cc in range(2):
            ch = 2 * i + cc
            cres = []
            for hh in range(2):
                chh = hh * FC + ch
                bk = P.bank()
                for k in range(8):
                    P.mm(bk[:, 0:G], ring[s][:, k, hh, cc * 128:(cc + 1) * 128], C.hnT[:, k, :],
                         k == 0, k == 7, [ringR[s], C.hnT], [bk])
                zs = C.fw[hh * 2]
                cc_ = C.fw[hh * 2 + 1]
                P.cp("act", zs[:, 2:2 + G], bk[:, 0:G], [bk], [zs])
                P.cp("dve", zs[:, 0:2], halo[:, chh, :], [halo], [zs])
                P.act(cc_[:, 0:G], bk[:, 0:G], AF.Identity, [bk, cw, cb], [cc_],
                      scale=cw[:, 2, chh:chh + 1], bias=cb[:, chh:chh + 1])
                P.stt("dve", cc_[:, 0:G], zs[:, 1:1 + G], cw[:, 1, chh:chh + 1], cc_[:, 0:G], ALU.mult, ALU.add,
                      [zs, cw, cc_], [cc_])
                P.stt("dve", cc_[:, 0:G], zs[:, 0:G], cw[:, 0, chh:chh + 1], cc_[:, 0:G], ALU.mult, ALU.add,
                      [zs, cw, cc_], [cc_])
                P.cp("dve", halo[:, chh, :], zs[:, G:G + 2], [zs], [halo])
                cres.append(cc_)
            sg = C.fw[4]
            P.act(sg[:, 0:G], cres[0][:, 0:G], AF.Silu, [cres[0]], [sg])
            P.tt("dve", actT[:, ch, :], sg[:, 0:G], cres[1][:, 0:G], ALU.mult, [sg, cres[1]], [C.big])
        if i + 2 < 11:
            load_piece(i + 2)
    for j in range(GT):
        for hf in range(2):
            bk = P.bank()
            for c in range(FC):
                P.mm(bk[:, 0:512], actT[:, c, j * 128:(j + 1) * 128], Wdn[:, c, hf * 512:(hf + 1) * 512],
                     c == 0, c == FC - 1, [C.big] + WdnR, [bk])
            P.tt("dve", C.h[j][:, hf * 512:(hf + 1) * 512], bk[:, 0:512], C.h[j][:, hf * 512:(hf + 1) * 512], ALU.add,
                 [bk, C.h[j]], [C.h[j]])


def final_norm(P, C, g):
    gB = C.bc[0]
    bcast_load(P, gB, C.I["gfinal"])
    for j in range(GT):
        h = C.h[j]
        st = C.stat[C.nstat % 2]
        C.nstat += 1
        P.act(C.hn[j % 2][:], h[:], AF.Square, [h], [C.hn[j % 2], st], accum_out=st[:, 0:1])
        rstd_from_ss(P, st, 1.0 / D, RMS_EPS)
        P.stt("dve", h[:], h[:], st[:, 2:3], gB[:], ALU.mult, ALU.mult, [h, st, gB], [h])


def sgu_setup(P, C, layers):
    C.sgu = {}
    if any(l % 2 == 0 for l in layers):
        C.wsT = P.sb("wsT", [128, 16, 128], BF16)
    for l in layers:
        if l % 2 != 0:
            continue
        X = Ctx()
        C.sgu[l] = X
        X.biu = P.sb("biu%d" % l, [128, 8], F32)
        P.dma("sp", X.biu[:], C.I["sgu_b_in_u%d" % l], writes=[X.biu])


def sgu_mixer(P, C, l, g):
    I = C.I
    X = C.sgu[l]
    Win = C.arena[:, 0:16384].rearrange("p (k f) -> p k f", k=8)
    WinR = [C.A[0], C.A[1]]
    Wout = C.arena[:, 16384:24576].rearrange("p (k f) -> p k f", k=8)
    WoutR = [C.A[2]]
    gB, gv = C.bc[0], C.bc[1]
    bcast_load(P, gB, I["gmix%d" % l])
    bcast_load(P, gv, I["sgu_g_v%d" % l])
    P.dma("pool", C.rows[0:1, 0:2048], I["sgu_b_s%d" % l], writes=[C.rows])
    P.dma("pool", C.rows[32:33, 0:1024], I["sgu_b_in_v%d" % l], writes=[C.rows])
    for q in range(2):
        P.dma("pool", Win[:, :, q * 1024:(q + 1) * 1024],
              I["sgu_w_in%d" % l][:, q * 1024:(q + 1) * 1024].rearrange("(k p) f -> p k f", p=128), writes=WinR)
    P.dma("pool", Wout, I["sgu_w_out%d" % l].rearrange("(k p) f -> p k f", p=128), writes=WoutR)
    stg = C.big[:, 8192:8192 + 2048].bitcast(F32).rearrange("p (g t) -> p g t", g=8)
    for q in range(2):
        P.dma("sp", stg, I["sgu_w_sT%d" % l][:, q * 8:(q + 1) * 8, :], writes=[C.big])
        P.tt("dve", C.wsT[:, q * 8:(q + 1) * 8, :], stg,
             C.msk[:, 0:1, :].to_broadcast([128, 8, 128]), ALU.mult, [C.big, C.msk], [C.wsT])
    import os
    STG = int(os.environ.get("SGU_STAGE", "9"))
    if STG < 1:
        return
    norm_and_transpose_group(P, C, gB)
    if STG < 2:
        return
    uT = C.big[:, 0:8192].bitcast(F32).rearrange("p (c t) -> p c t", c=8)
    for c in range(8):
        bk = P.bank()
        for k in range(8):
            P.mm(bk[:, 0:G], Win[:, k, c * 128:(c + 1) * 128], C.hnT[:, k, :], k == 0, k == 7,
                 WinR + [C.hnT], [bk])
        P.act(uT[:, c, :], bk[:, 0:G], AF.Gelu, [bk, X.biu], [C.big], bias=X.biu[:, c:c + 1])
    if STG < 3:
        return
    for j in range(GT):
        st = C.stat[C.nstat % 2]
        C.nstat += 1
        vh = [C.fw[4 + hf] for hf in range(2)]
        for hf in range(2):
            bk = P.bank()
            for k in range(8):
                P.mm(bk[:, 0:512], C.hnT[:, k, j * 128:(j + 1) * 128],
                     Win[:, k, 1024 + hf * 512: 1024 + (hf + 1) * 512],
                     k == 0, False, WinR + [C.hnT], [bk])
            P.mm(bk[:, 0:512], C.ones[32:33, 0:128], C.rows[32:33, hf * 512:(hf + 1) * 512], False, True,
                 [C.ones, C.rows], [bk])
            P.act(vh[hf][:, 0:512], bk[:, 0:512], AF.Gelu, [bk], [vh[hf]])
            P.act(C.hn[0][:, 0:512], vh[hf][:, 0:512], AF.Square, [vh[hf]], [C.hn[0], st],
                  accum_out=st[:, 3 + hf:4 + hf])
        P.tt("dve", st[:, 0:1], st[:, 3:4], st[:, 4:5], ALU.add, [st], [st])
        rstd_from_ss(P, st, 1.0 / D, RMS_EPS)
        if STG < 4:
            continue
        vn = [C.bw[(j % 2) * 4 + q] for q in range(2)]
        yT = [C.bw[(j % 2) * 4 + 2 + q] for q in range(2)]
        for q in range(2):
            P.stt("dve", vn[q][:], vh[q][:, 0:512], st[:, 2:3], gv[:, q * 512:(q + 1) * 512], ALU.mult, ALU.mult,
                  [vh[q], st, gv], [vn[q]])
        for q in range(2):
            bk = P.bank()
            for c4 in range(4):
                for gg in range(2):
                    gi = 8 * q + 2 * c4 + gg
                    gl = 2 * c4 + gg
                    o = bk[gg * 64:(gg + 1) * 64, c4 * 128:(c4 + 1) * 128]
                    P.mm(o, vn[q][:, gl * 64:(gl + 1) * 64], C.wsT[:, gi, :], True, False, [vn[q], C.wsT], [bk])
                    P.mm(o, C.ones[0:1, 0:64], C.rows[0:1, gi * 128:(gi + 1) * 128], False, True,
                         [C.ones, C.rows], [bk])
            P.tt("dve", yT[q][:].rearrange("p (c t) -> p c t", c=4),
                 bk[:, 0:512].rearrange("p (c t) -> p c t", c=4),
                 uT[:, q * 4:(q + 1) * 4, j * 128:(j + 1) * 128], ALU.mult, [bk, C.big], [yT[q]])
        if STG < 5:
            continue
        for hf in range(2):
            bk = P.bank()
            for c in range(8):
                P.mm(bk[:, 0:512], yT[c // 4][:, (c % 4) * 128:(c % 4 + 1) * 128], Wout[:, c, hf * 512:(hf + 1) * 512],
                     c == 0, c == 7, [yT[c // 4]] + WoutR, [bk])
            P.tt("dve", C.h[j][:, hf * 512:(hf + 1) * 512], bk[:, 0:512], C.h[j][:, hf * 512:(hf + 1) * 512], ALU.add,
                 [bk, C.h[j]], [C.h[j]])


def rwkv_setup(P, C, layers):
    C.rw = {}
    if not any(l % 2 == 1 for l in layers):
        return
    C.lw1 = P.sb("lw1", [128, 8, 288], BF16)
    C.w2 = P.sb("w2", [64, D], BF16)
    C.a2 = P.sb("a2", [64, D], BF16)
    C.g2a = P.sb("g2a", [128, D], BF16)
    C.g2b = P.sb("g2b", [32, D], BF16)
    C.lo = P.sb("lo", [128, 512], BF16)
    C.zT = P.sb("zT", [128, 8, 128], BF16)
    C.sst = P.sb("sst", [128, 64], F32)
    for l in layers:
        if l % 2 != 1:
            continue
        X = Ctx()
        C.rw[l] = X
        X.STf = P.sb("STf%d" % l, [128, 8, 64], F32)
        X.STz = P.sb("STz%d" % l, [128, 16, 64], BF16)
        X.carry = P.sb("carry%d" % l, [128, 8, 1], BF16)
        X.mu = P.sb("mu%d" % l, [128, 6, 8], F32)
        P.memset("pool", X.STf[:], 0.0, [X.STf])
        P.memset("pool", X.STz[:], 0.0, [X.STz])
        P.memset("pool", X.carry[:], 0.0, [X.carry])
        P.dma("sp", X.mu[:], C.I["rwkv_mu%d" % l], writes=[X.mu])


def rwkv_mixer(P, C, l, g):
    I = C.I
    X = C.rw[l]
    ar = C.arena
    Wr = ar[:, 0:8192].rearrange("p (k f) -> p k f", k=8)
    Wk = ar[:, 8192:16384].rearrange("p (k f) -> p k f", k=8)
    Wv = ar[:, 16384:24576].rearrange("p (k f) -> p k f", k=8)
    Wo = ar[:, 24576:32768].rearrange("p (k f) -> p k f", k=8)
    WrR, WkR, WvR, WoR = [C.A[0]], [C.A[1]], [C.A[2]], [C.A[3], C.A[4]]
    gB, bkk, bka, brk, blw, blb = C.bc
    bcast_load(P, gB, I["gmix%d" % l])
    bcast_load(P, bkk, I["rwkv_k_k%d" % l])
    bcast_load(P, bka, I["rwkv_k_a%d" % l])
    bcast_load(P, brk, I["rwkv_r_k%d" % l])
    bcast_load(P, blw, I["rwkv_ln_w%d" % l])
    bcast_load(P, blb, I["rwkv_ln_b%d" % l])
    P.dma("pool", C.rows[0:1, 0:2048], I["rwkv_w0a0%d" % l], writes=[C.rows])
    rk = lambda a: a.rearrange("(k p) f -> p k f", p=128)
    P.dma("pool", C.lw1[:, :, 0:64], rk(I["rwkv_w1%d" % l]), writes=[C.lw1])
    P.dma("pool", C.lw1[:, :, 64:128], rk(I["rwkv_a1%d" % l]), writes=[C.lw1])
    P.dma("pool", C.lw1[:, :, 128:288], rk(I["rwkv_g1%d" % l]), writes=[C.lw1])
    P.dma("pool", C.w2[:], I["rwkv_w2%d" % l], writes=[C.w2])
    P.dma("pool", C.a2[:], I["rwkv_a2%d" % l], writes=[C.a2])
    P.dma("pool", C.g2a[:], I["rwkv_g2%d" % l][0:128, :], writes=[C.g2a])
    P.dma("pool", C.g2b[:], I["rwkv_g2%d" % l][128:160, :], writes=[C.g2b])
    P.dma("pool", Wr, rk(I["rwkv_w_r%d" % l]), writes=WrR)
    P.dma("pool", Wk, rk(I["rwkv_w_k%d" % l]), writes=WkR)
    P.dma("pool", Wv, rk(I["rwkv_w_v%d" % l]), writes=WvR)
    P.dma("pool", Wo, rk(I["rwkv_w_o%d" % l]), writes=WoR)

    hx, xsR = Res("hx"), [Res("xs%d" % i) for i in range(4)]
    split_res(C.hnT.res, [hx] + xsR)
    hT = C.hnTfull
    big = C.big
    RA, KB = Res("RA"), Res("KB")
    Q = []
    for q in range(2):
        qq = Ctx()
        base = 2048 + q * 4608
        qq.AKBr = Res("AKB%d" % q)
        qq.XpR = [Res("Xp%d_%d" % (q, i)) for i in range(2)]
        qq.LpR = [Res("Lp%d_%d" % (q, i)) for i in range(2)]
        qq.WcR = [Res("Wc%d_%d" % (q, i)) for i in range(2)]
        qq.AKB4 = big[:, base:base + 2048].rearrange("p (h q t) -> p h q t", h=4, q=4)
        qq.Xp4 = big[:, base + 2048:base + 3072].rearrange("p (s h t) -> p s h t", s=2, h=4)
        qq.Lp4 = big[:, base + 3072:base + 4096].rearrange("p (s h t) -> p s h t", s=2, h=4)
        qq.Wc4 = big[:, base + 4096:base + 4608].rearrange("p (s h m) -> p s h m", s=2, h=4)
        qq.all = [qq.AKBr] + qq.XpR + qq.LpR + qq.WcR
        Q.append(qq)
    bigsub = [RA, KB] + Q[0].all + Q[1].all
    split_res(C.big.res, bigsub)
    RAt4 = big[:, 0:1024].rearrange("p (a q t) -> p a q t", a=4, q=2)
    KBt4 = big[:, 1024:2048].rearrange("p (a q t) -> p a q t", a=4, q=2)
    xs = [hT[:, :, 130 + s * 128: 258 + s * 128] for s in range(4)]
    sst = C.sst
    lo = C.lo
    v3 = lambda t: t[:, 0:512].rearrange("p (h n) -> p h n", h=8)
    bc3 = lambda ap: ap.unsqueeze(2).to_broadcast([128, 8, 64])
    EW = "dve"

    for j in range(GT):
        hn = C.hn[j % 2]
        jk = C.hn[1 - j % 2]
        rmsnorm_to_bf(P, C, C.h[j], gB, hn, jk)
        P.cp("act", hT[:, :, 1:2], X.carry[:], [X.carry], [hx])
        transpose8(P, C, [hn[:, k * 128:(k + 1) * 128] for k in range(8)], hT[:, :, 2:130], [hn], [hx])
        P.cp("act", X.carry[:], hT[:, :, 129:130], [hx], [X.carry])
        xx = jk[:].rearrange("p (k t) -> p k t", k=8)
        P.tt("dve", xx, hT[:, :, 1:129], hT[:, :, 2:130], ALU.subtract, [hx], [jk])

        def gen_x(i, slot):
            P.tt(EW, xs[slot], xx, X.mu[:, i, :].unsqueeze(2).to_broadcast([128, 8, 128]), ALU.mult,
                 [jk, X.mu], [xsR[slot]])
            P.tt(EW, xs[slot], xs[slot], hT[:, :, 2:130], ALU.add, [hx, xsR[slot]], [xsR[slot]])

        bk = P.bank()
        gen_x(1, 0)
        for k in range(8):
            P.mm(bk[0:64, 0:128], C.lw1[:, k, 0:64], xs[0][:, k, :], k == 0, k == 7, [C.lw1, xsR[0]], [bk])
        gen_x(4, 1)
        for k in range(8):
            P.mm(bk[0:64, 128:256], C.lw1[:, k, 64:128], xs[1][:, k, :], k == 0, k == 7, [C.lw1, xsR[1]], [bk])
        gen_x(5, 2)
        for k in range(8):
            P.mm(bk[:, 256:384], C.lw1[:, k, 128:256], xs[2][:, k, :], k == 0, k == 7, [C.lw1, xsR[2]], [bk])
        for k in range(8):
            P.mm(bk[0:32, 384:512], C.lw1[:, k, 256:288], xs[2][:, k, :], k == 0, k == 7, [C.lw1, xsR[2]], [bk])
        P.act(lo[0:64, 0:128], bk[0:64, 0:128], AF.Tanh, [bk], [lo])
        P.cp("act", lo[0:64, 128:256], bk[0:64, 128:256], [bk], [lo])
        P.act(lo[:, 256:384], bk[:, 256:384], AF.Sigmoid, [bk], [lo])
        P.act(lo[0:32, 384:512], bk[0:32, 384:512], AF.Sigmoid, [bk], [lo])
        gen_x(0, 3)
        gen_x(2, 0)
        gen_x(3, 1)

        for hf in range(2):
            f0 = hf * 512
            r_t, k_t, v_t, a_t, sg_t, tA, tB, tE = C.fw
            rb, ab, kb, bb, khb, bhb, vb, zb = C.bw

            def proj(slot, W, WR, dst):
                bk = P.bank()
                for k in range(8):
                    P.mm(bk[:, 0:512], xs[slot][:, k, :], W[:, k, f0:f0 + 512], k == 0, k == 7,
                         [xsR[slot]] + WR, [bk])
                P.cp("act", dst[:, 0:512], bk[:, 0:512], [bk], [dst])

            proj(3, Wr, WrR, r_t)
            proj(0, Wk, WkR, k_t)
            proj(1, Wv, WvR, v_t)
            bk = P.bank()
            P.mm(bk[:, 0:512], lo[0:64, 0:128], C.w2[0:64, f0:f0 + 512], True, False, [lo, C.w2], [bk])
            P.mm(bk[:, 0:512], C.ones[0:1, 0:128], C.rows[0:1, f0:f0 + 512], False, True, [C.ones, C.rows], [bk])
            P.act(sg_t[:, 0:512], bk[:, 0:512], AF.Sigmoid, [bk], [sg_t])
            bk = P.bank()
            P.mm(bk[:, 0:512], lo[0:64, 128:256], C.a2[0:64, f0:f0 + 512], True, False, [lo, C.a2], [bk])
            P.mm(bk[:, 0:512], C.ones[0:1, 0:128], C.rows[0:1, 1024 + f0:1024 + f0 + 512], False, True,
                 [C.ones, C.rows], [bk])
            P.act(a_t[:, 0:512], bk[:, 0:512], AF.Sigmoid, [bk], [a_t])
            bI, bE, bR = P.bank(), P.bank(), P.bank()
            for bq, ti in ((bI, 0), (bE, 1), (bR, 2)):
                P.mm(bq[:, 0:512], C.trin[:, ti, :], sg_t[:, 0:512], True, True, [C.trin, sg_t], [bq])
            bG = P.bank()
            for pp in range(4):
                P.mm(bG[:, 2 * pp:2 * pp + 2], sg_t[:, pp * 128:(pp + 1) * 128], C.onesn[:, 0:2], True, True,
                     [sg_t, C.onesn], [bG])
            P.tt(EW, tA[:, 0:512], k_t[:, 0:512], bkk[:, f0:f0 + 512], ALU.mult, [k_t, bkk], [tA])
            P.tt(EW, tB[:, 0:512], tA[:, 0:512], tA[:, 0:512], ALU.mult, [tA], [tB])
            P.op("dve", lambda e: e.tensor_reduce(out=sst[:, 0:8], in_=v3(tB), axis=AX.X, op=ALU.add), [tB], [sst])
            P.ts("dve", sst[:, 0:8], sst[:, 0:8], 1e-24, None, ALU.max, None, [sst], [sst])
            P.act(sst[:, 8:16], sst[:, 0:8], AF.Sqrt, [sst], [sst])
            P.op("dve", lambda e: e.reciprocal(out=sst[:, 16:24], in_=sst[:, 8:16]), [sst], [sst])
            P.tt("dve", v3(tA), v3(tA), bc3(sst[:, 16:24]), ALU.mult, [tA, sst], [tA])
            P.stt("dve", tB[:, 0:512], a_t[:, 0:512], -1.0, bka[:, f0:f0 + 512], ALU.add, ALU.mult,
                  [a_t, bka], [tB])
            P.stt("dve", k_t[:, 0:512], tB[:, 0:512], 1.0, k_t[:, 0:512], ALU.add, ALU.mult, [tB, k_t], [k_t])
            P.tt(EW, a_t[:, 0:512], tA[:, 0:512], a_t[:, 0:512], ALU.mult, [tA, a_t], [a_t])
            P.tt(EW, tB[:, 0:512], r_t[:, 0:512], k_t[:, 0:512], ALU.mult, [r_t, k_t], [tB])
            P.tt(EW, tB[:, 0:512], tB[:, 0:512], brk[:, f0:f0 + 512], ALU.mult, [tB, brk], [tB])
            P.op("dve", lambda e: e.tensor_reduce(out=sst[:, 24:32], in_=v3(tB), axis=AX.X, op=ALU.add), [tB], [sst])
            P.cp("act", vb[:], v_t[:, 0:512], [v_t], [vb])
            P.act(tB[:, 0:512], bI[:, 0:512], AF.Exp, [bI], [tB])
            P.tt("dve", rb[:], r_t[:, 0:512], tB[:, 0:512], ALU.mult, [r_t, tB], [rb])
            P.act(tE[:, 0:512], bE[:, 0:512], AF.Exp, [bE], [tE])
            P.stt("dve", ab[:], tA[:, 0:512], -1.0, tE[:, 0:512], ALU.mult, ALU.mult, [tA, tE], [ab])
            P.act(tB[:, 0:512], bI[:, 0:512], AF.Exp, [bI], [tB], scale=-1.0)
            P.tt("dve", kb[:], k_t[:, 0:512], tB[:, 0:512], ALU.mult, [k_t, tB], [kb])
            P.tt(EW, bb[:], a_t[:, 0:512], tB[:, 0:512], ALU.mult, [a_t, tB], [bb])
            P.act(tE[:, 0:512], bR[:, 0:512], AF.Exp, [bR], [tE])
            P.tt("dve", khb[:], k_t[:, 0:512], tE[:, 0:512], ALU.mult, [k_t, tE], [khb])
            P.tt(EW, bhb[:], a_t[:, 0:512], tE[:, 0:512], ALU.mult, [a_t, tE], [bhb])
            P.act(sst[:, 32:40], bG[:, 0:8], AF.Exp, [bG], [sst])
            srcs = []
            for pp in range(4):
                srcs += [rb[:, pp * 128:(pp + 1) * 128], ab[:, pp * 128:(pp + 1) * 128]]
            transpose8(P, C, srcs, big[:, 0:1024].rearrange("p (k t) -> p k t", k=8), [rb, ab], [RA], "act")
            srcs = []
            for pp in range(4):
                srcs += [kb[:, pp * 128:(pp + 1) * 128], bb[:, pp * 128:(pp + 1) * 128]]
            transpose8(P, C, srcs, big[:, 1024:2048].rearrange("p (k t) -> p k t", k=8), [kb, bb], [KB], "dve")
            y_t = r_t
            HD = [[(hd, qd * 2 + hd // 2, hd % 2, qd * 4 + hd) for hd in range(4)] for qd in range(2)]
            for qd in range(2):
                qq, heads = Q[qd], HD[qd]
                for (qsel, dst_q) in ((0, 0), (1, 2)):
                    bnk = [P.bank(), P.bank()]
                    for (hd, pp, hh, hl) in heads:
                        col = (hd // 2) * 256
                        P.mm(bnk[hh][:, col:col + 256], KBt4[hh * 64:(hh + 1) * 64, pp, qsel, :],
                             big[hh * 64:(hh + 1) * 64, pp * 256:(pp + 1) * 256], True, True, [KB, RA], [bnk[hh]])
                    for hh in range(2):
                        P.tt("dve", qq.AKB4[:, hh::2, dst_q:dst_q + 2, :],
                             bnk[hh][:, 0:512].rearrange("p (h q t) -> p h q t", h=2, q=2),
                             C.msk[:, 0:2, :].unsqueeze(1).to_broadcast([128, 2, 2, 128]), ALU.mult,
                             [bnk[hh], C.msk], [qq.AKBr])
                bnk = [P.bank(), P.bank()]
                for (hd, pp, hh, hl) in heads:
                    col = (hd // 2) * 128
                    P.mm(bnk[hh][:, col:col + 128], RAt4[hh * 64:(hh + 1) * 64, pp, 1, :],
                         KBt4[hh * 64:(hh + 1) * 64, pp, 1, :], True, True, [KB, RA], [bnk[hh]])
                for hh in range(2):
                    P.tt("dve", qq.Lp4[:, 0, hh::2, :], bnk[hh][:, 0:256].rearrange("p (h t) -> p h t", h=2),
                         C.msk[:, 2:3, :].to_broadcast([128, 2, 128]), ALU.mult, [bnk[hh], C.msk], [qq.LpR[0]])
            for qd in range(2):
                qq, heads = Q[qd], HD[qd]
                bW = P.bank()
                for (hd, pp, hh, hl) in heads:
                    gh = hf * 8 + hl
                    o = bW[:, hd * 64:(hd + 1) * 64]
                    P.mm(o, RAt4[:, pp, 1, :], X.STz[:, gh, :], True, False, [RA, X.STz], [bW])
                    P.mm(o, qq.AKB4[:, hd, 1, :], vb[:, hl * 64:(hl + 1) * 64], False, True, [qq.AKBr, vb], [bW])
                P.cp("act", qq.Wc4[:, 0, :, :], bW[:, 0:256].rearrange("p (h m) -> p h m", h=4), [bW], [qq.WcR[0]])
            for kl in range(7):
                for qd in range(2):
                    qq, heads = Q[qd], HD[qd]

                    def Xk(hd):
                        return qq.AKB4[:, hd, 3, :] if kl == 0 else qq.Xp4[:, (kl - 1) % 2, hd, :]
                    XkR = qq.AKBr if kl == 0 else qq.XpR[(kl - 1) % 2]
                    LkR = qq.LpR[kl % 2]
                    if kl < 6:
                        bX = P.bank()
                        for (hd, pp, hh, hl) in heads:
                            P.mm(bX[:, hd * 128:(hd + 1) * 128], qq.Lp4[:, kl % 2, hd, :], Xk(hd), True, True,
                                 [LkR, XkR], [bX])
                    if kl < 5:
                        bL = P.bank()
                        for (hd, pp, hh, hl) in heads:
                            P.mm(bL[:, hd * 128:(hd + 1) * 128], Xk(hd), qq.Lp4[:, kl % 2, hd, :], True, True,
                                 [LkR, XkR], [bL])
                    bU = P.bank()
                    for (hd, pp, hh, hl) in heads:
                        P.mm(bU[:, hd * 64:(hd + 1) * 64], Xk(hd), qq.Wc4[:, kl % 2, hd, :], True, True,
                             [XkR, qq.WcR[kl % 2]], [bU])
                    if kl < 6:
                        P.cp("act", qq.Xp4[:, kl % 2, :, :], bX[:, 0:512].rearrange("p (h t) -> p h t", h=4),
                             [bX], [qq.XpR[kl % 2]])
                    if kl < 5:
                        P.cp("act" if qd == 0 else "dve", qq.Lp4[:, (kl + 1) % 2, :, :],
                             bL[:, 0:512].rearrange("p (h t) -> p h t", h=4), [bL], [qq.LpR[(kl + 1) % 2]])
                    P.tt("dve", qq.Wc4[:, (kl + 1) % 2, :, :], bU[:, 0:256].rearrange("p (h m) -> p h m", h=4),
                         qq.Wc4[:, kl % 2, :, :], ALU.add, [bU, qq.WcR[kl % 2]], [qq.WcR[(kl + 1) % 2]])
            for qd in range(2):
                qq, heads = Q[qd], HD[qd]
                U = lambda hd: qq.Wc4[:, 1, hd, :]
                bY = P.bank()
                for (hd, pp, hh, hl) in heads:
                    gh = hf * 8 + hl
                    o = bY[:, hd * 64:(hd + 1) * 64]
                    P.mm(o, RAt4[:, pp, 0, :], X.STz[:, gh, :], True, False, [RA, X.STz], [bY])
                    P.mm(o, qq.AKB4[:, hd, 0, :], vb[:, hl * 64:(hl + 1) * 64], False, False, [qq.AKBr, vb], [bY])
                    P.mm(o, qq.AKB4[:, hd, 2, :], U(hd), False, True, [qq.AKBr, qq.WcR[1]], [bY])
                P.cp("act", y_t[:, qd * 256:(qd + 1) * 256], bY[:, 0:256], [bY], [y_t])
                bS = P.bank()
                for (hd, pp, hh, hl) in heads:
                    o = bS[hh * 64:(hh + 1) * 64, (hd // 2) * 64:(hd // 2 + 1) * 64]
                    P.mm(o, khb[:, hl * 64:(hl + 1) * 64], vb[:, hl * 64:(hl + 1) * 64], True, False, [khb, vb], [bS])
                    P.mm(o, bhb[:, hl * 64:(hl + 1) * 64], U(hd), False, True, [bhb, qq.WcR[1]], [bS])
                for pr in range(2):
                    ppl = qd * 2 + pr
                    gp = hf * 4 + ppl
                    P.stt("dve", X.STf[:, gp, :], X.STf[:, gp, :], sst[:, 32 + 2 * ppl:33 + 2 * ppl],
                          bS[:, pr * 64:(pr + 1) * 64], ALU.mult, ALU.add, [X.STf, sst, bS], [X.STf])
                    P.cp("act", X.STz[0:64, 2 * gp, :], X.STf[0:64, gp, :], [X.STf], [X.STz])
                    P.cp("act", X.STz[64:128, 2 * gp + 1, :], X.STf[64:128, gp, :], [X.STf], [X.STz])
            P.op("dve", lambda e: e.tensor_reduce(out=sst[:, 40:48], in_=v3(y_t), axis=AX.X, op=ALU.add), [y_t], [sst])
            P.act(tB[:, 0:512], y_t[:, 0:512], AF.Square, [y_t], [tB])
            P.op("dve", lambda e: e.tensor_reduce(out=sst[:, 48:56], in_=v3(tB), axis=AX.X, op=ALU.add), [tB], [sst])
            P.ts("dve", sst[:, 40:48], sst[:, 40:48], 1.0 / 64, None, ALU.mult, None, [sst], [sst])
            P.tt("dve", sst[:, 56:64], sst[:, 40:48], sst[:, 40:48], ALU.mult, [sst], [sst])
            P.stt("dve", sst[:, 48:56], sst[:, 48:56], 1.0 / 64, sst[:, 56:64], ALU.mult, ALU.subtract, [sst], [sst])
            P.act(sst[:, 56:64], sst[:, 48:56], AF.Sqrt, [sst], [sst], bias=GN_EPS)
            P.op("dve", lambda e: e.reciprocal(out=sst[:, 48:56], in_=sst[:, 56:64]), [sst], [sst])
            P.tt("dve", v3(y_t), v3(y_t), bc3(sst[:, 40:48]), ALU.subtract, [y_t, sst], [y_t])
            P.tt("dve", v3(y_t), v3(y_t), bc3(sst[:, 48:56]), ALU.mult, [y_t, sst], [y_t])
            P.tt(EW, y_t[:, 0:512], y_t[:, 0:512], blw[:, f0:f0 + 512], ALU.mult, [y_t, blw], [y_t])
            P.tt(EW, y_t[:, 0:512], y_t[:, 0:512], blb[:, f0:f0 + 512], ALU.add, [y_t, blb], [y_t])
            P.tt("dve", v3(tB), v3(v_t), bc3(sst[:, 24:32]), ALU.mult, [v_t, sst], [tB])
            P.tt(EW, y_t[:, 0:512], y_t[:, 0:512], tB[:, 0:512], ALU.add, [y_t, tB], [y_t])
            bk = P.bank()
            P.mm(bk[:, 0:512], lo[:, 256:384], C.g2a[:, f0:f0 + 512], True, False, [lo, C.g2a], [bk])
            P.mm(bk[:, 0:512], lo[0:32, 384:512], C.g2b[0:32, f0:f0 + 512], False, True, [lo, C.g2b], [bk])
            P.tt("dve", zb[:], y_t[:, 0:512], bk[:, 0:512], ALU.mult, [y_t, bk], [zb])
            transpose8(P, C, [zb[:, pp * 128:(pp + 1) * 128] for pp in range(4)],
                       C.zT[:, hf * 4:(hf + 1) * 4, :], [zb], [C.zT], "act")
        for hfo in range(2):
            bk = P.bank()
            for c in range(8):
                P.mm(bk[:, 0:512], C.zT[:, c, :], Wo[:, c, hfo * 512:(hfo + 1) * 512], c == 0, c == 7,
                     [C.zT] + WoR, [bk])
            P.tt("dve", C.h[j][:, hfo * 512:(hfo + 1) * 512], bk[:, 0:512], C.h[j][:, hfo * 512:(hfo + 1) * 512],
                 ALU.add, [bk, C.h[j]], [C.h[j]])
    merge_res(C.hnT.res, [hx] + xsR)
    merge_res(C.big.res, bigsub)


def prep_layer_inputs(inp, layers, last):
    c = np.ascontiguousarray
    W = {}
    pp_, ff_ = np.arange(128)[:, None], np.arange(128)[None, :]
    W["cmask"] = c(np.stack([(pp_ <= ff_), (pp_ < ff_), (pp_ > ff_)], axis=1).astype(np.float32))
    for l in layers:
        j = l // 2
        W["gmix%d" % l] = c(inp["norm_mix_g"][l].reshape(1, D))
        W["gffn%d" % l] = c(inp["norm_ffn_g"][l].reshape(1, D))
        if l % 2 == 0:
            W["sgu_w_in%d" % l] = c(inp["sgu_w_in"][j])
            b = inp["sgu_b_in"][j]
            W["sgu_b_in_u%d" % l] = c(b[:D].reshape(8, 128).T)
            W["sgu_b_in_v%d" % l] = c(b[D:].reshape(1, D))
            W["sgu_g_v%d" % l] = c(inp["sgu_g_v"][j].reshape(1, D))
            W["sgu_w_sT%d" % l] = c(np.transpose(inp["sgu_w_s"][j], (2, 0, 1)))
            W["sgu_b_s%d" % l] = c(inp["sgu_b_s"][j].reshape(1, 2048))
            W["sgu_w_out%d" % l] = c(inp["sgu_w_out"][j])
        else:
            W["rwkv_mu%d" % l] = c(np.transpose(inp["rwkv_mu"][j].reshape(6, 8, 128), (2, 0, 1)))
            for nm in ("w_r", "w_k", "w_v", "w_o", "w1", "a1", "g1", "w2", "a2", "g2"):
                W["rwkv_%s%d" % (nm, l)] = c(inp["rwkv_" + nm][j])
            W["rwkv_w0a0%d" % l] = c(np.concatenate([inp["rwkv_w0"][j], inp["rwkv_a0"][j]]).reshape(1, 2048))
            for nm in ("k_k", "k_a", "r_k", "ln_w", "ln_b"):
                W["rwkv_%s%d" % (nm, l)] = c(inp["rwkv_" + nm][j].reshape(1, D))
        W["ffn_w_up%d" % l] = c(inp["ffn_w_up"][l])
        W["ffn_cw%d" % l] = c(np.transpose(inp["ffn_conv_w"][l].reshape(3, 44, 128), (2, 0, 1)))
        W["ffn_cb%d" % l] = c(inp["ffn_conv_b"][l].reshape(44, 128).T)
        W["ffn_w_down%d" % l] = c(inp["ffn_w_down"][l])
    if last:
        W["gfinal"] = c(inp["norm_final_g"].reshape(1, D))
    return W


_NC_CACHE = {}


def run_launch(h, inp, layers, last, ncores=8, **bk):
    key = (tuple(layers), last, tuple(sorted(bk.items())))
    if key not in _NC_CACHE:
        _NC_CACHE[key] = build_program(layers, last=last, **bk)
    nc = _NC_CACHE[key]
    W = prep_layer_inputs(inp, layers, last)
    in_maps = []
    for b in range(ncores):
        m = dict(W)
        m["hin"] = np.ascontiguousarray(h[b])
        in_maps.append(m)
    res = run_bass_kernel_spmd(nc, in_maps, core_ids=list(range(ncores)))
    return np.stack([np.asarray(r["hout"]) for r in res.results], axis=0)


FUSED = True


def kernel(**inputs):
    inp = {k: np.asarray(v) for k, v in inputs.items()}
    h = np.ascontiguousarray(inp["x"], dtype=np.float32)
    if FUSED:
        return run_launch(h, inp, [0, 1, 2, 3], True).astype(np.float32)
    for l in range(DEPTH):
        h = run_launch(h, inp, [l], l == DEPTH - 1)
    return h.astype(np.float32)
```

```python
import numpy as np
from contextlib import ExitStack
import concourse.bass as bass
import concourse.mybir as mybir
from concourse.bass_utils import run_bass_kernel_spmd

F32 = mybir.dt.float32
BF16 = mybir.dt.bfloat16
AF = mybir.ActivationFunctionType
ALU = mybir.AluOpType
AX = mybir.AxisListType

D = 1024
S = 4096
DEPTH = 4
FF = 2816
FC = 22
G = 512
GT = G // 128
NG = S // G
NH = 16
RMS_EPS = 1e-6
GN_EPS = 64e-5


class Res:
    __slots__ = ("name", "w", "r", "dsem")

    def __init__(self, name):
        self.name = name
        self.w = None
        self.r = []
        self.dsem = None


class T:
    def __init__(self, h, res):
        self.h = h
        self.res = res

    def __getitem__(self, k):
        return self.h[k]


def _res(t):
    return t.res if isinstance(t, T) else t


class Prog:
    ENG = ("pe", "dve", "act", "pool", "sp")

    def __init__(self, nc):
        self.nc = nc
        self.es = ExitStack()
        self.streams = {e: [] for e in self.ENG}
        self.count = {e: 0 for e in self.ENG}
        self.semvals = {}
        self.seen = {e: {} for e in self.ENG}
        self.ndsem = 0
        self.banks = []
        self.bank_i = 0

    def sb(self, name, shape, dtype):
        t = self.es.enter_context(self.nc.sbuf_tensor(name, list(shape), dtype))
        return T(t, Res(name))

    def ps(self, name, shape, dtype=F32):
        t = self.es.enter_context(self.nc.psum_tensor(name, list(shape), dtype))
        return T(t, Res(name))

    def bank(self):
        b = self.banks[self.bank_i % len(self.banks)]
        self.bank_i += 1
        return b

    def _need(self, eng, ev, waits):
        if ev is None:
            return
        key, val = ev
        if key == "pe" and eng == "pe":
            return
        if self.seen[eng].get(key, 0) >= val:
            return
        self.seen[eng][key] = val
        for i, (k, v) in enumerate(waits):
            if k == key:
                waits[i] = (k, max(v, val))
                return
        waits.append((key, val))

    def _deps(self, eng, reads, writes):
        waits = []
        for t in reads:
            self._need(eng, _res(t).w, waits)
        for t in writes:
            r = _res(t)
            self._need(eng, r.w, waits)
            for ev in r.r:
                self._need(eng, ev, waits)
        return waits

    def _commit(self, ev, reads, writes):
        for t in reads:
            r = _res(t)
            r.r.append(ev)
            if len(r.r) > 48:
                best = {}
                for k, v in r.r:
                    best[k] = max(best.get(k, 0), v)
                r.r = list(best.items())
        for t in writes:
            r = _res(t)
            r.w = ev
            r.r = []

    def op(self, eng, fn, reads=(), writes=()):
        waits = self._deps(eng, reads, writes)
        self.count[eng] += 1
        ev = (eng, self.count[eng])
        self.streams[eng].append((waits, fn, (eng, 1)))
        self._commit(ev, reads, writes)
        return ev

    def dma(self, q, out_ap, in_ap, reads=(), writes=(), sem_res=None, **kw):
        sr = sem_res if sem_res is not None else (writes[0] if writes else reads[0])
        sr = _res(sr)
        if sr.dsem is None:
            sr.dsem = "d%d" % self.ndsem
            self.ndsem += 1
            self.semvals[sr.dsem] = 0
        key = sr.dsem
        waits = self._deps(q, reads, writes)
        self.semvals[key] += 16
        ev = (key, self.semvals[key])

        def fn(e, out_ap=out_ap, in_ap=in_ap, kw=kw):
            return e.dma_start(out=out_ap, in_=in_ap, **kw)
        self.streams[q].append((waits, fn, (key, 16)))
        self._commit(ev, reads, writes)
        return ev

    def wait_all(self, eng, events):
        waits = []
        for ev in events:
            self._need(eng, ev, waits)
        self.streams[eng].append((waits, None, None))

    def mm(self, out, lhsT, rhs, start, stop, R, W):
        return self.op("pe", lambda e: e.matmul(out, lhsT=lhsT, rhs=rhs, start=start, stop=stop), R, W)

    def tr(self, out, in_, ident, R, W):
        return self.op("pe", lambda e: e.transpose(out, in_, ident[:]), list(R) + [ident], W)

    def act(self, out, in_, func, R, W, **kw):
        return self.op("act", lambda e: e.activation(out=out, in_=in_, func=func, **kw), R, W)

    def tt(self, eng, out, in0, in1, op, R, W):
        return self.op(eng, lambda e: e.tensor_tensor(out=out, in0=in0, in1=in1, op=op), R, W)

    def stt(self, eng, out, in0, scalar, in1, op0, op1, R, W, **kw):
        return self.op(eng, lambda e: e.scalar_tensor_tensor(out=out, in0=in0, scalar=scalar, in1=in1,
                                                             op0=op0, op1=op1, **kw), R, W)

    def ts(self, eng, out, in0, s1, s2, op0, op1, R, W):
        if s2 is None:
            return self.op(eng, lambda e: e.tensor_scalar(out=out, in0=in0, scalar1=s1, scalar2=None, op0=op0), R, W)
        return self.op(eng, lambda e: e.tensor_scalar(out=out, in0=in0, scalar1=s1, scalar2=s2, op0=op0, op1=op1), R, W)

    def cp(self, eng, out, in_, R, W):
        if eng == "act":
            return self.op("act", lambda e: e.activation(out=out, in_=in_, func=AF.Copy), R, W)
        return self.op(eng, lambda e: e.tensor_copy(out=out, in_=in_), R, W)

    def memset(self, eng, out, val, W):
        return self.op(eng, lambda e: e.memset(out, val), (), W)

    def emit(self):
        nc = self.nc
        keys = list(self.ENG) + list(self.semvals.keys())
        sems = {}
        for k in keys:
            sems[k] = self.es.enter_context(nc.semaphore("s_" + k))
        streams = self.streams
        with nc.Block() as block:
            def mk(ename):
                def body(e):
                    for waits, fn, inc in streams[ename]:
                        for (k, v) in waits:
                            e.wait_ge(sems[k], v)
                        if fn is not None:
                            ins = fn(e)
                            ins.then_inc(sems[inc[0]], inc[1])
                return body
            block.tensor(mk("pe"))
            block.vector(mk("dve"))
            block.scalar(mk("act"))
            block.gpsimd(mk("pool"))
            block.sync(mk("sp"))
        self.es.close()


NEGC = -float(np.exp(-0.5))


class Ctx:
    pass


def split_res(parent, children):
    for c in children:
        c.w = parent.w
        c.r = list(parent.r)


def merge_res(parent, children):
    evs = list(parent.r)
    for c in children:
        evs.extend(c.r)
        if c.w is not None:
            evs.append(c.w)
    best = {}
    for k, v in evs:
        best[k] = max(best.get(k, 0), v)
    parent.r = list(best.items())


def split_multi(parents, children):
    evs = []
    for p_ in parents:
        evs.extend(p_.r)
        if p_.w is not None:
            evs.append(p_.w)
    for c in children:
        c.w = None
        c.r = list(evs)


def merge_multi(parents, children):
    evs = []
    for c in children:
        evs.extend(c.r)
        if c.w is not None:
            evs.append(c.w)
    for p_ in parents:
        best = {}
        for k, v in list(p_.r) + evs:
            best[k] = max(best.get(k, 0), v)
        p_.r = list(best.items())


def scr_get(P, C, key, shape, conv):
    if key not in C.scr:
        t = P.nc.dram_tensor("scr_" + key, list(shape), BF16, kind="Internal").ap()
        grp = C.cur_grp
        if grp not in C.scrR:
            C.scrR[grp] = Res("scrg_" + grp)
        rs = C.scrR[grp]
        for fn, src in conv:
            hist = C.conv_hist.setdefault(grp, [])
            if len(hist) >= 5:
                P.wait_all("pool", [hist[-5]])
            hist.append(P.dma("pool", fn(t), src, writes=[rs]))
        C.scr[key] = (t, rs)
    return C.scr[key]


def wload(P, C, key, shape, conv, dst_ap, WR, mode, split=1):
    t, rs = scr_get(P, C, key, shape, conv)
    if mode == "load":
        if split == 1:
            P.dma("sp", dst_ap, t, reads=[rs], writes=WR)
        else:
            n = shape[1] // split
            for q in range(split):
                P.dma("sp", dst_ap[:, q * n:(q + 1) * n], t[:, q * n:(q + 1) * n], reads=[rs], writes=WR)


def rk(a):
    return a.rearrange("(k p) f -> p k f", p=128)


def ffn_weights(P, C, l, mode, pieces=None):
    I = C.I
    Wup = I["ffn_w_up%d" % l]
    Wdn_d = I["ffn_w_down%d" % l]
    ring = [C.arena[:, 24576 + s_ * 4096: 24576 + (s_ + 1) * 4096].rearrange("p (k h f) -> p k h f", k=8, h=2)
            for s_ in range(2)]
    ringR = [C.A[3], C.A[4]]
    Wdn = C.arena[:, 0:FC * 1024].rearrange("p (c f) -> p c f", c=FC)
    WdnR = [C.A[0], C.A[1], C.A[2]]
    if pieces is None:
        pieces = list(range(11)) + ["dn"]
    for i in pieces:
        if i == "dn":
            conv = [(lambda t, part=part: t[:, part * 11:(part + 1) * 11, :],
                     Wdn_d[part * 11 * 128:(part + 1) * 11 * 128, :].rearrange("(c p) f -> p c f", p=128))
                    for part in range(2)]
            wload(P, C, "dn%d" % l, [128, FC, 1024], conv, Wdn, WdnR, mode, split=2)
        else:
            conv = [(lambda t, hh=hh: t[:, :, hh, :], rk(Wup[:, hh * FF + i * 256: hh * FF + i * 256 + 256]))
                    for hh in range(2)]
            wload(P, C, "up%d_%d" % (l, i), [128, 8, 2, 256], conv, ring[i % 2], [ringR[i % 2]], mode)


def sgu_weights(P, C, l, mode):
    I = C.I
    Win = C.arena[:, 0:16384].rearrange("p (k f) -> p k f", k=8)
    Wout = C.arena[:, 16384:24576].rearrange("p (k f) -> p k f", k=8)
    conv = [(lambda t, q=q: t[:, :, q * 1024:(q + 1) * 1024], rk(I["sgu_w_in%d" % l][:, q * 1024:(q + 1) * 1024]))
            for q in range(2)]
    wload(P, C, "win%d" % l, [128, 8, 2048], conv, Win, [C.A[0], C.A[1]], mode, split=2)
    wload(P, C, "wout%d" % l, [128, 8, 1024], [(lambda t: t, rk(I["sgu_w_out%d" % l]))], Wout, [C.A[2]], mode)


def rwkv_weights(P, C, l, mode):
    I = C.I
    ar = C.arena
    v8 = lambda a: a.rearrange("p (k f) -> p k f", k=8)
    conv = [(lambda t: t[:, :, 0:64], rk(I["rwkv_w1%d" % l])), (lambda t: t[:, :, 64:128], rk(I["rwkv_a1%d" % l])),
            (lambda t: t[:, :, 128:288], rk(I["rwkv_g1%d" % l]))]
    wload(P, C, "lw1%d" % l, [128, 8, 288], conv, C.lw1[:], [C.lw1], mode)
    wload(P, C, "w2%d" % l, [64, D], [(lambda t: t, I["rwkv_w2%d" % l])], C.w2[:], [C.w2], mode)
    wload(P, C, "a2%d" % l, [64, D], [(lambda t: t, I["rwkv_a2%d" % l])], C.a2[:], [C.a2], mode)
    wload(P, C, "g2a%d" % l, [128, D], [(lambda t: t, I["rwkv_g2%d" % l][0:128, :])], C.g2a[:], [C.g2a], mode)
    wload(P, C, "g2b%d" % l, [32, D], [(lambda t: t, I["rwkv_g2%d" % l][128:160, :])], C.g2b[:], [C.g2b], mode)
    for nm, a0, WR in (("w_r", 0, [C.A[0]]), ("w_k", 8192, [C.A[1]]), ("w_v", 16384, [C.A[2]]),
                       ("w_o", 24576, [C.A[3], C.A[4]])):
        wload(P, C, "%s%d" % (nm, l), [128, 8, 1024], [(lambda t: t, rk(I["rwkv_%s%d" % (nm, l)]))],
              v8(ar[:, a0:a0 + 8192]), WR, mode)


def sub_weights(P, C, kind, l, mode):
    C.cur_grp = "%s%d" % (kind, l)
    if kind == "sgu":
        sgu_weights(P, C, l, mode)
    elif kind == "rwkv":
        rwkv_weights(P, C, l, mode)
    else:
        ffn_weights(P, C, l, mode)


def declare_inputs(nc, layers, last):
    I = {}

    def din(name, shape):
        I[name] = nc.dram_tensor(name, list(shape), F32, kind="ExternalInput").ap()

    din("hin", [S, D])
    din("cmask", [128, 3, 128])
    for l in layers:
        din("gmix%d" % l, [1, D])
        din("gffn%d" % l, [1, D])
        if l % 2 == 0:
            din("sgu_w_in%d" % l, [D, 2 * D])
            din("sgu_b_in_v%d" % l, [1, D])
            din("sgu_b_in_u%d" % l, [128, 8])
            din("sgu_g_v%d" % l, [1, D])
            din("sgu_w_sT%d" % l, [128, 16, 128])
            din("sgu_b_s%d" % l, [1, 2048])
            din("sgu_w_out%d" % l, [D, D])
        else:
            din("rwkv_mu%d" % l, [128, 6, 8])
            for nm in ("w_r", "w_k", "w_v", "w_o"):
                din("rwkv_%s%d" % (nm, l), [D, D])
            din("rwkv_w0a0%d" % l, [1, 2048])
            din("rwkv_w1%d" % l, [D, 64])
            din("rwkv_a1%d" % l, [D, 64])
            din("rwkv_g1%d" % l, [D, 160])
            din("rwkv_w2%d" % l, [64, D])
            din("rwkv_a2%d" % l, [64, D])
            din("rwkv_g2%d" % l, [160, D])
            for nm in ("k_k", "k_a", "r_k", "ln_w", "ln_b"):
                din("rwkv_%s%d" % (nm, l), [1, D])
        din("ffn_w_up%d" % l, [D, 2 * FF])
        din("ffn_cw%d" % l, [128, 3, 44])
        din("ffn_cb%d" % l, [128, 44])
        din("ffn_w_down%d" % l, [FF, D])
    if last:
        din("gfinal", [1, D])
    O = nc.dram_tensor("hout", [S, D], F32, kind="ExternalOutput").ap()
    return I, O


def build_program(layers, last=True, stop_after=None, ngroups=NG):
    nc = bass.Bass("TRN2", target_bir_lowering=False)
    I, O = declare_inputs(nc, layers, last)
    P = Prog(nc)
    C = Ctx()
    C.I = I
    C.scr = {}
    C.scrR = {}
    C.conv_hist = {}
    C.cur_grp = 'x'
    C.conv_next = None
    P.banks = [P.ps("bk%d" % i, [128, 512], F32) for i in range(8)]
    C.ident = P.sb("ident", [128, 128], BF16)
    C.ones = P.sb("ones", [128, 128], BF16)
    C.msk = P.sb("msk", [128, 3, 128], F32)
    C.trin = P.sb("trin", [128, 3, 128], F32)
    C.onesn = P.sb("onesn", [128, 2], F32)
    P.memset("pool", C.ident[:], 1.0, [C.ident])
    P.op("pool", lambda e: e.affine_select(out=C.ident[:], in_=C.ident[:], pattern=[[-1, 128]],
                                           compare_op=ALU.is_equal, fill=0.0, base=0, channel_multiplier=1),
         [C.ident], [C.ident])
    P.memset("pool", C.ones[:], 1.0, [C.ones])
    P.memset("pool", C.onesn[:], NEGC, [C.onesn])
    P.dma("sp", C.msk[:], I["cmask"], writes=[C.msk])
    P.ts("dve", C.trin[:], C.msk[:], NEGC, None, ALU.mult, None, [C.msk], [C.trin])
    C.arena = P.sb("arena", [128, 32768], BF16)
    C.A = [Res("A0"), Res("A1"), Res("A2"), Res("A3a"), Res("A3b")]
    C.h = [P.sb("h%d" % j, [128, D], F32) for j in range(GT)]
    C.bc = [P.sb("bc%d" % i, [128, D], F32) for i in range(6)]
    C.rows = P.sb("rows", [128, 2048], BF16)
    C.hn = [P.sb("hn%d" % i, [128, D], BF16) for i in range(2)]
    C.stat = [P.sb("stat%d" % i, [128, 8], F32) for i in range(2)]
    C.hnTfull = P.sb("hnT", [128, 8, 642], BF16)
    C.hnT = T(C.hnTfull.h[:, :, 0:G], C.hnTfull.res)
    C.hnTr = [Res("hnT_t%d" % j) for j in range(GT)]
    C.fw = [P.sb("fw%d" % i, [128, 520], F32) for i in range(8)]
    C.bw = [P.sb("bw%d" % i, [128, 512], BF16) for i in range(8)]
    C.big = P.sb("big", [128, FC * G], BF16)
    C.nstat = 0
    C.halo, C.cw, C.cb = {}, {}, {}
    for l in layers:
        C.halo[l] = P.sb("halo%d" % l, [128, 44, 2], F32)
        P.memset("pool", C.halo[l][:], 0.0, [C.halo[l]])
        C.cw[l] = P.sb("cw%d" % l, [128, 3, 44], F32)
        C.cb[l] = P.sb("cb%d" % l, [128, 44], F32)
        P.dma("sp", C.cw[l][:], I["ffn_cw%d" % l], writes=[C.cw[l]])
        P.dma("sp", C.cb[l][:], I["ffn_cb%d" % l], writes=[C.cb[l]])
    sgu_setup(P, C, layers)
    rwkv_setup(P, C, layers)

    out_events = []
    seq = []
    for l in layers:
        seq.append(("sgu" if l % 2 == 0 else "rwkv", l))
        if stop_after == ("mix", l):
            break
        seq.append(("ffn", l))
        if stop_after == ("ffn", l):
            break
    do_final = last and stop_after is None
    for g in range(ngroups):
        for j in range(GT):
            r0 = (g * GT + j) * 128
            P.dma("sp", C.h[j][:], I["hin"][r0:r0 + 128, :], writes=[C.h[j]])

        def store_tile(j, g=g):
            r0 = (g * GT + j) * 128
            out_events.append(P.dma("sp", O[r0:r0 + 128, :], C.h[j][:], reads=[C.h[j]]))

        pre_done = False
        for i, (kind, l) in enumerate(seq):
            nxt = seq[i + 1] if i + 1 < len(seq) else None
            next_pro = None
            nxt_pre = False
            if nxt is not None and (kind, nxt[0]) in (("sgu", "ffn"), ("ffn", "sgu")):
                gname = ("gffn%d" if nxt[0] == "ffn" else "gmix%d") % nxt[1]

                def next_pro(j, gname=gname):
                    if j == 0:
                        bcast_load(P, C.bc[0], I[gname])
                    norm_and_transpose_tile(P, C, C.bc[0], j)
                nxt_pre = True
            elif nxt is None and kind == "ffn":
                def next_pro(j):
                    if do_final:
                        if j == 0:
                            bcast_load(P, C.bc[0], I["gfinal"])
                        final_norm_tile(P, C, j)
                    store_tile(j)
            if g == 0:
                if i == 0:
                    sub_weights(P, C, kind, l, "conv")
                C.conv_next = (lambda nxt=nxt: sub_weights(P, C, nxt[0], nxt[1], "conv")) if nxt is not None else None
            else:
                C.conv_next = None
            if kind == "sgu":
                sgu_mixer(P, C, l, g, pre_done, next_pro)
            elif kind == "rwkv":
                rwkv_mixer(P, C, l, g)
            else:
                ffn(P, C, l, g, pre_done, next_pro)
            pre_done = nxt_pre
        if not (seq[-1][0] == "ffn"):
            for j in range(GT):
                store_tile(j)
    P.wait_all("sp", out_events)
    P.emit()
    return nc


def bcast_load(P, tile, dram_row):
    P.dma("sp", tile[:], dram_row.partition_broadcast(128), writes=[tile])


def rstd_from_ss(P, st, n_inv, eps):
    P.act(st[:, 1:2], st[:, 0:1], AF.Sqrt, [st], [st], scale=n_inv, bias=eps)
    P.op("dve", lambda e: e.reciprocal(out=st[:, 2:3], in_=st[:, 1:2]), [st], [st])


def rmsnorm_to_bf(P, C, h, gB, out_bf, junk):
    st = C.stat[C.nstat % 2]
    C.nstat += 1
    P.act(junk[:], h[:], AF.Square, [h], [junk, st], accum_out=st[:, 0:1])
    rstd_from_ss(P, st, 1.0 / D, RMS_EPS)
    P.stt("dve", out_bf[:], h[:], st[:, 2:3], gB[:], ALU.mult, ALU.mult, [h, st, gB], [out_bf])


def transpose8(P, C, srcs, dst_ap, R, W, evac_eng="act"):
    bk = P.bank()
    bkb = bk[:].bitcast(BF16)
    n = len(srcs)
    for k, s_ap in enumerate(srcs):
        P.tr(bkb[:, k * 128:(k + 1) * 128], s_ap, C.ident, list(R), [bk])
    P.cp(evac_eng, dst_ap, bkb[:, 0:n * 128].rearrange("p (k t) -> p k t", k=n), [bk], W)


def norm_and_transpose_tile(P, C, gB, j):
    hn = C.hn[j % 2]
    rmsnorm_to_bf(P, C, C.h[j], gB, hn, C.hn[1 - j % 2])
    transpose8(P, C, [hn[:, k * 128:(k + 1) * 128] for k in range(8)],
               C.hnT[:, :, j * 128:(j + 1) * 128], [hn], [C.hnTr[j]],
               evac_eng="act" if j % 2 == 0 else "dve")


def norm_and_transpose_group(P, C, gB):
    for j in range(GT):
        norm_and_transpose_tile(P, C, gB, j)


def ffn(P, C, l, g, pre_done=False, next_pro=None):
    I = C.I
    Wup = I["ffn_w_up%d" % l]
    Wdn_d = I["ffn_w_down%d" % l]
    gB = C.bc[0]
    if not pre_done:
        bcast_load(P, gB, I["gffn%d" % l])
    ring = [C.arena[:, 24576 + s * 4096: 24576 + (s + 1) * 4096].rearrange("p (k h f) -> p k h f", k=8, h=2)
            for s in range(2)]
    ringR = [C.A[3], C.A[4]]
    Wdn = C.arena[:, 0:FC * 1024].rearrange("p (c f) -> p c f", c=FC)
    WdnR = [C.A[0], C.A[1], C.A[2]]
    actT = C.big[:].rearrange("p (c t) -> p c t", c=FC)

    def load_piece(i):
        ffn_weights(P, C, l, "load", [i])

    load_piece(0)
    load_piece(1)
    ffn_weights(P, C, l, "load", ["dn"])
    if C.conv_next is not None:
        C.conv_next()
    if not pre_done:
        norm_and_transpose_group(P, C, gB)
    cw, cb, halo = C.cw[l], C.cb[l], C.halo[l]
    for i in range(11):
        s = i % 2
        for cc in range(2):
            ch = 2 * i + cc
            par = ch % 2
            cres = []
            zss = []
            for hh in range(2):
                chh = hh * FC + ch
                bk = P.bank()
                for k in range(8):
                    P.mm(bk[:, 0:G], ring[s][:, k, hh, cc * 128:(cc + 1) * 128], C.hnT[:, k, :],
                         k == 0, k == 7, [ringR[s]] + C.hnTr, [bk])
                zs = C.fw[par * 4 + hh * 2]
                cc_ = C.fw[par * 4 + hh * 2 + 1]
                P.cp("act", zs[:, 2:2 + G], bk[:, 0:G], [bk], [zs])
                P.cp("dve", zs[:, 0:2], halo[:, chh, :], [halo], [zs])
                P.act(cc_[:, 0:G], bk[:, 0:G], AF.Identity, [bk, cw, cb], [cc_],
                      scale=cw[:, 2, chh:chh + 1], bias=cb[:, chh:chh + 1])
                P.stt("dve", cc_[:, 0:G], zs[:, 1:1 + G], cw[:, 1, chh:chh + 1], cc_[:, 0:G], ALU.mult, ALU.add,
                      [zs, cw, cc_], [cc_])
                P.stt("dve", cc_[:, 0:G], zs[:, 0:G], cw[:, 0, chh:chh + 1], cc_[:, 0:G], ALU.mult, ALU.add,
                      [zs, cw, cc_], [cc_])
                P.cp("dve", halo[:, chh, :], zs[:, G:G + 2], [zs], [halo])
                cres.append(cc_)
                zss.append(zs)
            sg = zss[0]
            P.act(sg[:, 0:G], cres[0][:, 0:G], AF.Silu, [cres[0]], [sg])
            P.tt("dve", actT[:, ch, :], sg[:, 0:G], cres[1][:, 0:G], ALU.mult, [sg, cres[1]], [C.big])
        if i + 2 < 11:
            load_piece(i + 2)
    for j in range(GT):
        for hf in range(2):
            bk = P.bank()
            for c in range(FC):
                P.mm(bk[:, 0:512], actT[:, c, j * 128:(j + 1) * 128], Wdn[:, c, hf * 512:(hf + 1) * 512],
                     c == 0, c == FC - 1, [C.big] + WdnR, [bk])
            P.tt("dve", C.h[j][:, hf * 512:(hf + 1) * 512], bk[:, 0:512], C.h[j][:, hf * 512:(hf + 1) * 512], ALU.add,
                 [bk, C.h[j]], [C.h[j]])
        if next_pro is not None:
            next_pro(j)


def final_norm_tile(P, C, j):
    gB = C.bc[0]
    h = C.h[j]
    st = C.stat[C.nstat % 2]
    C.nstat += 1
    P.act(C.hn[j % 2][:], h[:], AF.Square, [h], [C.hn[j % 2], st], accum_out=st[:, 0:1])
    rstd_from_ss(P, st, 1.0 / D, RMS_EPS)
    P.stt("dve", h[:], h[:], st[:, 2:3], gB[:], ALU.mult, ALU.mult, [h, st, gB], [h])


def sgu_setup(P, C, layers):
    C.sgu = {}
    if any(l % 2 == 0 for l in layers):
        C.wsT = P.sb("wsT", [128, 16, 128], BF16)
    for l in layers:
        if l % 2 != 0:
            continue
        X = Ctx()
        C.sgu[l] = X
        X.biu = P.sb("biu%d" % l, [128, 8], F32)
        P.dma("sp", X.biu[:], C.I["sgu_b_in_u%d" % l], writes=[X.biu])


def sgu_mixer(P, C, l, g, pre_done=False, next_pro=None):
    I = C.I
    X = C.sgu[l]
    Win = C.arena[:, 0:16384].rearrange("p (k f) -> p k f", k=8)
    WinR = [C.A[0], C.A[1]]
    Wout = C.arena[:, 16384:24576].rearrange("p (k f) -> p k f", k=8)
    WoutR = [C.A[2]]
    gB, gv = C.bc[0], C.bc[1]
    if not pre_done:
        bcast_load(P, gB, I["gmix%d" % l])
    bcast_load(P, gv, I["sgu_g_v%d" % l])
    P.dma("pool", C.rows[0:1, 0:2048], I["sgu_b_s%d" % l], writes=[C.rows])
    P.dma("pool", C.rows[32:33, 0:1024], I["sgu_b_in_v%d" % l], writes=[C.rows])
    sgu_weights(P, C, l, "load")
    if C.conv_next is not None:
        C.conv_next()
    stg = C.big[:, 8192:8192 + 2048].bitcast(F32).rearrange("p (g t) -> p g t", g=8)
    for q in range(2):
        P.dma("sp", stg, I["sgu_w_sT%d" % l][:, q * 8:(q + 1) * 8, :], writes=[C.big])
        P.tt("dve", C.wsT[:, q * 8:(q + 1) * 8, :], stg,
             C.msk[:, 0:1, :].to_broadcast([128, 8, 128]), ALU.mult, [C.big, C.msk], [C.wsT])
    if not pre_done:
        norm_and_transpose_group(P, C, gB)
    uT = C.big[:, 0:8192].bitcast(F32).rearrange("p (c t) -> p c t", c=8)
    for c in range(8):
        bk = P.bank()
        for k in range(8):
            P.mm(bk[:, 0:G], Win[:, k, c * 128:(c + 1) * 128], C.hnT[:, k, :], k == 0, k == 7,
                 WinR + C.hnTr, [bk])
        P.act(uT[:, c, :], bk[:, 0:G], AF.Gelu, [bk, X.biu], [C.big], bias=X.biu[:, c:c + 1])
    for j in range(GT):
        st = C.stat[C.nstat % 2]
        C.nstat += 1
        vh = [C.fw[4 + hf] for hf in range(2)]
        for hf in range(2):
            bk = P.bank()
            for k in range(8):
                P.mm(bk[:, 0:512], C.hnT[:, k, j * 128:(j + 1) * 128],
                     Win[:, k, 1024 + hf * 512: 1024 + (hf + 1) * 512],
                     k == 0, False, WinR + [C.hnTr[j]], [bk])
            P.mm(bk[:, 0:512], C.ones[32:33, 0:128], C.rows[32:33, hf * 512:(hf + 1) * 512], False, True,
                 [C.ones, C.rows], [bk])
            P.act(vh[hf][:, 0:512], bk[:, 0:512], AF.Gelu, [bk], [vh[hf]])
            P.act(C.hn[0][:, 0:512], vh[hf][:, 0:512], AF.Square, [vh[hf]], [C.hn[0], st],
                  accum_out=st[:, 3 + hf:4 + hf])
        P.tt("dve", st[:, 0:1], st[:, 3:4], st[:, 4:5], ALU.add, [st], [st])
        rstd_from_ss(P, st, 1.0 / D, RMS_EPS)
        vn = [C.bw[(j % 2) * 4 + q] for q in range(2)]
        yT = [C.bw[(j % 2) * 4 + 2 + q] for q in range(2)]
        for q in range(2):
            P.stt("dve", vn[q][:], vh[q][:, 0:512], st[:, 2:3], gv[:, q * 512:(q + 1) * 512], ALU.mult, ALU.mult,
                  [vh[q], st, gv], [vn[q]])
        for q in range(2):
            bk = P.bank()
            for c4 in range(4):
                for gg in range(2):
                    gi = 8 * q + 2 * c4 + gg
                    gl = 2 * c4 + gg
                    o = bk[gg * 64:(gg + 1) * 64, c4 * 128:(c4 + 1) * 128]
                    P.mm(o, vn[q][:, gl * 64:(gl + 1) * 64], C.wsT[:, gi, :], True, False, [vn[q], C.wsT], [bk])
                    P.mm(o, C.ones[0:1, 0:64], C.rows[0:1, gi * 128:(gi + 1) * 128], False, True,
                         [C.ones, C.rows], [bk])
            P.tt("dve", yT[q][:].rearrange("p (c t) -> p c t", c=4),
                 bk[:, 0:512].rearrange("p (c t) -> p c t", c=4),
                 uT[:, q * 4:(q + 1) * 4, j * 128:(j + 1) * 128], ALU.mult, [bk, C.big], [yT[q]])
        for hf in range(2):
            bk = P.bank()
            for c in range(8):
                P.mm(bk[:, 0:512], yT[c // 4][:, (c % 4) * 128:(c % 4 + 1) * 128], Wout[:, c, hf * 512:(hf + 1) * 512],
                     c == 0, c == 7, [yT[c // 4]] + WoutR, [bk])
            P.tt("dve", C.h[j][:, hf * 512:(hf + 1) * 512], bk[:, 0:512], C.h[j][:, hf * 512:(hf + 1) * 512], ALU.add,
                 [bk, C.h[j]], [C.h[j]])
        if next_pro is not None:
            next_pro(j)


def rwkv_setup(P, C, layers):
    C.rw = {}
    if not any(l % 2 == 1 for l in layers):
        return
    C.lw1 = P.sb("lw1", [128, 8, 288], BF16)
    C.w2 = P.sb("w2", [64, D], BF16)
    C.a2 = P.sb("a2", [64, D], BF16)
    C.g2a = P.sb("g2a", [128, D], BF16)
    C.g2b = P.sb("g2b", [32, D], BF16)
    C.lo = P.sb("lo", [128, 512], BF16)
    C.zT = P.sb("zT", [128, 8, 128], BF16)
    C.sst = P.sb("sst", [128, 64], F32)
    for l in layers:
        if l % 2 != 1:
            continue
        X = Ctx()
        C.rw[l] = X
        X.STf = P.sb("STf%d" % l, [128, 8, 64], F32)
        X.STz = P.sb("STz%d" % l, [128, 16, 64], BF16)
        X.carry = P.sb("carry%d" % l, [128, 8, 1], BF16)
        X.mu = P.sb("mu%d" % l, [128, 6, 8], F32)
        P.memset("pool", X.STf[:], 0.0, [X.STf])
        P.memset("pool", X.STz[:], 0.0, [X.STz])
        P.memset("pool", X.carry[:], 0.0, [X.carry])
        P.dma("sp", X.mu[:], C.I["rwkv_mu%d" % l], writes=[X.mu])


def rwkv_mixer(P, C, l, g):
    I = C.I
    X = C.rw[l]
    ar = C.arena
    Wr = ar[:, 0:8192].rearrange("p (k f) -> p k f", k=8)
    Wk = ar[:, 8192:16384].rearrange("p (k f) -> p k f", k=8)
    Wv = ar[:, 16384:24576].rearrange("p (k f) -> p k f", k=8)
    Wo = ar[:, 24576:32768].rearrange("p (k f) -> p k f", k=8)
    WrR, WkR, WvR, WoR = [C.A[0]], [C.A[1]], [C.A[2]], [C.A[3], C.A[4]]
    gB, bkk, bka, brk, blw, blb = C.bc
    bcast_load(P, gB, I["gmix%d" % l])
    bcast_load(P, bkk, I["rwkv_k_k%d" % l])
    bcast_load(P, bka, I["rwkv_k_a%d" % l])
    bcast_load(P, brk, I["rwkv_r_k%d" % l])
    bcast_load(P, blw, I["rwkv_ln_w%d" % l])
    bcast_load(P, blb, I["rwkv_ln_b%d" % l])
    P.dma("pool", C.rows[0:1, 0:2048], I["rwkv_w0a0%d" % l], writes=[C.rows])
    rwkv_weights(P, C, l, "load")
    if C.conv_next is not None:
        C.conv_next()

    hx, xsR = Res("hx"), [Res("xs%d" % i) for i in range(4)]
    split_multi(C.hnTr, [hx] + xsR)
    hT = C.hnTfull
    big = C.big
    RA, KB = Res("RA"), Res("KB")
    Q = []
    for q in range(2):
        qq = Ctx()
        base = 2048 + q * 4608
        qq.AKBr = Res("AKB%d" % q)
        qq.XpR = [Res("Xp%d_%d" % (q, i)) for i in range(2)]
        qq.LpR = [Res("Lp%d_%d" % (q, i)) for i in range(2)]
        qq.WcR = [Res("Wc%d_%d" % (q, i)) for i in range(2)]
        qq.AKB4 = big[:, base:base + 2048].rearrange("p (h q t) -> p h q t", h=4, q=4)
        qq.Xp4 = big[:, base + 2048:base + 3072].rearrange("p (s h t) -> p s h t", s=2, h=4)
        qq.Lp4 = big[:, base + 3072:base + 4096].rearrange("p (s h t) -> p s h t", s=2, h=4)
        qq.Wc4 = big[:, base + 4096:base + 4608].rearrange("p (s h m) -> p s h m", s=2, h=4)
        qq.all = [qq.AKBr] + qq.XpR + qq.LpR + qq.WcR
        Q.append(qq)
    bigsub = [RA, KB] + Q[0].all + Q[1].all
    split_res(C.big.res, bigsub)
    RAt4 = big[:, 0:1024].rearrange("p (a q t) -> p a q t", a=4, q=2)
    KBt4 = big[:, 1024:2048].rearrange("p (a q t) -> p a q t", a=4, q=2)
    xs = [hT[:, :, 130 + s * 128: 258 + s * 128] for s in range(4)]
    sst = C.sst
    lo = C.lo
    v3 = lambda t: t[:, 0:512].rearrange("p (h n) -> p h n", h=8)
    bc3 = lambda ap: ap.unsqueeze(2).to_broadcast([128, 8, 64])
    EW = "dve"

    for j in range(GT):
        hn = C.hn[j % 2]
        jk = C.hn[1 - j % 2]
        rmsnorm_to_bf(P, C, C.h[j], gB, hn, jk)
        P.cp("act", hT[:, :, 1:2], X.carry[:], [X.carry], [hx])
        transpose8(P, C, [hn[:, k * 128:(k + 1) * 128] for k in range(8)], hT[:, :, 2:130], [hn], [hx])
        P.cp("act", X.carry[:], hT[:, :, 129:130], [hx], [X.carry])
        xx = jk[:].rearrange("p (k t) -> p k t", k=8)
        P.tt("dve", xx, hT[:, :, 1:129], hT[:, :, 2:130], ALU.subtract, [hx], [jk])

        def gen_x(i, slot):
            P.tt(EW, xs[slot], xx, X.mu[:, i, :].unsqueeze(2).to_broadcast([128, 8, 128]), ALU.mult,
                 [jk, X.mu], [xsR[slot]])
            P.tt(EW, xs[slot], xs[slot], hT[:, :, 2:130], ALU.add, [hx, xsR[slot]], [xsR[slot]])

        bk = P.bank()
        gen_x(1, 0)
        for k in range(8):
            P.mm(bk[0:64, 0:128], C.lw1[:, k, 0:64], xs[0][:, k, :], k == 0, k == 7, [C.lw1, xsR[0]], [bk])
        gen_x(4, 1)
        for k in range(8):
            P.mm(bk[0:64, 128:256], C.lw1[:, k, 64:128], xs[1][:, k, :], k == 0, k == 7, [C.lw1, xsR[1]], [bk])
        gen_x(5, 2)
        for k in range(8):
            P.mm(bk[:, 256:384], C.lw1[:, k, 128:256], xs[2][:, k, :], k == 0, k == 7, [C.lw1, xsR[2]], [bk])
        for k in range(8):
            P.mm(bk[0:32, 384:512], C.lw1[:, k, 256:288], xs[2][:, k, :], k == 0, k == 7, [C.lw1, xsR[2]], [bk])
        P.act(lo[0:64, 0:128], bk[0:64, 0:128], AF.Tanh, [bk], [lo])
        P.cp("act", lo[0:64, 128:256], bk[0:64, 128:256], [bk], [lo])
        P.act(lo[:, 256:384], bk[:, 256:384], AF.Sigmoid, [bk], [lo])
        P.act(lo[0:32, 384:512], bk[0:32, 384:512], AF.Sigmoid, [bk], [lo])
        gen_x(0, 3)
        gen_x(2, 0)
        gen_x(3, 1)

        r_t, k_t, v_t, a_t, sg_t, tA, tB, tE = C.fw
        rb, ab, kb, bb, khb, bhb, vb, zb = C.bw
        y_t = tE

        def prepA(hf):
            f0 = hf * 512

            def proj(slot, W, WR, dst):
                bk = P.bank()
                for k in range(8):
                    P.mm(bk[:, 0:512], xs[slot][:, k, :], W[:, k, f0:f0 + 512], k == 0, k == 7,
                         [xsR[slot]] + WR, [bk])
                P.cp("act", dst[:, 0:512], bk[:, 0:512], [bk], [dst])

            proj(3, Wr, WrR, r_t)
            proj(0, Wk, WkR, k_t)
            proj(1, Wv, WvR, v_t)
            bk = P.bank()
            P.mm(bk[:, 0:512], lo[0:64, 0:128], C.w2[0:64, f0:f0 + 512], True, False, [lo, C.w2], [bk])
            P.mm(bk[:, 0:512], C.ones[0:1, 0:128], C.rows[0:1, f0:f0 + 512], False, True, [C.ones, C.rows], [bk])
            P.act(sg_t[:, 0:512], bk[:, 0:512], AF.Sigmoid, [bk], [sg_t])
            bk = P.bank()
            P.mm(bk[:, 0:512], lo[0:64, 128:256], C.a2[0:64, f0:f0 + 512], True, False, [lo, C.a2], [bk])
            P.mm(bk[:, 0:512], C.ones[0:1, 0:128], C.rows[0:1, 1024 + f0:1024 + f0 + 512], False, True,
                 [C.ones, C.rows], [bk])
            P.act(a_t[:, 0:512], bk[:, 0:512], AF.Sigmoid, [bk], [a_t])
            bI, bE, bR = P.bank(), P.bank(), P.bank()
            for bq, ti in ((bI, 0), (bE, 1), (bR, 2)):
                P.mm(bq[:, 0:512], C.trin[:, ti, :], sg_t[:, 0:512], True, True, [C.trin, sg_t], [bq])
            bG = P.bank()
            for pp in range(4):
                P.mm(bG[:, 2 * pp:2 * pp + 2], sg_t[:, pp * 128:(pp + 1) * 128], C.onesn[:, 0:2], True, True,
                     [sg_t, C.onesn], [bG])
            return bI, bE, bR, bG

        def mid(hf, bks):
            f0 = hf * 512
            bI, bE, bR, bG = bks
            P.tt(EW, tA[:, 0:512], k_t[:, 0:512], bkk[:, f0:f0 + 512], ALU.mult, [k_t, bkk], [tA])
            P.tt(EW, tB[:, 0:512], tA[:, 0:512], tA[:, 0:512], ALU.mult, [tA], [tB])
            P.op("dve", lambda e: e.tensor_reduce(out=sst[:, 0:8], in_=v3(tB), axis=AX.X, op=ALU.add), [tB], [sst])
            P.ts("dve", sst[:, 0:8], sst[:, 0:8], 1e-24, None, ALU.max, None, [sst], [sst])
            P.act(sst[:, 8:16], sst[:, 0:8], AF.Sqrt, [sst], [sst])
            P.op("dve", lambda e: e.reciprocal(out=sst[:, 16:24], in_=sst[:, 8:16]), [sst], [sst])
            P.tt("dve", v3(tA), v3(tA), bc3(sst[:, 16:24]), ALU.mult, [tA, sst], [tA])
            P.stt("dve", tB[:, 0:512], a_t[:, 0:512], -1.0, bka[:, f0:f0 + 512], ALU.add, ALU.mult,
                  [a_t, bka], [tB])
            P.stt("dve", k_t[:, 0:512], tB[:, 0:512], 1.0, k_t[:, 0:512], ALU.add, ALU.mult, [tB, k_t], [k_t])
            P.tt(EW, a_t[:, 0:512], tA[:, 0:512], a_t[:, 0:512], ALU.mult, [tA, a_t], [a_t])
            P.tt(EW, tB[:, 0:512], r_t[:, 0:512], k_t[:, 0:512], ALU.mult, [r_t, k_t], [tB])
            P.tt(EW, tB[:, 0:512], tB[:, 0:512], brk[:, f0:f0 + 512], ALU.mult, [tB, brk], [tB])
            P.op("dve", lambda e: e.tensor_reduce(out=sst[:, 24:32], in_=v3(tB), axis=AX.X, op=ALU.add), [tB], [sst])
            P.cp("act", vb[:], v_t[:, 0:512], [v_t], [vb])
            P.act(tB[:, 0:512], bI[:, 0:512], AF.Exp, [bI], [tB])
            P.tt("dve", rb[:], r_t[:, 0:512], tB[:, 0:512], ALU.mult, [r_t, tB], [rb])
            P.act(tE[:, 0:512], bE[:, 0:512], AF.Exp, [bE], [tE])
            P.stt("dve", ab[:], tA[:, 0:512], -1.0, tE[:, 0:512], ALU.mult, ALU.mult, [tA, tE], [ab])
            P.act(tB[:, 0:512], bI[:, 0:512], AF.Exp, [bI], [tB], scale=-1.0)
            P.tt("dve", kb[:], k_t[:, 0:512], tB[:, 0:512], ALU.mult, [k_t, tB], [kb])
            P.tt(EW, bb[:], a_t[:, 0:512], tB[:, 0:512], ALU.mult, [a_t, tB], [bb])
            P.act(tE[:, 0:512], bR[:, 0:512], AF.Exp, [bR], [tE])
            P.tt("dve", khb[:], k_t[:, 0:512], tE[:, 0:512], ALU.mult, [k_t, tE], [khb])
            P.tt(EW, bhb[:], a_t[:, 0:512], tE[:, 0:512], ALU.mult, [a_t, tE], [bhb])
            P.act(sst[:, 32:40], bG[:, 0:8], AF.Exp, [bG], [sst])
            srcs = []
            for pp in range(4):
                srcs += [rb[:, pp * 128:(pp + 1) * 128], ab[:, pp * 128:(pp + 1) * 128]]
            transpose8(P, C, srcs, big[:, 0:1024].rearrange("p (k t) -> p k t", k=8), [rb, ab], [RA], "act")
            srcs = []
            for pp in range(4):
                srcs += [kb[:, pp * 128:(pp + 1) * 128], bb[:, pp * 128:(pp + 1) * 128]]
            transpose8(P, C, srcs, big[:, 1024:2048].rearrange("p (k t) -> p k t", k=8), [kb, bb], [KB], "dve")
            P.tt("dve", v3(tA), v3(v_t), bc3(sst[:, 24:32]), ALU.mult, [v_t, sst], [tA])
            y_t = tE
            HD = [[(hd, qd * 2 + hd // 2, hd % 2, qd * 4 + hd) for hd in range(4)] for qd in range(2)]
            for qd in range(2):
                qq, heads = Q[qd], HD[qd]
                for (qsel, dst_q) in ((0, 0), (1, 2)):
                    bnk = [P.bank(), P.bank()]
                    for (hd, pp, hh, hl) in heads:
                        col = (hd // 2) * 256
                        P.mm(bnk[hh][:, col:col + 256], KBt4[hh * 64:(hh + 1) * 64, pp, qsel, :],
                             big[hh * 64:(hh + 1) * 64, pp * 256:(pp + 1) * 256], True, True, [KB, RA], [bnk[hh]])
                    for hh in range(2):
                        P.tt("dve", qq.AKB4[:, hh::2, dst_q:dst_q + 2, :],
                             bnk[hh][:, 0:512].rearrange("p (h q t) -> p h q t", h=2, q=2),
                             C.msk[:, 0:2, :].unsqueeze(1).to_broadcast([128, 2, 2, 128]), ALU.mult,
                             [bnk[hh], C.msk], [qq.AKBr])
                bnk = [P.bank(), P.bank()]
                for (hd, pp, hh, hl) in heads:
                    col = (hd // 2) * 128
                    P.mm(bnk[hh][:, col:col + 128], RAt4[hh * 64:(hh + 1) * 64, pp, 1, :],
                         KBt4[hh * 64:(hh + 1) * 64, pp, 1, :], True, True, [KB, RA], [bnk[hh]])
                for hh in range(2):
                    P.tt("dve", qq.Lp4[:, 0, hh::2, :], bnk[hh][:, 0:256].rearrange("p (h t) -> p h t", h=2),
                         C.msk[:, 2:3, :].to_broadcast([128, 2, 128]), ALU.mult, [bnk[hh], C.msk], [qq.LpR[0]])
            for qd in range(2):
                qq, heads = Q[qd], HD[qd]
                bW = P.bank()
                for (hd, pp, hh, hl) in heads:
                    gh = hf * 8 + hl
                    o = bW[:, hd * 64:(hd + 1) * 64]
                    P.mm(o, RAt4[:, pp, 1, :], X.STz[:, gh, :], True, False, [RA, X.STz], [bW])
                    P.mm(o, qq.AKB4[:, hd, 1, :], vb[:, hl * 64:(hl + 1) * 64], False, True, [qq.AKBr, vb], [bW])
                P.cp("act", qq.Wc4[:, 0, :, :], bW[:, 0:256].rearrange("p (h m) -> p h m", h=4), [bW], [qq.WcR[0]])
            for kl in range(7):
                for qd in range(2):
                    qq, heads = Q[qd], HD[qd]

                    def Xk(hd):
                        return qq.AKB4[:, hd, 3, :] if kl == 0 else qq.Xp4[:, (kl - 1) % 2, hd, :]
                    XkR = qq.AKBr if kl == 0 else qq.XpR[(kl - 1) % 2]
                    LkR = qq.LpR[kl % 2]
                    if kl < 6:
                        bX = P.bank()
                        for (hd, pp, hh, hl) in heads:
                            P.mm(bX[:, hd * 128:(hd + 1) * 128], qq.Lp4[:, kl % 2, hd, :], Xk(hd), True, True,
                                 [LkR, XkR], [bX])
                    if kl < 5:
                        bL = P.bank()
                        for (hd, pp, hh, hl) in heads:
                            P.mm(bL[:, hd * 128:(hd + 1) * 128], Xk(hd), qq.Lp4[:, kl % 2, hd, :], True, True,
                                 [LkR, XkR], [bL])
                    bU = P.bank()
                    for (hd, pp, hh, hl) in heads:
                        P.mm(bU[:, hd * 64:(hd + 1) * 64], Xk(hd), qq.Wc4[:, kl % 2, hd, :], True, True,
                             [XkR, qq.WcR[kl % 2]], [bU])
                    if kl < 6:
                        P.cp("act", qq.Xp4[:, kl % 2, :, :], bX[:, 0:512].rearrange("p (h t) -> p h t", h=4),
                             [bX], [qq.XpR[kl % 2]])
                    if kl < 5:
                        P.cp("act" if qd == 0 else "dve", qq.Lp4[:, (kl + 1) % 2, :, :],
                             bL[:, 0:512].rearrange("p (h t) -> p h t", h=4), [bL], [qq.LpR[(kl + 1) % 2]])
                    P.tt("dve", qq.Wc4[:, (kl + 1) % 2, :, :], bU[:, 0:256].rearrange("p (h m) -> p h m", h=4),
                         qq.Wc4[:, kl % 2, :, :], ALU.add, [bU, qq.WcR[kl % 2]], [qq.WcR[(kl + 1) % 2]])
            for qd in range(2):
                qq, heads = Q[qd], HD[qd]
                U = lambda hd: qq.Wc4[:, 1, hd, :]
                bY = P.bank()
                for (hd, pp, hh, hl) in heads:
                    gh = hf * 8 + hl
                    o = bY[:, hd * 64:(hd + 1) * 64]
                    P.mm(o, RAt4[:, pp, 0, :], X.STz[:, gh, :], True, False, [RA, X.STz], [bY])
                    P.mm(o, qq.AKB4[:, hd, 0, :], vb[:, hl * 64:(hl + 1) * 64], False, False, [qq.AKBr, vb], [bY])
                    P.mm(o, qq.AKB4[:, hd, 2, :], U(hd), False, True, [qq.AKBr, qq.WcR[1]], [bY])
                P.cp("act", y_t[:, qd * 256:(qd + 1) * 256], bY[:, 0:256], [bY], [y_t])
                bS = P.bank()
                for (hd, pp, hh, hl) in heads:
                    o = bS[hh * 64:(hh + 1) * 64, (hd // 2) * 64:(hd // 2 + 1) * 64]
                    P.mm(o, khb[:, hl * 64:(hl + 1) * 64], vb[:, hl * 64:(hl + 1) * 64], True, False, [khb, vb], [bS])
                    P.mm(o, bhb[:, hl * 64:(hl + 1) * 64], U(hd), False, True, [bhb, qq.WcR[1]], [bS])
                for pr in range(2):
                    ppl = qd * 2 + pr
                    gp = hf * 4 + ppl
                    P.stt("dve", X.STf[:, gp, :], X.STf[:, gp, :], sst[:, 32 + 2 * ppl:33 + 2 * ppl],
                          bS[:, pr * 64:(pr + 1) * 64], ALU.mult, ALU.add, [X.STf, sst, bS], [X.STf])
                    P.cp("act", X.STz[0:64, 2 * gp, :], X.STf[0:64, gp, :], [X.STf], [X.STz])
                    P.cp("act", X.STz[64:128, 2 * gp + 1, :], X.STf[64:128, gp, :], [X.STf], [X.STz])

        def post(hf):
            f0 = hf * 512
            P.op("dve", lambda e: e.tensor_reduce(out=sst[:, 40:48], in_=v3(y_t), axis=AX.X, op=ALU.add), [y_t], [sst])
            P.act(tB[:, 0:512], y_t[:, 0:512], AF.Square, [y_t], [tB])
            P.op("dve", lambda e: e.tensor_reduce(out=sst[:, 48:56], in_=v3(tB), axis=AX.X, op=ALU.add), [tB], [sst])
            P.ts("dve", sst[:, 40:48], sst[:, 40:48], 1.0 / 64, None, ALU.mult, None, [sst], [sst])
            P.tt("dve", sst[:, 56:64], sst[:, 40:48], sst[:, 40:48], ALU.mult, [sst], [sst])
            P.stt("dve", sst[:, 48:56], sst[:, 48:56], 1.0 / 64, sst[:, 56:64], ALU.mult, ALU.subtract, [sst], [sst])
            P.act(sst[:, 56:64], sst[:, 48:56], AF.Sqrt, [sst], [sst], bias=GN_EPS)
            P.op("dve", lambda e: e.reciprocal(out=sst[:, 48:56], in_=sst[:, 56:64]), [sst], [sst])
            P.tt("dve", v3(y_t), v3(y_t), bc3(sst[:, 40:48]), ALU.subtract, [y_t, sst], [y_t])
            P.tt("dve", v3(y_t), v3(y_t), bc3(sst[:, 48:56]), ALU.mult, [y_t, sst], [y_t])
            P.tt(EW, y_t[:, 0:512], y_t[:, 0:512], blw[:, f0:f0 + 512], ALU.mult, [y_t, blw], [y_t])
            P.tt(EW, y_t[:, 0:512], y_t[:, 0:512], blb[:, f0:f0 + 512], ALU.add, [y_t, blb], [y_t])
            P.tt(EW, y_t[:, 0:512], y_t[:, 0:512], tA[:, 0:512], ALU.add, [y_t, tA], [y_t])
            bk = P.bank()
            P.mm(bk[:, 0:512], lo[:, 256:384], C.g2a[:, f0:f0 + 512], True, False, [lo, C.g2a], [bk])
            P.mm(bk[:, 0:512], lo[0:32, 384:512], C.g2b[0:32, f0:f0 + 512], False, True, [lo, C.g2b], [bk])
            P.tt("dve", zb[:], y_t[:, 0:512], bk[:, 0:512], ALU.mult, [y_t, bk], [zb])
            transpose8(P, C, [zb[:, pp * 128:(pp + 1) * 128] for pp in range(4)],
                       C.zT[:, hf * 4:(hf + 1) * 4, :], [zb], [C.zT], "act")

        bks0 = prepA(0)
        mid(0, bks0)
        bks1 = prepA(1)
        post(0)
        mid(1, bks1)
        post(1)
        for hfo in range(2):
            bk = P.bank()
            for c in range(8):
                P.mm(bk[:, 0:512], C.zT[:, c, :], Wo[:, c, hfo * 512:(hfo + 1) * 512], c == 0, c == 7,
                     [C.zT] + WoR, [bk])
            P.tt("dve", C.h[j][:, hfo * 512:(hfo + 1) * 512], bk[:, 0:512], C.h[j][:, hfo * 512:(hfo + 1) * 512],
                 ALU.add, [bk, C.h[j]], [C.h[j]])
    merge_multi(C.hnTr, [hx] + xsR)
    merge_res(C.big.res, bigsub)


def prep_layer_inputs(inp, layers, last):
    c = np.ascontiguousarray
    W = {}
    pp_, ff_ = np.arange(128)[:, None], np.arange(128)[None, :]
    W["cmask"] = c(np.stack([(pp_ <= ff_), (pp_ < ff_), (pp_ > ff_)], axis=1).astype(np.float32))
    for l in layers:
        j = l // 2
        W["gmix%d" % l] = c(inp["norm_mix_g"][l].reshape(1, D))
        W["gffn%d" % l] = c(inp["norm_ffn_g"][l].reshape(1, D))
        if l % 2 == 0:
            W["sgu_w_in%d" % l] = c(inp["sgu_w_in"][j])
            b = inp["sgu_b_in"][j]
            W["sgu_b_in_u%d" % l] = c(b[:D].reshape(8, 128).T)
            W["sgu_b_in_v%d" % l] = c(b[D:].reshape(1, D))
            W["sgu_g_v%d" % l] = c(inp["sgu_g_v"][j].reshape(1, D))
            W["sgu_w_sT%d" % l] = c(np.transpose(inp["sgu_w_s"][j], (2, 0, 1)))
            W["sgu_b_s%d" % l] = c(inp["sgu_b_s"][j].reshape(1, 2048))
            W["sgu_w_out%d" % l] = c(inp["sgu_w_out"][j])
        else:
            W["rwkv_mu%d" % l] = c(np.transpose(inp["rwkv_mu"][j].reshape(6, 8, 128), (2, 0, 1)))
            for nm in ("w_r", "w_k", "w_v", "w_o", "w1", "a1", "g1", "w2", "a2", "g2"):
                W["rwkv_%s%d" % (nm, l)] = c(inp["rwkv_" + nm][j])
            W["rwkv_w0a0%d" % l] = c(np.concatenate([inp["rwkv_w0"][j], inp["rwkv_a0"][j]]).reshape(1, 2048))
            for nm in ("k_k", "k_a", "r_k", "ln_w", "ln_b"):
                W["rwkv_%s%d" % (nm, l)] = c(inp["rwkv_" + nm][j].reshape(1, D))
        W["ffn_w_up%d" % l] = c(inp["ffn_w_up"][l])
        W["ffn_cw%d" % l] = c(np.transpose(inp["ffn_conv_w"][l].reshape(3, 44, 128), (2, 0, 1)))
        W["ffn_cb%d" % l] = c(inp["ffn_conv_b"][l].reshape(44, 128).T)
        W["ffn_w_down%d" % l] = c(inp["ffn_w_down"][l])
    if last:
        W["gfinal"] = c(inp["norm_final_g"].reshape(1, D))
    return W


_NC_CACHE = {}


def run_launch(h, inp, layers, last, ncores=8, **bk):
    key = (tuple(layers), last, tuple(sorted(bk.items())))
    if key not in _NC_CACHE:
        _NC_CACHE[key] = build_program(layers, last=last, **bk)
    nc = _NC_CACHE[key]
    W = prep_layer_inputs(inp, layers, last)
    in_maps = []
    for b in range(ncores):
        m = dict(W)
        m["hin"] = np.ascontiguousarray(h[b])
        in_maps.append(m)
    res = run_bass_kernel_spmd(nc, in_maps, core_ids=list(range(ncores)))
    return np.stack([np.asarray(r["hout"]) for r in res.results], axis=0)


FUSED = True


def kernel(**inputs):
    inp = {k: np.asarray(v) for k, v in inputs.items()}
    h = np.ascontiguousarray(inp["x"], dtype=np.float32)
    if FUSED:
        return run_launch(h, inp, [0, 1, 2, 3], True).astype(np.float32)
    for l in range(DEPTH):
        h = run_launch(h, inp, [l], l == DEPTH - 1)
    return h.astype(np.float32)
```
